# Optimizing a Trainium2 kernel written in Bass

```python
import jax, jax.numpy as jnp
from jax import lax
import numpy as np

D_MODEL = 1024
BATCH = 4
SEQ = 8192
DEPTH = 1

GRID_W = 64
CTX_LEN = 256
HEAD_DIM = 64
ATTN_HEADS = 8
ATTN_KV_HEADS = 2
GQA_GROUP = ATTN_HEADS // ATTN_KV_HEADS
RET_HEADS = 8
ATTN_WIDTH = ATTN_HEADS * HEAD_DIM
KV_WIDTH = ATTN_KV_HEADS * HEAD_DIM
RET_WIDTH = RET_HEADS * HEAD_DIM
MIX_WIDTH = ATTN_WIDTH + RET_WIDTH
IN_PROJ_WIDTH = ATTN_WIDTH + 2 * KV_WIDTH + 4 * RET_WIDTH
SPLITS = (ATTN_WIDTH, ATTN_WIDTH + KV_WIDTH, ATTN_WIDTH + 2 * KV_WIDTH,
          ATTN_WIDTH + 2 * KV_WIDTH + RET_WIDTH, ATTN_WIDTH + 2 * KV_WIDTH + 2 * RET_WIDTH,
          ATTN_WIDTH + 2 * KV_WIDTH + 3 * RET_WIDTH)
FFN_HIDDEN = -(-8 * D_MODEL // (3 * 256)) * 256
N_MOD = 6
QBLOCK = 128
RET_CHUNK = 128
ROPE_BASE = 10000.0
ATTN_SCALE = HEAD_DIM ** -0.5
EPS = 1e-6

kernel_name = 'hybrid_gqa_retention_dit_block'


def rms_norm(x, g):
    xf = x.astype(jnp.float32)
    y = xf * lax.rsqrt(jnp.mean(xf * xf, axis=-1, keepdims=True) + EPS)
    return (y * g.astype(jnp.float32)).astype(x.dtype)


def head_group_norm(o):
    mu = jnp.mean(o, axis=-1, keepdims=True)
    var = jnp.mean(jnp.square(o - mu), axis=-1, keepdims=True)
    return (o - mu) * lax.rsqrt(var + EPS)


def modulate(h, shift, scale):
    return h * (1 + scale) + shift


def axial_rope(rows):
    row = jnp.broadcast_to(jnp.arange(rows)[:, None], (rows, GRID_W)).reshape(-1).astype(jnp.float32)
    col = jnp.broadcast_to(jnp.arange(GRID_W)[None, :], (rows, GRID_W)).reshape(-1).astype(jnp.float32)
    n_freq = HEAD_DIM // 4
    inv = ROPE_BASE ** (-jnp.arange(n_freq, dtype=jnp.float32) / n_freq)
    ang = jnp.concatenate([row[:, None] * inv, col[:, None] * inv], axis=-1)
    return jnp.cos(ang), jnp.sin(ang)


def apply_rope(x, cos, sin):
    half = HEAD_DIM // 2
    x1, x2 = x[..., :half], x[..., half:]
    c, s = cos[None, :, None, :], sin[None, :, None, :]
    return jnp.concatenate([x1 * c - x2 * s, x1 * s + x2 * c], axis=-1).astype(x.dtype)


def project(h, w_in, q_norm_g, k_norm_g):
    B, N, _ = h.shape
    qa, ka, va, qr, kr, vr, gr = jnp.split(h @ w_in, SPLITS, axis=-1)
    heads = lambda a: a.reshape(B, N, -1, HEAD_DIM)
    qa = rms_norm(heads(qa), q_norm_g)
    ka = rms_norm(heads(ka), k_norm_g)
    return qa, ka, heads(va), heads(qr), heads(kr) * ATTN_SCALE, heads(vr), gr


def gqa_block(q, k, v):
    B, Q = q.shape[:2]
    qg = q.reshape(B, Q, ATTN_KV_HEADS, GQA_GROUP, HEAD_DIM)
    s = jnp.einsum('bqkgd,btkd->bkgqt', qg, k, preferred_element_type=jnp.float32) * ATTN_SCALE
    p = jax.nn.softmax(s, axis=-1).astype(v.dtype)
    o = jnp.einsum('bkgqt,btkd->bqkgd', p, v)
    return o.reshape(B, Q, ATTN_WIDTH)


def latent_attention(q, k_lat, v_lat, k_ctx, v_ctx):
    B, N = q.shape[:2]
    k_all = jnp.concatenate([k_ctx, k_lat], axis=1)
    v_all = jnp.concatenate([v_ctx, v_lat], axis=1)
    qb = q.reshape(B, N // QBLOCK, QBLOCK, ATTN_HEADS, HEAD_DIM).swapaxes(0, 1)
    o = lax.map(lambda qi: gqa_block(qi, k_all, v_all), qb)
    return o.swapaxes(0, 1).reshape(B, N, ATTN_WIDTH)


def retention_chunkwise(q, k, v, log_gamma, state0):
    B, N, H, d = q.shape
    C = RET_CHUNK
    nc = N // C
    lg = log_gamma.astype(jnp.float32)
    chunks = lambda a: a.astype(jnp.float32).reshape(B, nc, C, H, d).transpose(1, 0, 3, 2, 4)
    pos = jnp.arange(C, dtype=jnp.float32)
    diff = pos[:, None] - pos[None, :]
    decay_mat = jnp.where(diff >= 0, jnp.exp(lg[:, None, None] * jnp.maximum(diff, 0.0)), 0.0)
    q_decay = jnp.exp(lg[:, None] * (pos + 1.0))[..., None]
    k_decay = jnp.exp(lg[:, None] * (C - 1.0 - pos))[..., None]
    chunk_decay = jnp.exp(lg * C)[:, None, None]

    def step(state, qkv):
        qc, kc, vc = qkv
        inner = jnp.einsum('bhid,bhjd->bhij', qc, kc) * decay_mat
        o = jnp.einsum('bhij,bhjd->bhid', inner, vc) + jnp.einsum('bhid,bhde->bhie', qc, state) * q_decay
        state = state * chunk_decay + jnp.einsum('bhjd,bhje->bhde', kc * k_decay, vc)
        return state, o

    state, o = lax.scan(step, state0.astype(jnp.float32), (chunks(q), chunks(k), chunks(v)))
    return o.transpose(1, 0, 3, 2, 4).reshape(B, N, H, d), state


def bidir_retention(q, k, v, lg_f, lg_b, state_f, state_b):
    o_f, s_f = retention_chunkwise(q, k, v, lg_f, state_f)
    o_b, s_b = retention_chunkwise(q[:, ::-1], k[:, ::-1], v[:, ::-1], lg_b, state_b)
    return o_f + o_b[:, ::-1], s_f, s_b


def merge_heads(att, ret, g_r, w_out):
    B, N = att.shape[:2]
    ret = head_group_norm(ret).reshape(B, N, RET_WIDTH).astype(g_r.dtype) * jax.nn.silu(g_r)
    return jnp.concatenate([att, ret], axis=-1) @ w_out


def swiglu(h, w_ffn_in, w_ffn_out):
    a, b = jnp.split(h @ w_ffn_in, 2, axis=-1)
    return (jax.nn.silu(a) * b) @ w_ffn_out


def setup_inputs(seed: int = 0) -> dict:
    key = jax.random.key(seed)
    ks = jax.random.split(key, 18)
    D = D_MODEL
    nrm = lambda k, shape: jax.random.normal(k, shape, jnp.float32)
    w = lambda k, shape, fan_in, g=1.0: nrm(k, shape) * (g * fan_in ** -0.5)
    gain = lambda k, shape: 1.0 + 0.05 * nrm(k, shape)
    gammas = 1.0 - 2.0 ** (-5.0 - jnp.arange(RET_HEADS, dtype=jnp.float32))
    base = jnp.log(-jnp.log(gammas))
    return {
        'x': nrm(ks[0], (BATCH, SEQ, D)),
        'c': nrm(ks[1], (BATCH, D)),
        'ctx': nrm(ks[2], (BATCH, CTX_LEN, D)),
        'c_ctx': nrm(ks[3], (D,)),
        'w_mod': w(ks[4], (DEPTH, D, N_MOD * D), D, 0.5),
        'b_mod': 0.02 * nrm(ks[5], (DEPTH, N_MOD * D)),
        'g_pre_mix': gain(ks[6], (DEPTH, D)),
        'g_post_mix': gain(ks[7], (DEPTH, D)),
        'g_pre_ffn': gain(ks[8], (DEPTH, D)),
        'g_post_ffn': gain(ks[9], (DEPTH, D)),
        'w_in': w(ks[10], (DEPTH, D, IN_PROJ_WIDTH), D),
        'q_norm_g': gain(ks[11], (DEPTH, HEAD_DIM)),
        'k_norm_g': gain(ks[12], (DEPTH, HEAD_DIM)),
        'ret_decay_fwd': base + 0.1 * nrm(ks[13], (DEPTH, RET_HEADS)),
        'ret_decay_bwd': base + 0.1 * nrm(ks[14], (DEPTH, RET_HEADS)),
        'w_out': w(ks[15], (DEPTH, MIX_WIDTH, D), MIX_WIDTH),
        'w_ffn_in': w(ks[16], (DEPTH, D, 2 * FFN_HIDDEN), D),
        'w_ffn_out': w(ks[17], (DEPTH, FFN_HIDDEN, D), FFN_HIDDEN),
    }


def reference(x, c, ctx, c_ctx, w_mod, b_mod, g_pre_mix, g_post_mix, g_pre_ffn, g_post_ffn,
              w_in, q_norm_g, k_norm_g, ret_decay_fwd, ret_decay_bwd, w_out, w_ffn_in, w_ffn_out):
    B, n_lat, _ = x.shape
    rows = n_lat // GRID_W
    cos, sin = axial_rope(rows)
    for layer in range(DEPTH):
        mod_x = (jax.nn.silu(c) @ w_mod[layer] + b_mod[layer])[:, None, :]
        mod_c = jax.nn.silu(c_ctx) @ w_mod[layer] + b_mod[layer]
        sh_m, sc_m, gt_m, sh_f, sc_f, gt_f = jnp.split(mod_x, N_MOD, axis=-1)
        csh_m, csc_m, cgt_m, csh_f, csc_f, cgt_f = jnp.split(mod_c, N_MOD, axis=-1)
        lg_f = -jnp.exp(ret_decay_fwd[layer].astype(jnp.float32))
        lg_b = -jnp.exp(ret_decay_bwd[layer].astype(jnp.float32))

        hx = modulate(rms_norm(x, g_pre_mix[layer]), sh_m, sc_m)
        hc = modulate(rms_norm(ctx, g_pre_mix[layer]), csh_m, csc_m)
        qa_x, ka_x, va_x, qr_x, kr_x, vr_x, gr_x = project(hx, w_in[layer], q_norm_g[layer], k_norm_g[layer])
        qa_c, ka_c, va_c, qr_c, kr_c, vr_c, gr_c = project(hc, w_in[layer], q_norm_g[layer], k_norm_g[layer])

        zero_state = jnp.zeros((B, RET_HEADS, HEAD_DIM, HEAD_DIM), jnp.float32)
        ret_c, st_f, st_b = bidir_retention(qr_c, kr_c, vr_c, lg_f, lg_b, zero_state, zero_state)

        att_x = latent_attention(apply_rope(qa_x, cos, sin), apply_rope(ka_x, cos, sin), va_x, ka_c, va_c)
        ret_x, _, _ = bidir_retention(apply_rope(qr_x, cos, sin), apply_rope(kr_x, cos, sin), vr_x,
                                      lg_f, lg_b, st_f, st_b)
        mix_x = merge_heads(att_x, ret_x, gr_x, w_out[layer])
        x_new = x + gt_m * rms_norm(mix_x, g_post_mix[layer])
        hf = modulate(rms_norm(x_new, g_pre_ffn[layer]), sh_f, sc_f)
        x_new = x_new + gt_f * rms_norm(swiglu(hf, w_ffn_in[layer], w_ffn_out[layer]), g_post_ffn[layer])

        if layer < DEPTH - 1:
            att_c = gqa_block(qa_c, ka_c, va_c)
            mix_c = merge_heads(att_c, ret_c, gr_c, w_out[layer])
            ctx = ctx + cgt_m * rms_norm(mix_c, g_post_mix[layer])
            hfc = modulate(rms_norm(ctx, g_pre_ffn[layer]), csh_f, csc_f)
            ctx = ctx + cgt_f * rms_norm(swiglu(hfc, w_ffn_in[layer], w_ffn_out[layer]), g_post_ffn[layer])
        x = x_new
    return x
```

```python
import itertools
import numpy as np
from contextlib import ExitStack

import concourse.bass as bass
import concourse.mybir as mybir
from concourse.bass_utils import run_bass_kernel_spmd

F32 = mybir.dt.float32
BF16 = mybir.dt.bfloat16
ALU = mybir.AluOpType
AF = mybir.ActivationFunctionType
AX = mybir.AxisListType

D = 1024
NOWN = 4096
NT = 32
NKT = 66
EPS = 1e-6
HID = 2816
C_ID, C_POS, C_RIJ, C_RJI, C_MGE, C_MLE, C_END = 0, 128, 132, 260, 388, 516, 644


class Buf:
    __slots__ = ("name", "excl", "w", "r")

    def __init__(self, name, excl=False):
        self.name, self.excl, self.w, self.r = name, excl, {}, {}


class Sched:
    EPOCH = 30000

    def __init__(self, nc, es):
        self.nc, self.es = nc, es
        self.engs = {"pe": nc.tensor, "act": nc.scalar, "dve": nc.vector, "pool": nc.gpsimd, "sp": nc.sync}
        self.count = {e: 0 for e in self.engs}
        self.floor = {e: 0 for e in self.engs}
        self.ops = []
        self.waited = {e: {} for e in self.engs}
        self.ewaited = {e: {} for e in self.engs}
        self.semval = {}
        self.cur = {}
        self.nsem = 0
        self.dq = {}
        for q, n in (("sp", 16), ("pool", 8)):
            sems = [es.enter_context(nc.semaphore(f"dq_{q}{i}")) for i in range(n)]
            self.dq[q] = {"sems": sems, "cnt": [0] * n, "rr": 0, "floor": [0] * n}

    def _deps(self, eng, reads, writes, pwrites=()):
        deps = {}

        def add(t):
            k = (t[0], t[1])
            if k not in deps or deps[k][2] < t[2]:
                deps[k] = t

        for b in reads:
            for t in b.w.values():
                add(t)
            if b.excl:
                for t in b.r.values():
                    add(t)
        for b in writes:
            for t in b.w.values():
                add(t)
            for t in b.r.values():
                add(t)
        for b in pwrites:
            for t in b.r.values():
                add(t)
        out = []
        for k, t in deps.items():
            if t[0] == "c":
                if t[1] == "pe" and eng == "pe":
                    continue
                if t[2] < self.floor[t[1]]:
                    continue
            if self.waited[eng].get(k, -1) >= t[2]:
                continue
            self.waited[eng][k] = t[2]
            out.append(t)
        return out

    def _mark(self, tok, reads, writes, pwrites=()):
        k = (tok[0], tok[1])
        for b in reads:
            if b.excl:
                b.w, b.r = {k: tok}, {}
            else:
                b.r[k] = tok
        for b in writes:
            b.w, b.r = {k: tok}, {}
        for b in pwrites:
            b.w[k] = tok

    def op(self, eng, reads, writes, name, pw=(), **kw):
        deps = self._deps(eng, reads, writes, pw)
        idx = self.count[eng]
        self.count[eng] += 1
        self.ops.append(["c", eng, idx, name, kw, deps, None])
        self._mark(("c", eng, idx), reads, writes, pw)

    def pe(self, r, w, name, **kw):
        self.op("pe", r, w, name, **kw)

    def act(self, r, w, name, **kw):
        self.op("act", r, w, name, **kw)

    def dve(self, r, w, name, **kw):
        self.op("dve", r, w, name, **kw)

    def pool(self, r, w, name, **kw):
        self.op("pool", r, w, name, **kw)

    def convert(self, Bsrc, Bdst, dst, src, n):
        a = (n * 9 // 20) // 32 * 32
        b = (n * 18 // 20) // 32 * 32
        self.op("act", [Bsrc], [], "copy", pw=[Bdst], out=dst[:, 0:a], in_=src[:, 0:a])
        self.op("dve", [Bsrc], [], "tensor_copy", pw=[Bdst], out=dst[:, a:b], in_=src[:, a:b])
        if b < n:
            self.op("pool", [Bsrc], [], "tensor_copy", pw=[Bdst], out=dst[:, b:n], in_=src[:, b:n])

    def dma(self, q, reads, writes, **kw):
        dq = self.dq[q]
        k = dq["rr"]
        dq["rr"] = (k + 1) % len(dq["sems"])
        sem = dq["sems"][k]
        deps = self._deps(q, reads, writes)
        if dq["cnt"][k] > dq["floor"][k]:
            key = ("d", sem)
            val = 16 * dq["cnt"][k]
            if self.waited[q].get(key, -1) < val:
                self.waited[q][key] = val
                deps.append(("d", sem, val))
        dq["cnt"][k] += 1
        tok = ("d", sem, 16 * dq["cnt"][k])
        self.ops.append(["d", q, None, "dma_start", kw, deps, sem])
        self._mark(tok, reads, writes)

    def _newsem(self, eng):
        s = self.es.enter_context(self.nc.semaphore(f"c_{eng}{self.nsem}"))
        self.nsem += 1
        self.cur[eng] = [s, 0]

    def flush(self, final=False):
        need = {e: set() for e in self.engs}
        for o in self.ops:
            for t in o[5]:
                if t[0] == "c":
                    need[t[1]].add(t[2])
        last = {}
        for e in self.engs:
            if self.count[e] > self.floor[e]:
                last[e] = self.count[e] - 1
                need[e].add(last[e])
        for o in self.ops:
            kind, eng, idx, name, kw, deps, dsem = o
            E = self.engs[eng]
            for t in deps:
                if t[0] == "c":
                    sem, val = self.semval[(t[1], t[2])]
                else:
                    sem, val = t[1], t[2]
                if self.ewaited[eng].get(sem, -1) < val:
                    E.wait_ge(sem, val)
                    self.ewaited[eng][sem] = val
            ins = getattr(E, name)(**kw)
            if kind == "d":
                ins.then_inc(dsem, 16)
            elif idx in need[eng]:
                if eng not in self.cur or self.cur[eng][1] >= self.EPOCH:
                    self._newsem(eng)
                self.cur[eng][1] += 1
                ins.then_inc(self.cur[eng][0], 1)
                self.semval[(eng, idx)] = (self.cur[eng][0], self.cur[eng][1])
        self.ops = []
        toks = [self.semval[(e, i)] for e, i in last.items()]
        for q, dq in self.dq.items():
            for k, sem in enumerate(dq["sems"]):
                if dq["cnt"][k] > dq["floor"][k]:
                    toks.append((sem, 16 * dq["cnt"][k]))
                    dq["floor"][k] = dq["cnt"][k]
        targets = list(self.engs) if not final else ["sp"]
        for e in targets:
            for sem, val in toks:
                if self.ewaited[e].get(sem, -1) < val:
                    self.engs[e].wait_ge(sem, val)
                    self.ewaited[e][sem] = val
        for e in self.engs:
            self.floor[e] = self.count[e]


def bch(ap2, H, Dd):
    return ap2.unsqueeze(1).to_broadcast([ap2.shape[0], H, Dd])


def bcd(ap2, H, Dd):
    return ap2.unsqueeze(2).to_broadcast([ap2.shape[0], H, Dd])


def v3(ap2, Dd=64):
    return ap2.rearrange("p (h d) -> p h d", d=Dd)


class _Stop(Exception):
    pass


def build_nc(n_blocks=8, do_ffn=True, stage=None):
    nc = bass.Bass("TRN2", target_bir_lowering=False)
    try:
        _build(nc, n_blocks, do_ffn, stage)
    except _Stop:
        pass
    return nc


def _build(nc, n_blocks, do_ffn, stage):
    dr = lambda name, shape, dt=F32: nc.dram_tensor(name, shape, dt, kind="ExternalInput").ap()
    xo_d = dr("xo", [NOWN, D])
    xt_d = dr("xt", [NOWN, D])
    cx_d = dr("cx", [256, D])
    rope_d = dr("rope", [2 * NOWN, 64])
    cfm_d = dr("cfm", [128, 16])
    wmod_d = dr("w_mod", [D, 6 * D])
    bmod_d = dr("bmod", [128, 6 * D])
    gvec_d = dr("gvec", [128, 4 * D])
    gqk_d = dr("gqk", [128, 128])
    dec_d = dr("dec", [128, 16])
    win_d = dr("w_in", [D, 2816])
    wout_d = dr("w_out", [D, D])
    w1_d = dr("w_ffn_in", [D, 2 * HID])
    w2_d = dr("w_ffn_out", [HID, D])
    cst_d = dr("cst", [128, C_END])
    y_d = nc.dram_tensor("y", [NOWN, D], F32, kind="ExternalOutput").ap()
    T_d = nc.dram_tensor("tscr", [NT, 128, 256], BF16, kind="Internal").ap()
    rsc_d = nc.dram_tensor("rscr", [2, 512], F32, kind="Internal").ap()

    with ExitStack() as es:
        E = es.enter_context
        S = Sched(nc, es)
        _uid = [0]

        def sb(name, shape, dt=F32, st=es):
            _uid[0] += 1
            return st.enter_context(nc.sbuf_tensor(f"s{_uid[0]}_{name}", shape, dt))

        def stage_end(name, items):
            if stage != name:
                return
            for nm, ap, bufs in items:
                dt_ = nc.dram_tensor("dbg_" + nm, list(ap.shape), ap.dtype, kind="ExternalOutput").ap()
                S.dma("sp", bufs, [Buf("dbg")], out=dt_, in_=ap)
            S.flush(final=True)
            raise _Stop()

        ps = E(nc.psum_tensor("ps", [128, 8 * 512], F32))
        PB = [Buf(f"psb{i}", excl=True) for i in range(8)]

        def bank(b, n=512, off=0):
            return ps[:, b * 512 + off:b * 512 + off + n]

        def bank16(b):
            return ps[:, b * 512:(b + 1) * 512].bitcast(BF16).rearrange("p (s t) -> p s t", t=128)

        ident = sb("ident", [128, 128], BF16); B_ident = Buf("ident")
        ones_f = sb("ones_f", [128, 64]); B_ones = Buf("ones")
        lg = sb("lg", [128, 16]); B_lg = Buf("lg")
        dtab = sb("dtab", [128, 32]); B_dtab = Buf("dtab")
        cd8 = sb("cd8", [128, 16]); B_cd8 = Buf("cd8")
        cdA = sb("cdA", [128, 4, 64]); cdB = sb("cdB", [128, 4, 64]); B_cd = Buf("cd")
        DsT = sb("DsT", [128, 8, 128]); B_DsT = Buf("DsT")
        gqk = sb("gqk", [128, 128]); B_gqk = Buf("gqk")
        multM = sb("multM", [128, D]); shM = sb("shM", [128, D]); gateM = sb("gateM", [128, D])
        B_multM, B_shM, B_gateM = Buf("multM"), Buf("shM"), Buf("gateM")
        SA = sb("SA", [128, 4, 64]); SBs = sb("SBs", [128, 4, 64]); B_SA, B_SB = Buf("SA"), Buf("SB")
        SAbf = sb("SAbf", [128, 4, 64], BF16); B_SAbf = Buf("SAbf")
        small = sb("small", [128, 64]); B_small = [Buf(f"small{i}") for i in range(8)]
        es0 = ExitStack()
        cst = sb("cst", [128, C_END], F32, es0); B_cst = Buf("cst")

        def sm(i, n=8):
            return small[:, i * 8:i * 8 + n]

        S.dma("sp", [], [B_cst], out=cst[:], in_=cst_d[:, :])
        S.dma("sp", [], [B_lg], out=lg[:], in_=dec_d[:, :])
        S.dma("sp", [], [B_gqk], out=gqk[:], in_=gqk_d[:, :])
        S.dve([B_cst], [B_ident], "tensor_copy", out=ident[:], in_=cst[:, C_ID:C_ID + 128])
        S.dve([], [B_ones], "memset", ap=ones_f[:], constant=1.0)
        S.dve([], [B_SA], "memset", ap=SA[:], constant=0.0)
        S.dve([], [B_SB], "memset", ap=SBs[:], constant=0.0)
        S.act([B_lg], [B_lg], "activation", out=lg[:], in_=lg[:], func=AF.Exp)
        S.dve([B_lg], [B_lg], "tensor_scalar", out=lg[:], in0=lg[:], scalar1=-1.0, scalar2=0.0, op0=ALU.mult, op1=ALU.add)
        for j, (dirn, col, mul) in enumerate(((0, 1, 0.125), (1, 0, 0.125), (0, 2, 1.0), (1, 3, 1.0))):
            S.act([B_lg, B_cst], [B_dtab], "activation", out=dtab[:, j * 8:(j + 1) * 8], in_=lg[:, dirn * 8:(dirn + 1) * 8],
                  func=AF.Exp, scale=cst[:, C_POS + col:C_POS + col + 1])
            if mul != 1.0:
                S.dve([B_dtab], [B_dtab], "tensor_scalar", out=dtab[:, j * 8:(j + 1) * 8], in0=dtab[:, j * 8:(j + 1) * 8],
                      scalar1=mul, scalar2=0.0, op0=ALU.mult, op1=ALU.add)
        S.act([B_lg], [B_cd8], "activation", out=cd8[:], in_=lg[:], func=AF.Exp, scale=128.0)
        for dirn, cdt in ((0, cdA), (1, cdB)):
            c3 = cd8[:, dirn * 8:(dirn + 1) * 8].rearrange("p (hp r) -> p hp r", r=2)
            for r in range(2):
                S.dve([B_cd8], [B_cd], "tensor_copy", out=cdt[64 * r:64 * r + 64, :, :],
                      in_=c3[64 * r:64 * r + 64, :, r].unsqueeze(2).to_broadcast([64, 4, 64]))
        if True:
            ea = sb("ea", [128, 128], F32, es0); eb = sb("eb", [128, 128], F32, es0)
            B_ea, B_eb = Buf("ea"), Buf("eb")
            for h in range(8):
                S.act([B_lg, B_cst], [B_ea], "activation", out=ea[:], in_=cst[:, C_RIJ:C_RIJ + 128], func=AF.Exp, scale=lg[:, h:h + 1])
                S.act([B_lg, B_cst], [B_eb], "activation", out=eb[:], in_=cst[:, C_RJI:C_RJI + 128], func=AF.Exp, scale=lg[:, 8 + h:9 + h])
                S.dve([B_ea, B_cst], [B_ea], "tensor_tensor", out=ea[:], in0=ea[:], in1=cst[:, C_MGE:C_MGE + 128], op=ALU.mult)
                S.dve([B_eb, B_cst], [B_eb], "tensor_tensor", out=eb[:], in0=eb[:], in1=cst[:, C_MLE:C_MLE + 128], op=ALU.mult)
                S.dve([B_ea, B_eb], [B_ea], "tensor_tensor", out=ea[:], in0=ea[:], in1=eb[:], op=ALU.add)
                S.dve([B_ea], [B_DsT], "tensor_scalar", out=DsT[:, (h % 2) * 4 + h // 2, :], in0=ea[:], scalar1=0.125, scalar2=0.0, op0=ALU.mult, op1=ALU.add)
            S.flush()
            es0.close()

        def make_scb(st):
            scb_ = sb("scb", [128, 2, 8, 128], BF16, st); Bs = Buf("scb")
            cf = sb("cf", [128, 16], F32, st); B_cf = Buf("cf")
            S.dma("sp", [], [B_cf], out=cf[:], in_=cfm_d[:, :])
            S.act([B_cf], [B_cf], "activation", out=cf[:], in_=cf[:], func=AF.Silu)
            for v in range(2):
                S.dve([B_cf], [Bs], "tensor_copy", out=scb_[:, v, :, :], in_=cf[:, v * 8:(v + 1) * 8].unsqueeze(2).to_broadcast([128, 8, 128]))
            return scb_, Bs

        def mod_tables(st, jobs):
            scb, B_scb = make_scb(st)
            wst = [sb(f"wst{i}", [128, 8, 512], F32, st) for i in range(2)]; B_wst = [Buf("wst0"), Buf("wst1")]
            wbf = [sb(f"wbf{i}", [128, 8, 512], BF16, st) for i in range(2)]; B_wbf = [Buf("wbf0"), Buf("wbf1")]
            bst = [sb(f"bst{i}", [128, 512], F32, st) for i in range(2)]; B_bst = [Buf("bst0"), Buf("bst1")]
            gst = [sb(f"gst{i}", [128, 512], F32, st) for i in range(2)]; B_gst = [Buf("gst0"), Buf("gst1")]
            tmp = [sb(f"mtmp{i}", [128, 512], F32, st) for i in range(2)]; B_tmp = [Buf("mtmp0"), Buf("mtmp1")]
            slices = sorted(set(j[0] for j in jobs))
            for i, s in enumerate(slices):
                p = i % 2
                S.dma("sp", [], [B_wst[p]], out=wst[p][:], in_=wmod_d[:, s * 512:(s + 1) * 512].rearrange("(kc p) n -> p kc n", p=128))
                S.dma("sp", [], [B_bst[p]], out=bst[p][:], in_=bmod_d[:, s * 512:(s + 1) * 512])
                S.convert(B_wst[p], B_wbf[p], wbf[p][:].rearrange("p a n -> p (a n)"), wst[p][:].rearrange("p a n -> p (a n)"), 4096)
                for (s2, which, kind, gidx, dst, B_dst, c0) in jobs:
                    if s2 != s:
                        continue
                    bk = 2 * p + which
                    for kc in range(8):
                        S.pe([B_scb, B_wbf[p]], [PB[bk]], "matmul", out=bank(bk), lhsT=scb[:, which, kc, :], rhs=wbf[p][:, kc, :],
                             start=(kc == 0), stop=(kc == 7))
                    if kind == "sh":
                        S.dve([PB[bk], B_bst[p]], [B_dst], "tensor_tensor", out=dst[:, c0:c0 + 512], in0=bank(bk), in1=bst[p][:], op=ALU.add)
                    else:
                        gcol = gidx * D + (s % 2) * 512
                        S.dma("sp", [], [B_gst[which]], out=gst[which][:], in_=gvec_d[:, gcol:gcol + 512])
                        S.dve([PB[bk], B_bst[p]], [B_tmp[which]], "tensor_tensor", out=tmp[which][:], in0=bank(bk), in1=bst[p][:], op=ALU.add)
                        if kind == "sc":
                            S.dve([B_tmp[which], B_gst[which]], [B_dst], "scalar_tensor_tensor", out=dst[:, c0:c0 + 512], in0=tmp[which][:],
                                  scalar=1.0, in1=gst[which][:], op0=ALU.add, op1=ALU.mult)
                        else:
                            S.dve([B_tmp[which], B_gst[which]], [B_dst], "tensor_tensor", out=dst[:, c0:c0 + 512], in0=tmp[which][:],
                                  in1=gst[which][:], op=ALU.mult)

        def rstd_inplace(ap, B, n):
            S.act([B], [B], "activation", out=ap, in_=ap, func=AF.Ln, scale=1.0 / n, bias=EPS)
            S.act([B], [B], "activation", out=ap, in_=ap, func=AF.Exp, scale=-0.5)

        def make_front(st, nbuf=2):
            fxs = [sb("fx", [128, D], F32, st) for _ in range(nbuf)]; fBx = [Buf("fx") for _ in range(nbuf)]
            fts = [sb("ft", [128, D], F32, st) for _ in range(nbuf)]; fBt = [Buf("ft") for _ in range(nbuf)]
            fys = [sb("fy", [128, D], BF16, st) for _ in range(nbuf)]; fBy = [Buf("fy") for _ in range(nbuf)]
            fss = sb("fss", [128, 8], F32, st); fBs = [Buf("fss") for _ in range(nbuf)]
            cnt = [0]

            def front(src_ap, src_bufs, mult, B_mult, sh, B_sh, dst_ap, B_dst, trb, keep=None):
                i = cnt[0] % nbuf
                cnt[0] += 1
                fx, fB_x, ft, fB_t, fy, fB_y = fxs[i], fBx[i], fts[i], fBt[i], fys[i], fBy[i]
                ss, Bss = fss[:, i:i + 1], fBs[i]
                xt_, Bx = (fx, fB_x) if keep is None else keep
                S.dma("sp", src_bufs, [Bx], out=xt_[:] if keep is None else xt_, in_=src_ap)
                xin = xt_[:] if keep is None else xt_
                S.act([Bx], [fB_t, Bss], "activation", out=ft[:], in_=xin, func=AF.Square, accum_out=ss)
                rstd_inplace(ss, Bss, D)
                S.dve([Bx, Bss, B_mult], [fB_t], "scalar_tensor_tensor", out=ft[:], in0=xin, scalar=ss, in1=mult[:],
                      op0=ALU.mult, op1=ALU.mult)
                S.pool([fB_t, B_sh], [fB_y], "tensor_tensor", out=fy[:], in0=ft[:], in1=sh[:], op=ALU.add)
                tv = bank16(trb)
                for kc in range(8):
                    S.pe([fB_y, B_ident], [PB[trb]], "transpose", out=tv[:, kc, :], in_=fy[:, kc * 128:(kc + 1) * 128], identity=ident[:])
                S.dve([PB[trb]], [B_dst], "tensor_copy", out=dst_ap, in_=tv)
            front.fx, front.ft, front.B_x, front.B_t = fxs[0], fts[0], fBx[0], fBt[0]
            return front

        def rope(src, B_src, dst, B_lo, B_hi, H, rt, B_rt, tmp, B_tmp):
            s3, d3 = v3(src), v3(dst)
            t3 = v3(tmp, 32)
            cosb, sinb = bch(rt[:, 0:32], H, 32), bch(rt[:, 32:64], H, 32)
            S.dve([B_src, B_rt], [B_lo], "tensor_tensor", out=d3[:, :, 0:32], in0=s3[:, :, 0:32], in1=cosb, op=ALU.mult)
            S.pool([B_src, B_rt], [B_tmp], "tensor_tensor", out=t3, in0=s3[:, :, 32:64], in1=sinb, op=ALU.mult)
            S.dve([B_tmp], [B_lo], "tensor_tensor", out=d3[:, :, 0:32], in0=d3[:, :, 0:32], in1=t3, op=ALU.subtract)
            S.pool([B_src, B_rt], [B_hi], "tensor_tensor", out=d3[:, :, 32:64], in0=s3[:, :, 0:32], in1=sinb, op=ALU.mult)
            S.dve([B_src, B_rt, B_lo], [B_tmp], "tensor_tensor", out=t3, in0=s3[:, :, 32:64], in1=cosb, op=ALU.mult)
            S.pool([B_tmp], [B_hi], "tensor_tensor", out=d3[:, :, 32:64], in0=d3[:, :, 32:64], in1=t3, op=ALU.add)

        def head_rstd(src_ap, src_bufs, H, sq, B_sq, slot):
            S.dve(src_bufs, [B_sq], "tensor_tensor", out=sq[:, 0:H * 64], in0=src_ap, in1=src_ap, op=ALU.mult)
            S.dve([B_sq], [B_small[slot]], "tensor_reduce", out=sm(slot, H), in_=v3(sq[:, 0:H * 64]), axis=AX.X, op=ALU.add)
            rstd_inplace(sm(slot, H), B_small[slot], 64)

        def state_update(St, B_St, cdt, psb, KdT, B_Kd, Vr_, B_Vr):
            for hp in range(4):
                S.pe([B_Kd, B_Vr], [PB[psb]], "matmul", out=bank(psb, 128, hp * 128), lhsT=KdT[:, hp * 128:(hp + 1) * 128],
                     rhs=Vr_[:, hp * 128:(hp + 1) * 128], start=True, stop=True)
            u3 = bank(psb).rearrange("p (hp c) -> p hp c", c=128)
            for r in range(2):
                sl = slice(64 * r, 64 * r + 64)
                S.dve([B_St, B_cd], [B_St], "tensor_tensor", out=St[sl], in0=St[sl], in1=cdt[sl], op=ALU.mult)
                S.dve([B_St, PB[psb]], [B_St], "tensor_tensor", out=St[sl], in0=St[sl], in1=u3[sl, :, 64 * r:64 * r + 64], op=ALU.add)

        with ExitStack() as esBC:
            win = sb("win", [128, 8, 2816], BF16, esBC); B_win = Buf("win")
            KT = sb("KT", [128, NKT * 128], BF16, esBC); B_KT = Buf("KT")
            Vaug = sb("Vaug", [128, NKT, 2, 65], BF16, esBC); B_V = Buf("Vaug")
            S.pool([], [B_V], "memset", ap=Vaug[:, :, :, 64:65], constant=1.0)

            esAB = ExitStack()
            multC = sb("multC", [128, D], F32, esAB); shC = sb("shC", [128, D], F32, esAB)
            B_multC, B_shC = Buf("multC"), Buf("shC")
            with ExitStack() as esA:
                jobs = []
                for s in range(2):
                    jobs.append((s, 0, "sh", 0, shM, B_shM, s * 512))
                    jobs.append((s, 1, "sh", 0, shC, B_shC, s * 512))
                    jobs.append((2 + s, 0, "sc", 0, multM, B_multM, s * 512))
                    jobs.append((2 + s, 1, "sc", 0, multC, B_multC, s * 512))
                    jobs.append((4 + s, 0, "gt", 1, gateM, B_gateM, s * 512))
                mod_tables(esA, jobs)
                wstg = [sb(f"wstg{i}", [128, 2816], F32, esA) for i in range(3)]; B_wstg = [Buf("wstg0"), Buf("wstg1"), Buf("wstg2")]
                for kc in range(8):
                    p = kc % 3
                    S.dma("sp", [], [B_wstg[p]], out=wstg[p][:], in_=win_d[kc * 128:(kc + 1) * 128, :])
                    S.op("dve", [B_wstg[p]], [], "tensor_copy", pw=[B_win], out=win[:, kc, 0:512].rearrange("p (g kv d) -> p g kv d", g=4, kv=2),
                         in_=wstg[p][:, 0:512].rearrange("p (kv g d) -> p g kv d", g=4, kv=2))
                    S.convert(B_wstg[p], B_win, win[:, kc, 512:2816], wstg[p][:, 512:2816], 2304)
                S.flush()
                stage_end("A", [("multM", multM[:], [B_multM]), ("shM", shM[:], [B_shM]), ("gateM", gateM[:], [B_gateM]),
                                ("multC", multC[:], [B_multC]), ("shC", shC[:], [B_shC]), ("DsT", DsT[:], [B_DsT]),
                                ("dtab", dtab[:], [B_dtab]), ("cdA", cdA[:], [B_cd]), ("cdB", cdB[:], [B_cd]),
                                ("win", win[:, 3, :], [B_win]), ("lg", lg[:], [B_lg])])

            with ExitStack() as esB:
                front = make_front(esB)
                hT = sb("hT", [128, 8, 128], BF16, esB); B_hT = Buf("hT")
                rt = sb("rt", [128, 64], F32, esB); B_rt = Buf("rt")
                kk = sb("kk", [128, 640], F32, esB); B_kk = Buf("kk")
                rr = sb("rr", [128, 640], F32, esB); B_rlo, B_rhi = Buf("rlo"), Buf("rhi")
                rtmp = sb("rtmp", [128, 320], F32, esB); B_rtmp = Buf("rtmp")
                sq = sb("sq", [128, 128], F32, esB); B_sq = Buf("sq")
                katok = sb("katok", [128, 128], BF16, esB); B_katok = Buf("katok")
                KdB = sb("KdB", [128, 512], BF16, esB); B_KdB = Buf("KdB")
                KdAc = sb("KdAc", [128, 512], BF16, esB); B_KdAc = Buf("KdAc")
                Vr = sb("Vr", [128, 512], BF16, esB); B_Vr = Buf("Vr")
                Tbf = sb("Tbf", [128, 256], BF16, esB); B_Tbf = Buf("Tbf")
                utmp = sb("utmp", [128, 4, 64], F32, esB); B_utmp = Buf("utmp")
                B_T = [Buf(f"T{c}") for c in range(NT)]

                tiles = [("ctx", 0), ("ctx", 1)] + [("oth", t) for t in range(NT - 1, -1, -1)] + [("own", t) for t in range(NT - 1, -1, -1)]
                hTs = [hT, sb("hTb", [128, 8, 128], BF16, esB)]; B_hTs = [B_hT, Buf("hTb")]
                UB = 7

                def b_info(ti):
                    kind, t = tiles[ti]
                    if kind == "ctx":
                        return cx_d[t * 128:(t + 1) * 128, :], t, multC, B_multC, shC, B_shC
                    if kind == "own":
                        return xo_d[t * 128:(t + 1) * 128, :], 2 + t, multM, B_multM, shM, B_shM
                    return xt_d[t * 128:(t + 1) * 128, :], 2 + NT + t, multM, B_multM, shM, B_shM

                def b_front(ti):
                    src, kidx, mu, Bmu, shh, Bsh = b_info(ti)
                    front(src, [], mu, Bmu, shh, Bsh, hTs[ti % 2][:], B_hTs[ti % 2], 0)

                def b_proj(ti):
                    p = ti % 2
                    for (c0, c1, bk) in ((512, 768, 1 + 3 * p), (1280, 1792, 2 + 3 * p), (1792, 2304, 3 + 3 * p)):
                        for kc in range(8):
                            S.pe([B_hTs[p], B_win], [PB[bk]], "matmul", out=bank(bk, c1 - c0), lhsT=hTs[p][:, kc, :], rhs=win[:, kc, c0:c1],
                                 start=(kc == 0), stop=(kc == 7))

                kk2 = [kk, sb("kk", [128, 640], F32, esB)]; B_kk2 = [B_kk, Buf("kk1")]
                rr2 = [rr, sb("rr", [128, 640], F32, esB)]; B_rlo2, B_rhi2 = [B_rlo, Buf("rlo1")], [B_rhi, Buf("rhi1")]
                rtmp2 = [rtmp, sb("rtmp", [128, 320], F32, esB)]; B_rtmp2 = [B_rtmp, Buf("rtmp1")]
                sq2 = [sq, sb("sq", [128, 128], F32, esB)]; B_sq2 = [B_sq, Buf("sq1")]
                katok2 = [katok, sb("katok", [128, 128], BF16, esB)]; B_katok2 = [B_katok, Buf("katok1")]
                KdB2 = [KdB, sb("KdB", [128, 512], BF16, esB)]; B_KdB2 = [B_KdB, Buf("KdB1")]
                Vr2 = [Vr, sb("Vr", [128, 512], BF16, esB)]; B_Vr2 = [B_Vr, Buf("Vr1")]
                Tbf2 = [Tbf, sb("Tbf", [128, 256], BF16, esB)]; B_Tbf2 = [B_Tbf, Buf("Tbf1")]
                rt2 = [rt, sb("rt", [128, 64], F32, esB)]; B_rt2 = [B_rt, Buf("rt1")]

                def b_post(ti):
                    kind, t = tiles[ti]
                    q = ti % 2
                    kk_, Bkk, rr_, Brlo, Brhi, rtmp_, Brtmp = kk2[q], B_kk2[q], rr2[q], B_rlo2[q], B_rhi2[q], rtmp2[q], B_rtmp2[q]
                    sq_, Bsq, katok_, Bkatok, KdB_, BKdB, Vr_, BVr, Tbf_, BTbf, rt_, Brt = (sq2[q], B_sq2[q], katok2[q], B_katok2[q], KdB2[q], B_KdB2[q],
                                                                                  Vr2[q], B_Vr2[q], Tbf2[q], B_Tbf2[q], rt2[q], B_rt2[q])
                    slot = 1 + 4 * q
                    Bsl = B_small[slot]
                    kidx = b_info(ti)[1]
                    b1, b2, b3 = 1 + 3 * q, 2 + 3 * q, 3 + 3 * q
                    if kind != "ctx":
                        pos0 = t * 128 if kind == "own" else NOWN + t * 128
                        S.dma("sp", [], [Brt], out=rt_[:], in_=rope_d[pos0:pos0 + 128, :])
                    S.act([PB[b1]], [Bkk], "copy", out=kk_[:, 0:128], in_=bank(b1, 128))
                    S.act([PB[b2]], [Bkk], "copy", out=kk_[:, 128:640], in_=bank(b2))
                    yield
                    S.act([PB[b1]], [B_V], "copy", out=Vaug[:, kidx, :, 0:64], in_=v3(bank(b1, 128, 128)))
                    S.act([PB[b3]], [BVr], "copy", out=Vr_[:], in_=bank(b3))
                    S.dve([Bkk], [Bsq], "tensor_tensor", out=sq_[:], in0=kk_[:, 0:128], in1=kk_[:, 0:128], op=ALU.mult)
                    yield
                    S.dve([Bsq], [Bsl], "tensor_reduce", out=sm(slot, 2), in_=v3(sq_[:]), axis=AX.X, op=ALU.add)
                    yield
                    yield
                    S.act([Bsl], [Bsl], "activation", out=sm(slot, 2), in_=sm(slot, 2), func=AF.Ln, scale=1.0 / 64, bias=EPS)
                    yield
                    S.act([Bsl], [Bsl], "activation", out=sm(slot, 2), in_=sm(slot, 2), func=AF.Exp, scale=-0.5)
                    yield
                    S.dve([Bkk, Bsl], [Bkk], "tensor_tensor", out=v3(kk_[:, 0:128]), in0=v3(kk_[:, 0:128]), in1=bcd(sm(slot, 2), 2, 64), op=ALU.mult)
                    yield
                    S.pool([Bkk, B_gqk], [Bkk], "tensor_tensor", out=v3(kk_[:, 0:128]), in0=v3(kk_[:, 0:128]), in1=bch(gqk[:, 64:128], 2, 64), op=ALU.mult)
                    yield
                    if kind != "ctx":
                        s3, d3, t3 = v3(kk_[:]), v3(rr_[:]), v3(rtmp_[:], 32)
                        cosb, sinb = bch(rt_[:, 0:32], 10, 32), bch(rt_[:, 32:64], 10, 32)
                        S.dve([Bkk, Brt], [Brlo], "tensor_tensor", out=d3[:, :, 0:32], in0=s3[:, :, 0:32], in1=cosb, op=ALU.mult)
                        S.pool([Bkk, Brt], [Brtmp], "tensor_tensor", out=t3, in0=s3[:, :, 32:64], in1=sinb, op=ALU.mult)
                        yield
                        S.dve([Brtmp], [Brlo], "tensor_tensor", out=d3[:, :, 0:32], in0=d3[:, :, 0:32], in1=t3, op=ALU.subtract)
                        S.pool([Bkk, Brt], [Brhi], "tensor_tensor", out=d3[:, :, 32:64], in0=s3[:, :, 0:32], in1=sinb, op=ALU.mult)
                        yield
                        S.dve([Bkk, Brt, Brlo], [Brtmp], "tensor_tensor", out=t3, in0=s3[:, :, 32:64], in1=cosb, op=ALU.mult)
                        yield
                        S.pool([Brtmp], [Brhi], "tensor_tensor", out=d3[:, :, 32:64], in0=d3[:, :, 32:64], in1=t3, op=ALU.add)
                        yield
                        ksrc, Bks = rr_, [Brlo, Brhi]
                    else:
                        ksrc, Bks = kk_, [Bkk]
                    S.act(Bks, [Bkatok], "copy", out=katok_[:], in_=ksrc[:, 0:128])
                    S.pool(Bks + [B_dtab], [BKdB], "tensor_tensor", out=v3(KdB_[:]), in0=v3(ksrc[:, 128:640]), in1=bcd(dtab[:, 8:16], 8, 64), op=ALU.mult)
                    yield
                    S.pe([Bkatok, B_ident], [PB[UB]], "transpose", out=bank16(UB)[:, q, :], in_=katok_[:], identity=ident[:])
                    yield
                    S.act([PB[UB]], [B_KT], "copy", out=KT[:, kidx * 128:(kidx + 1) * 128], in_=bank16(UB)[:, q, :])
                    yield
                    if kind == "ctx":
                        S.pool(Bks + [B_dtab], [B_KdAc], "tensor_tensor", out=v3(KdAc[:]), in0=v3(ksrc[:, 128:640]), in1=bcd(dtab[:, 0:8], 8, 64), op=ALU.mult)
                        state_update(SA, B_SA, cdA, UB, KdAc, B_KdAc, Vr_, BVr)
                        if t == 0:
                            state_update(SBs, B_SB, cdB, UB, KdB_, BKdB, Vr_, BVr)
                        else:
                            for hp in range(4):
                                S.pe([BKdB, BVr], [PB[UB]], "matmul", out=bank(UB, 128, hp * 128), lhsT=KdB_[:, hp * 128:(hp + 1) * 128],
                                     rhs=Vr_[:, hp * 128:(hp + 1) * 128], start=True, stop=True)
                            u3 = bank(UB).rearrange("p (hp c) -> p hp c", c=128)
                            for r in range(2):
                                sl = slice(64 * r, 64 * r + 64)
                                S.dve([PB[UB], B_cd], [B_utmp], "tensor_tensor", out=utmp[sl], in0=u3[sl, :, 64 * r:64 * r + 64], in1=cdB[sl], op=ALU.mult)
                                S.dve([B_SB, B_utmp], [B_SB], "tensor_tensor", out=SBs[sl], in0=SBs[sl], in1=utmp[sl], op=ALU.add)
                    else:
                        if kind == "own":
                            S.dve([B_SB], [BTbf], "tensor_copy", out=Tbf_[:], in_=SBs[:].rearrange("p a b -> p (a b)"))
                            S.dma("pool", [BTbf], [B_T[t]], out=T_d[t], in_=Tbf_[:])
                        for hp in range(4):
                            S.pe([BKdB, BVr], [PB[UB]], "matmul", out=bank(UB, 128, hp * 128), lhsT=KdB_[:, hp * 128:(hp + 1) * 128],
                                 rhs=Vr_[:, hp * 128:(hp + 1) * 128], start=True, stop=True)
                        u3 = bank(UB).rearrange("p (hp c) -> p hp c", c=128)
                        for r in range(2):
                            sl = slice(64 * r, 64 * r + 64)
                            S.dve([B_SB, B_cd], [B_SB], "tensor_tensor", out=SBs[sl], in0=SBs[sl], in1=cdB[sl], op=ALU.mult)
                        for r in range(2):
                            sl = slice(64 * r, 64 * r + 64)
                            S.dve([B_SB, PB[UB]], [B_SB], "tensor_tensor", out=SBs[sl], in0=SBs[sl], in1=u3[sl, :, 64 * r:64 * r + 64], op=ALU.add)
                    yield

                active = []

                def pump(until_len):
                    while len(active) > until_len:
                        for gen_ in list(active):
                            if next(gen_, "done") == "done":
                                active.remove(gen_)

                b_front(0)
                for ti in range(len(tiles)):
                    b_proj(ti)
                    if ti + 1 < len(tiles):
                        b_front(ti + 1)
                    active.append(b_post(ti))
                    pump(1)
                pump(0)
                S.dve([B_SA], [B_SAbf], "tensor_copy", out=SAbf[:], in_=SA[:])
                S.flush()
                stage_end("B", [("KT", KT[:], [B_KT]), ("Vaug", Vaug[:], [B_V]), ("SA", SA[:], [B_SA]), ("SB", SBs[:], [B_SB])])
            esAB.close()

            with ExitStack() as esC:
                woutA = sb("woutA", [128, 4, D], BF16, esC); woutR = sb("woutR", [128, 4, D], BF16, esC); B_wout = Buf("wout")
                fx = sb("fx", [128, D], F32, esC); B_fx = Buf("fx")
                ft = sb("ft", [128, D], F32, esC); B_ft = Buf("ft")
                fy = sb("fy", [128, D], BF16, esC); B_fy = Buf("fy")
                for g in range(4):
                    for kv in range(2):
                        r0 = (kv * 4 + g) * 64
                        S.dma("sp", [], [B_fx], out=fx[64 * kv:64 * kv + 64, :], in_=wout_d[r0:r0 + 64, :])
                    S.dve([B_fx], [B_wout], "tensor_copy", out=woutA[:, g, :], in_=fx[:])
                for hp in range(4):
                    S.dma("sp", [], [B_fx], out=fx[:], in_=wout_d[512 + hp * 128:512 + (hp + 1) * 128, :])
                    S.dve([B_fx], [B_wout], "tensor_copy", out=woutR[:, hp, :], in_=fx[:])

                hT = sb("hT", [128, 8, 128], BF16, esC); B_hT = Buf("hT")
                QaT = [sb(f"QaT{i}", [128, 4, 512], BF16, esC) for i in range(2)]; B_QaT = [Buf("QaT0"), Buf("QaT1")]
                retT = [sb(f"retT{i}", [128, 4, 512], BF16, esC) for i in range(2)]; B_retT = [Buf("retT0"), Buf("retT1")]
                attP = [sb(f"attP{i}", [128, 4, 512], BF16, esC) for i in range(2)]
                B_attP = [[Buf(f"attP{i}_{g}") for g in range(4)] for i in range(2)]
                atmp = sb("atmp", [128, 512], BF16, esC); B_atmp = Buf("atmp")
                pT = [sb(f"pT{i}", [128, 1024], BF16, esC) for i in range(3)]; B_pT = [Buf("pT0"), Buf("pT1"), Buf("pT2")]
                rt = sb("rt", [128, 64], F32, esC); B_rt = Buf("rt")
                qq = sb("qq", [128, 1536], F32, esC); B_qq = Buf("qq")
                rr = sb("rr", [128, 1536], F32, esC); B_rlo, B_rhi = Buf("rlo"), Buf("rhi")
                rtmp = sb("rtmp", [128, 768], F32, esC); B_rtmp = Buf("rtmp")
                tok = sb("tok", [128, 5, 512], BF16, esC); B_tok = [Buf(f"tok{i}") for i in range(5)]
                RT = sb("RT", [128, 4, 4, 128], BF16, esC); B_RT = Buf("RT")
                KdA = sb("KdA", [128, 512], BF16, esC); B_KdA = Buf("KdA")
                Vr = sb("Vr", [128, 512], BF16, esC); B_Vr = Buf("Vr")
                graw = sb("graw", [128, 512], F32, esC); B_graw = Buf("graw")
                gexp = sb("gexp", [128, 512], F32, esC); B_gexp = Buf("gexp")
                innD = sb("innD", [128, 8, 128], BF16, esC); B_innD = Buf("innD")
                o32 = sb("o32", [128, 512], F32, esC); B_o32 = Buf("o32")
                sq, B_sq = rtmp[:, 0:512], B_rtmp
                rettok = sb("rettok", [128, 512], BF16, esC); B_rettok = Buf("rettok")
                Tbf = sb("Tbf", [128, 256], BF16, esC); B_Tbf = Buf("Tbf")
                oT = [sb(f"oT{i}", [128, 512], F32, esC) for i in range(2)]; B_oT = [Buf("oT0"), Buf("oT1")]
                xr = sb("xr", [128, D], F32, esC); B_xr = Buf("xr")
                mt, B_mt = ft, B_ft
                rcb = [sb(f"rcb{i}", [128, 512], F32, esC) for i in range(2)]; B_rcb = [Buf("rcb0"), Buf("rcb1")]
                B_rsc = [Buf("rsc0"), Buf("rsc1")]
                B_y = [Buf(f"y{t}") for t in range(NT)]
                BG0, BG1 = 6, 7

                def act_rstd(ap, B, n):
                    S.act([B], [B], "activation", out=ap, in_=ap, func=AF.Ln, scale=1.0 / n, bias=EPS)
                    S.act([B], [B], "activation", out=ap, in_=ap, func=AF.Exp, scale=-0.5)

                def front_part(t):
                    S.dma("sp", [], [B_fx], out=fx[:], in_=xo_d[t * 128:(t + 1) * 128, :])
                    yield 3
                    S.act([B_fx], [B_ft, B_small[0]], "activation", out=ft[:], in_=fx[:], func=AF.Square, accum_out=sm(0, 1))
                    yield 2
                    act_rstd(sm(0, 1), B_small[0], D)
                    S.dve([B_fx, B_small[0], B_multM], [B_ft], "scalar_tensor_tensor", out=ft[:], in0=fx[:], scalar=sm(0, 1), in1=multM[:],
                          op0=ALU.mult, op1=ALU.mult)
                    S.dve([B_ft, B_shM], [B_fy], "tensor_tensor", out=fy[:], in0=ft[:], in1=shM[:], op=ALU.add)

                def tile_work(t, par):
                    tl = t % 4
                    tc = slice(tl * 128, (tl + 1) * 128)
                    S.dma("sp", [], [B_rt], out=rt[:], in_=rope_d[t * 128:(t + 1) * 128, :])
                    S.dma("sp", [B_T[t]], [B_Tbf], out=Tbf[:], in_=T_d[t])
                    yield 1
                    for kc in range(8):
                        S.pe([B_fy, B_ident], [PB[BG0]], "transpose", out=bank16(BG0)[:, kc, :], in_=fy[:, kc * 128:(kc + 1) * 128], identity=ident[:])
                    S.dve([PB[BG0]], [B_hT], "tensor_copy", out=hT[:], in_=bank16(BG0))
                    yield 2
                    for i, c0 in enumerate((0, 768, 1280, 1792, 2304)):
                        bk = BG1 if i % 2 == 0 else BG0
                        for kc in range(8):
                            S.pe([B_hT, B_win], [PB[bk]], "matmul", out=bank(bk), lhsT=hT[:, kc, :], rhs=win[:, kc, c0:c0 + 512],
                                 start=(kc == 0), stop=(kc == 7))
                            if kc == 3:
                                yield 1
                        yield 2
                        if i < 3:
                            S.dve([PB[bk]], [B_qq], "tensor_copy", out=qq[:, i * 512:(i + 1) * 512], in_=bank(bk))
                        elif i == 3:
                            S.dve([PB[bk]], [B_Vr], "tensor_copy", out=Vr[:], in_=bank(bk))
                        else:
                            S.act([PB[bk]], [B_gexp], "activation", out=gexp[:], in_=bank(bk), func=AF.Exp, scale=-1.0)
                            S.dve([PB[bk]], [B_graw], "tensor_copy", out=graw[:], in_=bank(bk))
                    S.dve([B_qq], [B_sq], "tensor_tensor", out=sq[:], in0=qq[:, 0:512], in1=qq[:, 0:512], op=ALU.mult)
                    S.dve([B_sq], [B_small[1]], "tensor_reduce", out=sm(1), in_=v3(sq[:]), axis=AX.X, op=ALU.add)
                    yield 3
                    act_rstd(sm(1), B_small[1], 64)
                    S.dve([B_gexp], [B_gexp], "tensor_scalar", out=gexp[:], in0=gexp[:], scalar1=1.0, scalar2=0.0, op0=ALU.add, op1=ALU.add)
                    S.dve([B_gexp], [B_gexp], "reciprocal", out=gexp[:], in_=gexp[:])
                    S.pool([B_gexp, B_graw], [B_gexp], "tensor_tensor", out=gexp[:], in0=gexp[:], in1=graw[:], op=ALU.mult)
                    S.dve([B_qq, B_small[1]], [B_qq], "tensor_tensor", out=v3(qq[:, 0:512]), in0=v3(qq[:, 0:512]), in1=bcd(sm(1), 8, 64), op=ALU.mult)
                    S.dve([B_qq, B_gqk], [B_qq], "tensor_tensor", out=v3(qq[:, 0:512]), in0=v3(qq[:, 0:512]), in1=bch(gqk[:, 0:64], 8, 64), op=ALU.mult)
                    s3, d3, t3 = v3(qq[:]), v3(rr[:]), v3(rtmp[:], 32)
                    cosb, sinb = bch(rt[:, 0:32], 24, 32), bch(rt[:, 32:64], 24, 32)
                    S.dve([B_qq, B_rt], [B_rlo], "tensor_tensor", out=d3[:, :, 0:32], in0=s3[:, :, 0:32], in1=cosb, op=ALU.mult)
                    S.pool([B_qq, B_rt], [B_rtmp], "tensor_tensor", out=t3, in0=s3[:, :, 32:64], in1=sinb, op=ALU.mult)
                    S.dve([B_rtmp], [B_rlo], "tensor_tensor", out=d3[:, :, 0:32], in0=d3[:, :, 0:32], in1=t3, op=ALU.subtract)
                    S.pool([B_qq, B_rt], [B_rhi], "tensor_tensor", out=d3[:, :, 32:64], in0=s3[:, :, 0:32], in1=sinb, op=ALU.mult)
                    S.dve([B_qq, B_rt, B_rlo], [B_rtmp], "tensor_tensor", out=t3, in0=s3[:, :, 32:64], in1=cosb, op=ALU.mult)
                    S.pool([B_rtmp], [B_rhi], "tensor_tensor", out=d3[:, :, 32:64], in0=d3[:, :, 32:64], in1=t3, op=ALU.add)
                    Brr = [B_rlo, B_rhi]
                    S.act(Brr, [B_tok[0]], "copy", out=tok[:, 0, :], in_=rr[:, 0:512])
                    S.dve(Brr, [B_tok[1]], "tensor_copy", out=tok[:, 1, :], in_=rr[:, 512:1024])
                    S.dve(Brr + [B_dtab], [B_tok[2]], "tensor_tensor", out=v3(tok[:, 2, :]), in0=v3(rr[:, 512:1024]), in1=bcd(dtab[:, 16:24], 8, 64), op=ALU.mult)
                    S.pool(Brr + [B_dtab], [B_tok[3]], "tensor_tensor", out=v3(tok[:, 3, :]), in0=v3(rr[:, 512:1024]), in1=bcd(dtab[:, 24:32], 8, 64), op=ALU.mult)
                    S.dve(Brr, [B_tok[4]], "tensor_copy", out=tok[:, 4, :], in_=rr[:, 1024:1536])
                    S.dve(Brr + [B_dtab], [B_KdA], "tensor_tensor", out=v3(KdA[:]), in0=v3(rr[:, 1024:1536]), in1=bcd(dtab[:, 0:8], 8, 64), op=ALU.mult)
                    yield 1
                    if t + 1 < n_blocks * 4:
                        yield from front_part(t + 1)
                    yield 4
                    for (k, bk, s0) in ((0, BG0, 0), (1, BG0, 4), (2, BG1, 0), (3, BG1, 4)):
                        for j in range(4):
                            S.pe([B_tok[k], B_ident], [PB[bk]], "transpose", out=bank16(bk)[:, s0 + j, :], in_=tok[:, k, j * 128:(j + 1) * 128], identity=ident[:])
                    S.dve([PB[BG0]], [B_QaT[par]], "tensor_copy", out=QaT[par][:, :, tc], in_=bank16(BG0)[:, 0:4, :])
                    S.dve([PB[BG0]], [B_RT], "tensor_copy", out=RT[:, 0, :, :], in_=bank16(BG0)[:, 4:8, :])
                    S.dve([PB[BG1]], [B_RT], "tensor_copy", out=RT[:, 1:3, :, :], in_=bank16(BG1).rearrange("p (a b) t -> p a b t", a=2))
                    yield 2
                    for j in range(4):
                        S.pe([B_tok[4], B_ident], [PB[BG0]], "transpose", out=bank16(BG0)[:, j, :], in_=tok[:, 4, j * 128:(j + 1) * 128], identity=ident[:])
                    S.dve([PB[BG0]], [B_RT], "tensor_copy", out=RT[:, 3, :, :], in_=bank16(BG0)[:, 0:4, :])
                    yield 3
                    for h in range(8):
                        hp, r = h // 2, h % 2
                        sl = slice(64 * r, 64 * r + 64)
                        S.pe([B_RT], [PB[BG0 + r]], "matmul", out=bank(BG0 + r, 128, hp * 128), lhsT=RT[sl, 3, hp, :], rhs=RT[sl, 0, hp, :],
                             start=True, stop=True)
                    S.dve([PB[BG0], PB[BG1], B_DsT], [B_innD], "tensor_tensor", out=innD[:].rearrange("p h t -> p (h t)"), in0=ps[:, BG0 * 512:(BG0 + 2) * 512],
                          in1=DsT[:].rearrange("p h t -> p (h t)"), op=ALU.mult)
                    yield 3
                    for h in range(8):
                        hp, r = h // 2, h % 2
                        sl = slice(64 * r, 64 * r + 64)
                        oc = bank(BG0, 64, h * 64)
                        S.pe([B_innD, B_Vr], [PB[BG0]], "matmul", out=oc, lhsT=innD[:, r * 4 + hp, :], rhs=Vr[:, h * 64:(h + 1) * 64], start=True, stop=False)
                        S.pe([B_RT, B_SAbf], [PB[BG0]], "matmul", out=oc, lhsT=RT[sl, 1, hp, :], rhs=SAbf[sl, hp, :], start=False, stop=False)
                        S.pe([B_RT, B_Tbf], [PB[BG0]], "matmul", out=oc, lhsT=RT[sl, 2, hp, :], rhs=Tbf[sl, hp * 64:(hp + 1) * 64], start=False, stop=True)
                        if h % 4 == 3:
                            yield 1
                    for hp in range(4):
                        S.pe([B_KdA, B_Vr], [PB[BG1]], "matmul", out=bank(BG1, 128, hp * 128), lhsT=KdA[:, hp * 128:(hp + 1) * 128],
                             rhs=Vr[:, hp * 128:(hp + 1) * 128], start=True, stop=True)
                    S.dve([PB[BG0]], [B_o32], "tensor_copy", out=o32[:], in_=bank(BG0))
                    u3 = bank(BG1).rearrange("p (hp c) -> p hp c", c=128)
                    for r in range(2):
                        sl = slice(64 * r, 64 * r + 64)
                        S.dve([B_SA, B_cd], [B_SA], "tensor_tensor", out=SA[sl], in0=SA[sl], in1=cdA[sl], op=ALU.mult)
                        S.dve([B_SA, PB[BG1]], [B_SA], "tensor_tensor", out=SA[sl], in0=SA[sl], in1=u3[sl, :, 64 * r:64 * r + 64], op=ALU.add)
                    S.dve([B_SA], [B_SAbf], "tensor_copy", out=SAbf[:], in_=SA[:])
                    S.dve([B_o32], [B_small[2]], "tensor_reduce", out=sm(2), in_=v3(o32[:]), axis=AX.X, op=ALU.add)
                    S.dve([B_small[2]], [B_small[2]], "tensor_scalar", out=sm(2), in0=sm(2), scalar1=1.0 / 64, scalar2=0.0, op0=ALU.mult, op1=ALU.add)
                    S.dve([B_o32, B_small[2]], [B_o32], "tensor_tensor", out=v3(o32[:]), in0=v3(o32[:]), in1=bcd(sm(2), 8, 64), op=ALU.subtract)
                    S.dve([B_o32], [B_sq], "tensor_tensor", out=sq[:], in0=o32[:], in1=o32[:], op=ALU.mult)
                    S.dve([B_sq], [B_small[3]], "tensor_reduce", out=sm(3), in_=v3(sq[:]), axis=AX.X, op=ALU.add)
                    yield 8
                    act_rstd(sm(3), B_small[3], 64)
                    S.dve([B_o32, B_small[3]], [B_o32], "tensor_tensor", out=v3(o32[:]), in0=v3(o32[:]), in1=bcd(sm(3), 8, 64), op=ALU.mult)
                    S.dve([B_o32, B_gexp], [B_rettok], "tensor_tensor", out=rettok[:], in0=o32[:], in1=gexp[:], op=ALU.mult)
                    yield 6
                    for j in range(4):
                        S.pe([B_rettok, B_ident], [PB[BG0]], "transpose", out=bank16(BG0)[:, j, :], in_=rettok[:, j * 128:(j + 1) * 128], identity=ident[:])
                    S.dve([PB[BG0]], [B_retT[par]], "tensor_copy", out=retT[par][:, :, tc], in_=bank16(BG0)[:, 0:4, :])
                    yield 1

                def outproj_work(t, par):
                    tl = t % 4
                    tc = slice(tl * 128, (tl + 1) * 128)
                    S.dma("sp", [], [B_xr], out=xr[:], in_=xo_d[t * 128:(t + 1) * 128, :])
                    for n in range(2):
                        for g in range(4):
                            S.pe([B_attP[par][g], B_wout], [PB[BG0 + n]], "matmul", out=bank(BG0 + n), lhsT=attP[par][:, g, tc], rhs=woutA[:, g, n * 512:(n + 1) * 512],
                                 start=(g == 0), stop=False)
                        for hp in range(4):
                            S.pe([B_retT[par], B_wout], [PB[BG0 + n]], "matmul", out=bank(BG0 + n), lhsT=retT[par][:, hp, tc], rhs=woutR[:, hp, n * 512:(n + 1) * 512],
                                 start=False, stop=(hp == 3))
                    mix = ps[:, BG0 * 512:(BG0 + 2) * 512]
                    PBm = [PB[BG0], PB[BG1]]
                    yield 4
                    S.act(PBm, [B_mt, B_small[4]], "activation", out=mt[:], in_=mix, func=AF.Square, accum_out=sm(4, 1))
                    yield 2
                    act_rstd(sm(4, 1), B_small[4], D)
                    S.dve(PBm + [B_small[4], B_gateM], [B_mt], "scalar_tensor_tensor", out=mt[:], in0=mix, scalar=sm(4, 1), in1=gateM[:], op0=ALU.mult, op1=ALU.mult)
                    S.pool([B_mt, B_xr], [B_xr], "tensor_tensor", out=xr[:], in0=mt[:], in1=xr[:], op=ALU.add)
                    S.dma("pool", [B_xr], [B_y[t]], out=y_d[t * 128:(t + 1) * 128, :], in_=xr[:])
                    yield 1

                def attention(blk, bg):
                    par = blk % 2
                    steps = [(g, kt) for g in range(4) for kt in range(NKT)]
                    n = len(steps)
                    pending = []

                    def a_qk(i):
                        g, kt = steps[i]
                        sb0 = 2 * (i % 2)
                        for kv in range(2):
                            sl = slice(64 * kv, 64 * kv + 64)
                            S.pe([B_KT, B_QaT[par]], [PB[sb0 + kv]], "matmul", out=bank(sb0 + kv), lhsT=KT[sl, kt * 128:(kt + 1) * 128], rhs=QaT[par][sl, g, :],
                                 start=True, stop=True)

                    def a_ex(i):
                        sb0, pp = 2 * (i % 2), i % 3
                        S.act([PB[sb0], PB[sb0 + 1]], [B_pT[pp]], "activation", out=pT[pp][:], in_=ps[:, sb0 * 512:(sb0 + 2) * 512], func=AF.Exp, scale=0.125)

                    def a_pv(i):
                        g, kt = steps[i]
                        pp = i % 3
                        for kv in range(2):
                            S.pe([B_V, B_pT[pp]], [PB[4 + kv]], "matmul", out=ps[0:65, (4 + kv) * 512:(5 + kv) * 512], lhsT=Vaug[:, kt, kv, :],
                                 rhs=pT[pp][:, kv * 512:(kv + 1) * 512], start=(kt == 0), stop=(kt == NKT - 1))

                    def epi0(g):
                        for kv in range(2):
                            S.act([PB[4 + kv]], [B_oT[kv]], "copy", out=oT[kv][0:65, :], in_=ps[0:65, (4 + kv) * 512:(5 + kv) * 512])

                    def epi1(g):
                        for kv in range(2):
                            S.dve([B_oT[kv]], [B_oT[kv]], "reciprocal", out=oT[kv][64:65, :], in_=oT[kv][64:65, :])

                    def epi2(g):
                        for kv in range(2):
                            S.dma("sp", [B_oT[kv]], [B_rsc[kv]], out=rsc_d[kv:kv + 1, :], in_=oT[kv][64:65, :])

                    def epi2b(g):
                        for kv in range(2):
                            S.dma("sp", [B_rsc[kv]], [B_rcb[kv]], out=rcb[kv][0:64, :], in_=rsc_d[kv:kv + 1, :].to_broadcast([64, 512]))

                    def epi3(g):
                        S.dve([B_oT[0], B_rcb[0]], [B_attP[par][g]], "tensor_tensor", out=attP[par][0:64, g, :], in0=oT[0][0:64, :], in1=rcb[0][0:64, :], op=ALU.mult)
                        S.dve([B_oT[1], B_rcb[1]], [B_atmp], "tensor_tensor", out=atmp[0:64, :], in0=oT[1][0:64, :], in1=rcb[1][0:64, :], op=ALU.mult)
                        S.dma("sp", [B_atmp], [B_attP[par][g]], out=attP[par][64:128, g, :], in_=atmp[0:64, :])

                    def after_pv(i, now):
                        g, kt = steps[i]
                        if kt == NKT - 1:
                            epi0(g)
                            pending.extend([(now + 1, epi1, g), (now + 3, epi2, g), (now + 7, epi2b, g), (now + 11, epi3, g)])

                    a_qk(0)
                    for i in range(n):
                        if i + 1 < n:
                            a_qk(i + 1)
                        a_ex(i)
                        if i >= 1:
                            a_pv(i - 1)
                            after_pv(i - 1, i)
                        for item in [p for p in pending if p[0] <= i]:
                            pending.remove(item)
                            item[1](item[2])
                        if bgw[0] > 0:
                            bgw[0] -= 1
                        else:
                            bgw[0] = (next(bg, None) or 1) - 1
                    a_pv(n - 1)
                    after_pv(n - 1, n)
                    for item in sorted(pending, key=lambda p: p[0]):
                        item[1](item[2])

                def drain(gen):
                    for _ in gen:
                        pass

                bgw = [0]
                drain(front_part(0))
                drain(itertools.chain(*[tile_work(t, 0) for t in range(4)]))
                for blk in range(n_blocks):
                    parts = []
                    if blk > 0:
                        parts += [outproj_work(t, (blk - 1) % 2) for t in range((blk - 1) * 4, blk * 4)]
                    if blk + 1 < n_blocks:
                        parts += [tile_work(t, (blk + 1) % 2) for t in range((blk + 1) * 4, (blk + 2) * 4)]
                    bg = itertools.chain(*parts)
                    attention(blk, bg)
                    drain(bg)
                drain(itertools.chain(*[outproj_work(t, (n_blocks - 1) % 2) for t in range((n_blocks - 1) * 4, n_blocks * 4)]))
                S.flush()
            S.flush()

        if do_ffn:
            with ExitStack() as esD:
                multF = multM; shF = shM; gateF = gateM
                B_multF, B_shF, B_gateF = Buf("multF"), Buf("shF"), Buf("gateF")
                with ExitStack() as esD0:
                    jobs = []
                    for s in range(2):
                        jobs.append((6 + s, 0, "sh", 0, shF, B_shF, s * 512))
                        jobs.append((8 + s, 0, "sc", 2, multF, B_multF, s * 512))
                        jobs.append((10 + s, 0, "gt", 3, gateF, B_gateF, s * 512))
                    mod_tables(esD0, jobs)
                    S.flush()
                front = make_front(esD, nbuf=1)
                w1 = sb("w1", [128, 8, 2 * HID], BF16, esD); B_w1 = Buf("w1")
                w2 = sb("w2", [128, 22, D], BF16, esD); B_w2 = Buf("w2")
                with ExitStack() as esD1:
                    wstg = [sb(f"wstg{i}", [128, 2816], F32, esD1) for i in range(2)]; B_wstg = [Buf("wstg0"), Buf("wstg1")]
                    i = 0
                    for kc in range(8):
                        for half in range(2):
                            p = i % 2; i += 1
                            S.dma("sp", [], [B_wstg[p]], out=wstg[p][:], in_=w1_d[kc * 128:(kc + 1) * 128, half * HID:(half + 1) * HID])
                            S.convert(B_wstg[p], B_w1, w1[:, kc, half * HID:(half + 1) * HID], wstg[p][:, 0:HID], HID)
                    for jj in range(11):
                        p = i % 2; i += 1
                        S.dma("sp", [], [B_wstg[p]], out=wstg[p][:, 0:2048].rearrange("p (a n) -> p a n", a=2),
                              in_=w2_d[jj * 256:(jj + 1) * 256, :].rearrange("(a p) n -> p a n", p=128))
                        S.convert(B_wstg[p], B_w2, w2[:, 2 * jj:2 * jj + 2, :].rearrange("p a n -> p (a n)"), wstg[p][:, 0:2048], 2048)
                    S.flush()
                TS = 2
                NW = TS * 128
                NSB = n_blocks * 4 // TS
                xnb = [sb("xnb", [128, TS, D], F32, esD) for _ in range(2)]; B_xnb = [[Buf(f"xnb{p}_{i}") for i in range(TS)] for p in range(2)]
                hfT = [sb("hfT", [128, 8, NW], BF16, esD) for _ in range(2)]; B_hfT = [[Buf(f"hfT{p}_{i}") for i in range(TS)] for p in range(2)]
                uT = sb("uT", [128, 22, NW], BF16, esD); B_uT = Buf("uT")
                sa = [sb(f"sa{i}", [128, NW], F32, esD) for i in range(2)]; B_sa = [Buf("sa0"), Buf("sa1")]
                _mt = sb("mtD", [128, D], F32, esD); _Bmt = Buf("mtD")
                mts = [_mt, _mt]; B_mts = [_Bmt, _Bmt]

                def d_front(sbk):
                    p = sbk % 2
                    for tl in range(TS):
                        t = sbk * TS + tl
                        tc = slice(tl * 128, (tl + 1) * 128)
                        front(y_d[t * 128:(t + 1) * 128, :], [B_y[t]], multF, B_multF, shF, B_shF, hfT[p][:, :, tc], B_hfT[p][tl], 0,
                              keep=(xnb[p][:, tl, :], B_xnb[p][tl]))

                def d_in(sbk):
                    p = sbk % 2
                    for j in range(22):
                        q = j % 2
                        for (bk, c0) in ((2 * q, j * 128), (2 * q + 1, HID + j * 128)):
                            for kc in range(8):
                                S.pe(B_hfT[p] + [B_w1], [PB[bk]], "matmul", out=bank(bk, NW), lhsT=w1[:, kc, c0:c0 + 128], rhs=hfT[p][:, kc, :],
                                     start=(kc == 0), stop=(kc == 7))
                        S.act([PB[2 * q]], [B_sa[q]], "activation", out=sa[q][:], in_=bank(2 * q, NW), func=AF.Silu)
                        S.dve([B_sa[q], PB[2 * q + 1]], [B_uT], "tensor_tensor", out=uT[:, j, :], in0=sa[q][:], in1=bank(2 * q + 1, NW), op=ALU.mult)

                def d_out(sbk):
                    p = sbk % 2
                    for tl in range(TS):
                        t = sbk * TS + tl
                        tc = slice(tl * 128, (tl + 1) * 128)
                        ob = 4 + 2 * (tl % 2)
                        for n in range(2):
                            for j in range(22):
                                S.pe([B_uT, B_w2], [PB[ob + n]], "matmul", out=bank(ob + n), lhsT=uT[:, j, tc], rhs=w2[:, j, n * 512:(n + 1) * 512],
                                     start=(j == 0), stop=(j == 21))
                        f = ps[:, ob * 512:(ob + 2) * 512]
                        PBm = [PB[ob], PB[ob + 1]]
                        mt_, Bmt = mts[tl % 2], B_mts[tl % 2]
                        S.act(PBm, [Bmt, B_small[4 + tl % 2]], "activation", out=mt_[:], in_=f, func=AF.Square, accum_out=sm(4 + tl % 2, 1))
                        rstd_inplace(sm(4 + tl % 2, 1), B_small[4 + tl % 2], D)
                        S.dve(PBm + [B_small[4 + tl % 2], B_gateF], [Bmt], "scalar_tensor_tensor", out=mt_[:], in0=f, scalar=sm(4 + tl % 2, 1), in1=gateF[:],
                              op0=ALU.mult, op1=ALU.mult)
                        S.pool([Bmt, B_xnb[p][tl]], [Bmt], "tensor_tensor", out=mt_[:], in0=mt_[:], in1=xnb[p][:, tl, :], op=ALU.add)
                        S.dma("pool", [Bmt], [B_y[t]], out=y_d[t * 128:(t + 1) * 128, :], in_=mt_[:])

                d_front(0)
                for sbk in range(NSB):
                    d_in(sbk)
                    if sbk + 1 < NSB:
                        d_front(sbk + 1)
                    d_out(sbk)
                S.flush()
        S.flush(final=True)


def _consts():
    c = np.zeros((128, C_END), np.float32)
    c[:, C_ID:C_ID + 128] = np.eye(128, dtype=np.float32)
    p = np.arange(128, dtype=np.float32)
    c[:, C_POS + 0] = p
    c[:, C_POS + 1] = 127.0 - p
    c[:, C_POS + 2] = p + 1.0
    c[:, C_POS + 3] = 128.0 - p
    j = p[:, None]
    i = p[None, :]
    c[:, C_RIJ:C_RIJ + 128] = np.maximum(i - j, 0.0)
    c[:, C_RJI:C_RJI + 128] = np.maximum(j - i, 0.0)
    c[:, C_MGE:C_MGE + 128] = (i >= j).astype(np.float32)
    c[:, C_MLE:C_MLE + 128] = (j >= i).astype(np.float32)
    return c


def _rope_table(n_lat):
    pos = np.arange(n_lat)
    row = (pos // 64).astype(np.float32)
    col = (pos % 64).astype(np.float32)
    inv = (np.float32(10000.0) ** (-np.arange(16, dtype=np.float32) / np.float32(16))).astype(np.float32)
    ang = np.concatenate([row[:, None] * inv, col[:, None] * inv], axis=-1).astype(np.float32)
    return np.concatenate([np.cos(ang), np.sin(ang)], axis=-1).astype(np.float32)


_NC_CACHE = {}


def prep_inputs(x, c, ctx, c_ctx, w_mod, b_mod, g_pre_mix, g_post_mix, g_pre_ffn, g_post_ffn, w_in, q_norm_g, k_norm_g,
                ret_decay_fwd, ret_decay_bwd, w_out, w_ffn_in, w_ffn_out):
    f32 = lambda a: np.ascontiguousarray(np.asarray(a, dtype=np.float32))
    x, c, ctx, c_ctx = f32(x), f32(c), f32(ctx), f32(c_ctx)
    B, N, _ = x.shape
    rope = _rope_table(N)
    cst = _consts()
    rep = lambda v: np.ascontiguousarray(np.broadcast_to(np.asarray(v, np.float32).reshape(1, -1), (128, np.asarray(v).size)))
    gvec = np.concatenate([rep(g_pre_mix[0]), rep(g_post_mix[0]), rep(g_pre_ffn[0]), rep(g_post_ffn[0])], axis=1)
    gqk = np.concatenate([rep(q_norm_g[0]), rep(k_norm_g[0])], axis=1)
    bmod = rep(b_mod[0])
    shared = {"w_mod": f32(w_mod[0]), "bmod": bmod, "gvec": np.ascontiguousarray(gvec), "gqk": np.ascontiguousarray(gqk),
              "w_in": f32(w_in[0]), "w_out": f32(w_out[0]), "w_ffn_in": f32(w_ffn_in[0]), "w_ffn_out": f32(w_ffn_out[0]), "cst": cst}
    in_maps = []
    for core in range(8):
        b, h = core // 2, core % 2
        if h == 0:
            xf, rf, cf_ = x[b], rope, ctx[b]
            dA, dB = ret_decay_fwd[0], ret_decay_bwd[0]
        else:
            xf, rf, cf_ = x[b, ::-1], rope[::-1], ctx[b, ::-1]
            dA, dB = ret_decay_bwd[0], ret_decay_fwd[0]
        cfm = np.concatenate([c[b].reshape(8, 128).T, c_ctx.reshape(8, 128).T], axis=1)
        m = dict(shared)
        m.update({"xo": np.ascontiguousarray(xf[:NOWN]), "xt": np.ascontiguousarray(xf[NOWN:]), "cx": np.ascontiguousarray(cf_),
                  "rope": np.ascontiguousarray(rf), "cfm": np.ascontiguousarray(cfm, dtype=np.float32),
                  "dec": np.concatenate([rep(dA), rep(dB)], axis=1)})
        in_maps.append(m)
    return in_maps


def kernel(**inputs):
    in_maps = prep_inputs(**inputs)
    B, N, _ = np.asarray(inputs["x"]).shape
    if "nc" not in _NC_CACHE:
        _NC_CACHE["nc"] = build_nc()
    res = run_bass_kernel_spmd(_NC_CACHE["nc"], in_maps, core_ids=list(range(8)))
    out = np.empty((B, N, D), np.float32)
    for core in range(8):
        b, h = core // 2, core % 2
        yv = np.asarray(res.results[core]["y"], np.float32)
        if h == 0:
            out[b, :NOWN] = yv
        else:
            out[b, NOWN:] = yv[::-1]
    return out
```

```python
import itertools
import numpy as np
from contextlib import ExitStack

import concourse.bass as bass
import concourse.mybir as mybir
from concourse.bass_utils import run_bass_kernel_spmd

F32 = mybir.dt.float32
BF16 = mybir.dt.bfloat16
ALU = mybir.AluOpType
AF = mybir.ActivationFunctionType
AX = mybir.AxisListType

D = 1024
NOWN = 4096
NT = 32
NKT = 66
EPS = 1e-6
HID = 2816
C_ID, C_POS, C_RIJ, C_RJI, C_MGE, C_MLE, C_END = 0, 128, 132, 260, 388, 516, 644


class Buf:
    __slots__ = ("name", "excl", "w", "r")

    def __init__(self, name, excl=False):
        self.name, self.excl, self.w, self.r = name, excl, {}, {}


class Sched:
    EPOCH = 30000

    def __init__(self, nc, es):
        self.nc, self.es = nc, es
        self.engs = {"pe": nc.tensor, "act": nc.scalar, "dve": nc.vector, "pool": nc.gpsimd, "sp": nc.sync}
        self.count = {e: 0 for e in self.engs}
        self.floor = {e: 0 for e in self.engs}
        self.ops = []
        self.waited = {e: {} for e in self.engs}
        self.ewaited = {e: {} for e in self.engs}
        self.semval = {}
        self.cur = {}
        self.nsem = 0
        self.dq = {}
        for q, n in (("sp", 16), ("pool", 8)):
            sems = [es.enter_context(nc.semaphore(f"dq_{q}{i}")) for i in range(n)]
            self.dq[q] = {"sems": sems, "cnt": [0] * n, "rr": 0, "floor": [0] * n}

    def _deps(self, eng, reads, writes, pwrites=()):
        deps = {}

        def add(t):
            k = (t[0], t[1])
            if k not in deps or deps[k][2] < t[2]:
                deps[k] = t

        for b in reads:
            for t in b.w.values():
                add(t)
            if b.excl:
                for t in b.r.values():
                    add(t)
        for b in writes:
            for t in b.w.values():
                add(t)
            for t in b.r.values():
                add(t)
        for b in pwrites:
            for t in b.r.values():
                add(t)
        out = []
        for k, t in deps.items():
            if t[0] == "c":
                if t[1] == "pe" and eng == "pe":
                    continue
                if t[2] < self.floor[t[1]]:
                    continue
            if self.waited[eng].get(k, -1) >= t[2]:
                continue
            self.waited[eng][k] = t[2]
            out.append(t)
        return out

    def _mark(self, tok, reads, writes, pwrites=()):
        k = (tok[0], tok[1])
        for b in reads:
            if b.excl:
                b.w, b.r = {k: tok}, {}
            else:
                b.r[k] = tok
        for b in writes:
            b.w, b.r = {k: tok}, {}
        for b in pwrites:
            b.w[k] = tok

    def op(self, eng, reads, writes, name, pw=(), **kw):
        deps = self._deps(eng, reads, writes, pw)
        idx = self.count[eng]
        self.count[eng] += 1
        self.ops.append(["c", eng, idx, name, kw, deps, None])
        self._mark(("c", eng, idx), reads, writes, pw)

    def pe(self, r, w, name, **kw):
        self.op("pe", r, w, name, **kw)

    def act(self, r, w, name, **kw):
        self.op("act", r, w, name, **kw)

    def dve(self, r, w, name, **kw):
        self.op("dve", r, w, name, **kw)

    def pool(self, r, w, name, **kw):
        self.op("pool", r, w, name, **kw)

    def convert(self, Bsrc, Bdst, dst, src, n):
        a = (n * 9 // 20) // 32 * 32
        b = (n * 18 // 20) // 32 * 32
        self.op("act", [Bsrc], [], "copy", pw=[Bdst], out=dst[:, 0:a], in_=src[:, 0:a])
        self.op("dve", [Bsrc], [], "tensor_copy", pw=[Bdst], out=dst[:, a:b], in_=src[:, a:b])
        if b < n:
            self.op("pool", [Bsrc], [], "tensor_copy", pw=[Bdst], out=dst[:, b:n], in_=src[:, b:n])

    def dma(self, q, reads, writes, **kw):
        dq = self.dq[q]
        k = dq["rr"]
        dq["rr"] = (k + 1) % len(dq["sems"])
        sem = dq["sems"][k]
        deps = self._deps(q, reads, writes)
        if dq["cnt"][k] > dq["floor"][k]:
            key = ("d", sem)
            val = 16 * dq["cnt"][k]
            if self.waited[q].get(key, -1) < val:
                self.waited[q][key] = val
                deps.append(("d", sem, val))
        dq["cnt"][k] += 1
        tok = ("d", sem, 16 * dq["cnt"][k])
        self.ops.append(["d", q, None, "dma_start", kw, deps, sem])
        self._mark(tok, reads, writes)

    def _newsem(self, eng):
        s = self.es.enter_context(self.nc.semaphore(f"c_{eng}{self.nsem}"))
        self.nsem += 1
        self.cur[eng] = [s, 0]

    def flush(self, final=False):
        need = {e: set() for e in self.engs}
        for o in self.ops:
            for t in o[5]:
                if t[0] == "c":
                    need[t[1]].add(t[2])
        last = {}
        for e in self.engs:
            if self.count[e] > self.floor[e]:
                last[e] = self.count[e] - 1
                need[e].add(last[e])
        for o in self.ops:
            kind, eng, idx, name, kw, deps, dsem = o
            E = self.engs[eng]
            for t in deps:
                if t[0] == "c":
                    sem, val = self.semval[(t[1], t[2])]
                else:
                    sem, val = t[1], t[2]
                if self.ewaited[eng].get(sem, -1) < val:
                    E.wait_ge(sem, val)
                    self.ewaited[eng][sem] = val
            ins = getattr(E, name)(**kw)
            if kind == "d":
                ins.then_inc(dsem, 16)
            elif idx in need[eng]:
                if eng not in self.cur or self.cur[eng][1] >= self.EPOCH:
                    self._newsem(eng)
                self.cur[eng][1] += 1
                ins.then_inc(self.cur[eng][0], 1)
                self.semval[(eng, idx)] = (self.cur[eng][0], self.cur[eng][1])
        self.ops = []
        toks = [self.semval[(e, i)] for e, i in last.items()]
        for q, dq in self.dq.items():
            for k, sem in enumerate(dq["sems"]):
                if dq["cnt"][k] > dq["floor"][k]:
                    toks.append((sem, 16 * dq["cnt"][k]))
                    dq["floor"][k] = dq["cnt"][k]
        targets = list(self.engs) if not final else ["sp"]
        for e in targets:
            for sem, val in toks:
                if self.ewaited[e].get(sem, -1) < val:
                    self.engs[e].wait_ge(sem, val)
                    self.ewaited[e][sem] = val
        for e in self.engs:
            self.floor[e] = self.count[e]


def bch(ap2, H, Dd):
    return ap2.unsqueeze(1).to_broadcast([ap2.shape[0], H, Dd])


def bcd(ap2, H, Dd):
    return ap2.unsqueeze(2).to_broadcast([ap2.shape[0], H, Dd])


def v3(ap2, Dd=64):
    return ap2.rearrange("p (h d) -> p h d", d=Dd)


class _Stop(Exception):
    pass


def build_nc(n_blocks=8, do_ffn=True, stage=None):
    nc = bass.Bass("TRN2", target_bir_lowering=False)
    try:
        _build(nc, n_blocks, do_ffn, stage)
    except _Stop:
        pass
    return nc


def _build(nc, n_blocks, do_ffn, stage):
    dr = lambda name, shape, dt=F32: nc.dram_tensor(name, shape, dt, kind="ExternalInput").ap()
    xo_d = dr("xo", [NOWN, D])
    xt_d = dr("xt", [NOWN, D])
    cx_d = dr("cx", [256, D])
    rope_d = dr("rope", [2 * NOWN, 64])
    cfm_d = dr("cfm", [128, 16])
    wmod_d = dr("w_mod", [D, 6 * D])
    bmod_d = dr("bmod", [128, 6 * D])
    gvec_d = dr("gvec", [128, 4 * D])
    gqk_d = dr("gqk", [128, 128])
    dec_d = dr("dec", [128, 16])
    win_d = dr("w_in", [D, 2816])
    wout_d = dr("w_out", [D, D])
    w1_d = dr("w_ffn_in", [D, 2 * HID])
    w2_d = dr("w_ffn_out", [HID, D])
    cst_d = dr("cst", [128, C_END])
    y_d = nc.dram_tensor("y", [NOWN, D], F32, kind="ExternalOutput").ap()
    T_d = nc.dram_tensor("tscr", [NT, 128, 256], BF16, kind="Internal").ap()
    rsc_d = nc.dram_tensor("rscr", [2, 512], F32, kind="Internal").ap()

    with ExitStack() as es:
        E = es.enter_context
        S = Sched(nc, es)
        _uid = [0]

        def sb(name, shape, dt=F32, st=es):
            _uid[0] += 1
            return st.enter_context(nc.sbuf_tensor(f"s{_uid[0]}_{name}", shape, dt))

        def stage_end(name, items):
            if stage != name:
                return
            for nm, ap, bufs in items:
                dt_ = nc.dram_tensor("dbg_" + nm, list(ap.shape), ap.dtype, kind="ExternalOutput").ap()
                S.dma("sp", bufs, [Buf("dbg")], out=dt_, in_=ap)
            S.flush(final=True)
            raise _Stop()

        ps = E(nc.psum_tensor("ps", [128, 8 * 512], F32))
        PB = [Buf(f"psb{i}", excl=True) for i in range(8)]

        def bank(b, n=512, off=0):
            return ps[:, b * 512 + off:b * 512 + off + n]

        def bank16(b):
            return ps[:, b * 512:(b + 1) * 512].bitcast(BF16).rearrange("p (s t) -> p s t", t=128)

        ident = sb("ident", [128, 128], BF16); B_ident = Buf("ident")
        ones_f = sb("ones_f", [128, 64]); B_ones = Buf("ones")
        lg = sb("lg", [128, 16]); B_lg = Buf("lg")
        dtab = sb("dtab", [128, 32]); B_dtab = Buf("dtab")
        cd8 = sb("cd8", [128, 16]); B_cd8 = Buf("cd8")
        cdA = sb("cdA", [128, 4, 64]); cdB = sb("cdB", [128, 4, 64]); B_cd = Buf("cd")
        DsT = sb("DsT", [128, 8, 128]); B_DsT = Buf("DsT")
        gqk = sb("gqk", [128, 128]); B_gqk = Buf("gqk")
        multM = sb("multM", [128, D]); shM = sb("shM", [128, D]); gateM = sb("gateM", [128, D])
        B_multM, B_shM, B_gateM = Buf("multM"), Buf("shM"), Buf("gateM")
        SA = sb("SA", [128, 4, 64]); SBs = sb("SBs", [128, 4, 64]); B_SA, B_SB = Buf("SA"), Buf("SB")
        SAbf = sb("SAbf", [128, 4, 64], BF16); B_SAbf = Buf("SAbf")
        small = sb("small", [128, 64]); B_small = [Buf(f"small{i}") for i in range(8)]
        es0 = ExitStack()
        cst = sb("cst", [128, C_END], F32, es0); B_cst = Buf("cst")

        def sm(i, n=8):
            return small[:, i * 8:i * 8 + n]

        S.dma("sp", [], [B_cst], out=cst[:], in_=cst_d[:, :])
        S.dma("sp", [], [B_lg], out=lg[:], in_=dec_d[:, :])
        S.dma("sp", [], [B_gqk], out=gqk[:], in_=gqk_d[:, :])
        S.dve([B_cst], [B_ident], "tensor_copy", out=ident[:], in_=cst[:, C_ID:C_ID + 128])
        S.dve([], [B_ones], "memset", ap=ones_f[:], constant=1.0)
        S.dve([], [B_SA], "memset", ap=SA[:], constant=0.0)
        S.dve([], [B_SB], "memset", ap=SBs[:], constant=0.0)
        S.act([B_lg], [B_lg], "activation", out=lg[:], in_=lg[:], func=AF.Exp)
        S.dve([B_lg], [B_lg], "tensor_scalar", out=lg[:], in0=lg[:], scalar1=-1.0, scalar2=0.0, op0=ALU.mult, op1=ALU.add)
        for j, (dirn, col, mul) in enumerate(((0, 1, 0.125), (1, 0, 0.125), (0, 2, 1.0), (1, 3, 1.0))):
            S.act([B_lg, B_cst], [B_dtab], "activation", out=dtab[:, j * 8:(j + 1) * 8], in_=lg[:, dirn * 8:(dirn + 1) * 8],
                  func=AF.Exp, scale=cst[:, C_POS + col:C_POS + col + 1])
            if mul != 1.0:
                S.dve([B_dtab], [B_dtab], "tensor_scalar", out=dtab[:, j * 8:(j + 1) * 8], in0=dtab[:, j * 8:(j + 1) * 8],
                      scalar1=mul, scalar2=0.0, op0=ALU.mult, op1=ALU.add)
        S.act([B_lg], [B_cd8], "activation", out=cd8[:], in_=lg[:], func=AF.Exp, scale=128.0)
        for dirn, cdt in ((0, cdA), (1, cdB)):
            c3 = cd8[:, dirn * 8:(dirn + 1) * 8].rearrange("p (hp r) -> p hp r", r=2)
            for r in range(2):
                S.dve([B_cd8], [B_cd], "tensor_copy", out=cdt[64 * r:64 * r + 64, :, :],
                      in_=c3[64 * r:64 * r + 64, :, r].unsqueeze(2).to_broadcast([64, 4, 64]))
        if True:
            ea = sb("ea", [128, 128], F32, es0); eb = sb("eb", [128, 128], F32, es0)
            B_ea, B_eb = Buf("ea"), Buf("eb")
            for h in range(8):
                S.act([B_lg, B_cst], [B_ea], "activation", out=ea[:], in_=cst[:, C_RIJ:C_RIJ + 128], func=AF.Exp, scale=lg[:, h:h + 1])
                S.act([B_lg, B_cst], [B_eb], "activation", out=eb[:], in_=cst[:, C_RJI:C_RJI + 128], func=AF.Exp, scale=lg[:, 8 + h:9 + h])
                S.dve([B_ea, B_cst], [B_ea], "tensor_tensor", out=ea[:], in0=ea[:], in1=cst[:, C_MGE:C_MGE + 128], op=ALU.mult)
                S.dve([B_eb, B_cst], [B_eb], "tensor_tensor", out=eb[:], in0=eb[:], in1=cst[:, C_MLE:C_MLE + 128], op=ALU.mult)
                S.dve([B_ea, B_eb], [B_ea], "tensor_tensor", out=ea[:], in0=ea[:], in1=eb[:], op=ALU.add)
                S.dve([B_ea], [B_DsT], "tensor_scalar", out=DsT[:, (h % 2) * 4 + h // 2, :], in0=ea[:], scalar1=0.125, scalar2=0.0, op0=ALU.mult, op1=ALU.add)
            S.flush()
            es0.close()

        def make_scb(st):
            scb_ = sb("scb", [128, 2, 8, 128], BF16, st); Bs = Buf("scb")
            cf = sb("cf", [128, 16], F32, st); B_cf = Buf("cf")
            S.dma("sp", [], [B_cf], out=cf[:], in_=cfm_d[:, :])
            S.act([B_cf], [B_cf], "activation", out=cf[:], in_=cf[:], func=AF.Silu)
            for v in range(2):
                S.dve([B_cf], [Bs], "tensor_copy", out=scb_[:, v, :, :], in_=cf[:, v * 8:(v + 1) * 8].unsqueeze(2).to_broadcast([128, 8, 128]))
            return scb_, Bs

        def mod_tables(st, jobs):
            scb, B_scb = make_scb(st)
            wst = [sb(f"wst{i}", [128, 8, 512], F32, st) for i in range(2)]; B_wst = [Buf("wst0"), Buf("wst1")]
            wbf = [sb(f"wbf{i}", [128, 8, 512], BF16, st) for i in range(2)]; B_wbf = [Buf("wbf0"), Buf("wbf1")]
            bst = [sb(f"bst{i}", [128, 512], F32, st) for i in range(2)]; B_bst = [Buf("bst0"), Buf("bst1")]
            gst = [sb(f"gst{i}", [128, 512], F32, st) for i in range(2)]; B_gst = [Buf("gst0"), Buf("gst1")]
            tmp = [sb(f"mtmp{i}", [128, 512], F32, st) for i in range(2)]; B_tmp = [Buf("mtmp0"), Buf("mtmp1")]
            slices = sorted(set(j[0] for j in jobs))
            for i, s in enumerate(slices):
                p = i % 2
                S.dma("sp", [], [B_wst[p]], out=wst[p][:], in_=wmod_d[:, s * 512:(s + 1) * 512].rearrange("(kc p) n -> p kc n", p=128))
                S.dma("sp", [], [B_bst[p]], out=bst[p][:], in_=bmod_d[:, s * 512:(s + 1) * 512])
                S.convert(B_wst[p], B_wbf[p], wbf[p][:].rearrange("p a n -> p (a n)"), wst[p][:].rearrange("p a n -> p (a n)"), 4096)
                for (s2, which, kind, gidx, dst, B_dst, c0) in jobs:
                    if s2 != s:
                        continue
                    bk = 2 * p + which
                    for kc in range(8):
                        S.pe([B_scb, B_wbf[p]], [PB[bk]], "matmul", out=bank(bk), lhsT=scb[:, which, kc, :], rhs=wbf[p][:, kc, :],
                             start=(kc == 0), stop=(kc == 7))
                    if kind == "sh":
                        S.dve([PB[bk], B_bst[p]], [B_dst], "tensor_tensor", out=dst[:, c0:c0 + 512], in0=bank(bk), in1=bst[p][:], op=ALU.add)
                    else:
                        gcol = gidx * D + (s % 2) * 512
                        S.dma("sp", [], [B_gst[which]], out=gst[which][:], in_=gvec_d[:, gcol:gcol + 512])
                        S.dve([PB[bk], B_bst[p]], [B_tmp[which]], "tensor_tensor", out=tmp[which][:], in0=bank(bk), in1=bst[p][:], op=ALU.add)
                        if kind == "sc":
                            S.dve([B_tmp[which], B_gst[which]], [B_dst], "scalar_tensor_tensor", out=dst[:, c0:c0 + 512], in0=tmp[which][:],
                                  scalar=1.0, in1=gst[which][:], op0=ALU.add, op1=ALU.mult)
                        else:
                            S.dve([B_tmp[which], B_gst[which]], [B_dst], "tensor_tensor", out=dst[:, c0:c0 + 512], in0=tmp[which][:],
                                  in1=gst[which][:], op=ALU.mult)

        def rstd_inplace(ap, B, n):
            S.act([B], [B], "activation", out=ap, in_=ap, func=AF.Ln, scale=1.0 / n, bias=EPS)
            S.act([B], [B], "activation", out=ap, in_=ap, func=AF.Exp, scale=-0.5)

        def make_front(st, nbuf=2):
            fxs = [sb("fx", [128, D], F32, st) for _ in range(nbuf)]; fBx = [Buf("fx") for _ in range(nbuf)]
            fts = [sb("ft", [128, D], F32, st) for _ in range(nbuf)]; fBt = [Buf("ft") for _ in range(nbuf)]
            fys = [sb("fy", [128, D], BF16, st) for _ in range(nbuf)]; fBy = [Buf("fy") for _ in range(nbuf)]
            fss = sb("fss", [128, 8], F32, st); fBs = [Buf("fss") for _ in range(nbuf)]
            cnt = [0]

            def front(src_ap, src_bufs, mult, B_mult, sh, B_sh, dst_ap, B_dst, trb, keep=None):
                i = cnt[0] % nbuf
                cnt[0] += 1
                fx, fB_x, ft, fB_t, fy, fB_y = fxs[i], fBx[i], fts[i], fBt[i], fys[i], fBy[i]
                ss, Bss = fss[:, i:i + 1], fBs[i]
                xt_, Bx = (fx, fB_x) if keep is None else keep
                S.dma("sp", src_bufs, [Bx], out=xt_[:] if keep is None else xt_, in_=src_ap)
                xin = xt_[:] if keep is None else xt_
                S.act([Bx], [fB_t, Bss], "activation", out=ft[:], in_=xin, func=AF.Square, accum_out=ss)
                rstd_inplace(ss, Bss, D)
                S.dve([Bx, Bss, B_mult], [fB_t], "scalar_tensor_tensor", out=ft[:], in0=xin, scalar=ss, in1=mult[:],
                      op0=ALU.mult, op1=ALU.mult)
                S.pool([fB_t, B_sh], [fB_y], "tensor_tensor", out=fy[:], in0=ft[:], in1=sh[:], op=ALU.add)
                tv = bank16(trb)
                for kc in range(8):
                    S.pe([fB_y, B_ident], [PB[trb]], "transpose", out=tv[:, kc, :], in_=fy[:, kc * 128:(kc + 1) * 128], identity=ident[:])
                S.dve([PB[trb]], [B_dst], "tensor_copy", out=dst_ap, in_=tv)
            front.fx, front.ft, front.B_x, front.B_t = fxs[0], fts[0], fBx[0], fBt[0]
            return front

        def rope(src, B_src, dst, B_lo, B_hi, H, rt, B_rt, tmp, B_tmp):
            s3, d3 = v3(src), v3(dst)
            t3 = v3(tmp, 32)
            cosb, sinb = bch(rt[:, 0:32], H, 32), bch(rt[:, 32:64], H, 32)
            S.dve([B_src, B_rt], [B_lo], "tensor_tensor", out=d3[:, :, 0:32], in0=s3[:, :, 0:32], in1=cosb, op=ALU.mult)
            S.pool([B_src, B_rt], [B_tmp], "tensor_tensor", out=t3, in0=s3[:, :, 32:64], in1=sinb, op=ALU.mult)
            S.dve([B_tmp], [B_lo], "tensor_tensor", out=d3[:, :, 0:32], in0=d3[:, :, 0:32], in1=t3, op=ALU.subtract)
            S.pool([B_src, B_rt], [B_hi], "tensor_tensor", out=d3[:, :, 32:64], in0=s3[:, :, 0:32], in1=sinb, op=ALU.mult)
            S.dve([B_src, B_rt, B_lo], [B_tmp], "tensor_tensor", out=t3, in0=s3[:, :, 32:64], in1=cosb, op=ALU.mult)
            S.pool([B_tmp], [B_hi], "tensor_tensor", out=d3[:, :, 32:64], in0=d3[:, :, 32:64], in1=t3, op=ALU.add)

        def head_rstd(src_ap, src_bufs, H, sq, B_sq, slot):
            S.dve(src_bufs, [B_sq], "tensor_tensor", out=sq[:, 0:H * 64], in0=src_ap, in1=src_ap, op=ALU.mult)
            S.dve([B_sq], [B_small[slot]], "tensor_reduce", out=sm(slot, H), in_=v3(sq[:, 0:H * 64]), axis=AX.X, op=ALU.add)
            rstd_inplace(sm(slot, H), B_small[slot], 64)

        def state_update(St, B_St, cdt, psb, KdT, B_Kd, Vr_, B_Vr):
            for hp in range(4):
                S.pe([B_Kd, B_Vr], [PB[psb]], "matmul", out=bank(psb, 128, hp * 128), lhsT=KdT[:, hp * 128:(hp + 1) * 128],
                     rhs=Vr_[:, hp * 128:(hp + 1) * 128], start=True, stop=True)
            u3 = bank(psb).rearrange("p (hp c) -> p hp c", c=128)
            for r in range(2):
                sl = slice(64 * r, 64 * r + 64)
                S.dve([B_St, B_cd], [B_St], "tensor_tensor", out=St[sl], in0=St[sl], in1=cdt[sl], op=ALU.mult)
                S.dve([B_St, PB[psb]], [B_St], "tensor_tensor", out=St[sl], in0=St[sl], in1=u3[sl, :, 64 * r:64 * r + 64], op=ALU.add)

        with ExitStack() as esBC:
            win = sb("win", [128, 8, 2816], BF16, esBC); B_win = Buf("win")
            KT = sb("KT", [128, NKT * 128], BF16, esBC); B_KT = Buf("KT")
            Vaug = sb("Vaug", [128, NKT, 2, 65], BF16, esBC); B_V = Buf("Vaug")
            S.pool([], [B_V], "memset", ap=Vaug[:, :, :, 64:65], constant=1.0)

            esAB = ExitStack()
            multC = sb("multC", [128, D], F32, esAB); shC = sb("shC", [128, D], F32, esAB)
            B_multC, B_shC = Buf("multC"), Buf("shC")
            with ExitStack() as esA:
                jobs = []
                for s in range(2):
                    jobs.append((s, 0, "sh", 0, shM, B_shM, s * 512))
                    jobs.append((s, 1, "sh", 0, shC, B_shC, s * 512))
                    jobs.append((2 + s, 0, "sc", 0, multM, B_multM, s * 512))
                    jobs.append((2 + s, 1, "sc", 0, multC, B_multC, s * 512))
                    jobs.append((4 + s, 0, "gt", 1, gateM, B_gateM, s * 512))
                mod_tables(esA, jobs)
                wstg = [sb(f"wstg{i}", [128, 2816], F32, esA) for i in range(3)]; B_wstg = [Buf("wstg0"), Buf("wstg1"), Buf("wstg2")]
                for kc in range(8):
                    p = kc % 3
                    S.dma("sp", [], [B_wstg[p]], out=wstg[p][:], in_=win_d[kc * 128:(kc + 1) * 128, :])
                    S.op("dve", [B_wstg[p]], [], "tensor_copy", pw=[B_win], out=win[:, kc, 0:512].rearrange("p (g kv d) -> p g kv d", g=4, kv=2),
                         in_=wstg[p][:, 0:512].rearrange("p (kv g d) -> p g kv d", g=4, kv=2))
                    S.convert(B_wstg[p], B_win, win[:, kc, 512:2816], wstg[p][:, 512:2816], 2304)
                S.flush()
                stage_end("A", [("multM", multM[:], [B_multM]), ("shM", shM[:], [B_shM]), ("gateM", gateM[:], [B_gateM]),
                                ("multC", multC[:], [B_multC]), ("shC", shC[:], [B_shC]), ("DsT", DsT[:], [B_DsT]),
                                ("dtab", dtab[:], [B_dtab]), ("cdA", cdA[:], [B_cd]), ("cdB", cdB[:], [B_cd]),
                                ("win", win[:, 3, :], [B_win]), ("lg", lg[:], [B_lg])])

            with ExitStack() as esB:
                front = make_front(esB)
                hT = sb("hT", [128, 8, 128], BF16, esB); B_hT = Buf("hT")
                rt = sb("rt", [128, 64], F32, esB); B_rt = Buf("rt")
                kk = sb("kk", [128, 640], F32, esB); B_kk = Buf("kk")
                rr = sb("rr", [128, 640], F32, esB); B_rlo, B_rhi = Buf("rlo"), Buf("rhi")
                rtmp = sb("rtmp", [128, 320], F32, esB); B_rtmp = Buf("rtmp")
                sq = sb("sq", [128, 128], F32, esB); B_sq = Buf("sq")
                katok = sb("katok", [128, 128], BF16, esB); B_katok = Buf("katok")
                KdB = sb("KdB", [128, 512], BF16, esB); B_KdB = Buf("KdB")
                KdAc = sb("KdAc", [128, 512], BF16, esB); B_KdAc = Buf("KdAc")
                Vr = sb("Vr", [128, 512], BF16, esB); B_Vr = Buf("Vr")
                Tbf = sb("Tbf", [128, 256], BF16, esB); B_Tbf = Buf("Tbf")
                utmp = sb("utmp", [128, 4, 64], F32, esB); B_utmp = Buf("utmp")
                B_T = [Buf(f"T{c}") for c in range(NT)]

                tiles = [("ctx", 0), ("ctx", 1)] + [("oth", t) for t in range(NT - 1, -1, -1)] + [("own", t) for t in range(NT - 1, -1, -1)]
                hTs = [hT, sb("hTb", [128, 8, 128], BF16, esB)]; B_hTs = [B_hT, Buf("hTb")]
                UB = 7

                def b_info(ti):
                    kind, t = tiles[ti]
                    if kind == "ctx":
                        return cx_d[t * 128:(t + 1) * 128, :], t, multC, B_multC, shC, B_shC
                    if kind == "own":
                        return xo_d[t * 128:(t + 1) * 128, :], 2 + t, multM, B_multM, shM, B_shM
                    return xt_d[t * 128:(t + 1) * 128, :], 2 + NT + t, multM, B_multM, shM, B_shM

                def b_front(ti):
                    src, kidx, mu, Bmu, shh, Bsh = b_info(ti)
                    front(src, [], mu, Bmu, shh, Bsh, hTs[ti % 2][:], B_hTs[ti % 2], 0)

                def b_proj(ti):
                    p = ti % 2
                    for (c0, c1, bk) in ((512, 768, 1 + 3 * p), (1280, 1792, 2 + 3 * p), (1792, 2304, 3 + 3 * p)):
                        for kc in range(8):
                            S.pe([B_hTs[p], B_win], [PB[bk]], "matmul", out=bank(bk, c1 - c0), lhsT=hTs[p][:, kc, :], rhs=win[:, kc, c0:c1],
                                 start=(kc == 0), stop=(kc == 7))

                kk2 = [kk, sb("kk", [128, 640], F32, esB)]; B_kk2 = [B_kk, Buf("kk1")]
                rr2 = [rr, sb("rr", [128, 640], F32, esB)]; B_rlo2, B_rhi2 = [B_rlo, Buf("rlo1")], [B_rhi, Buf("rhi1")]
                rtmp2 = [rtmp, sb("rtmp", [128, 320], F32, esB)]; B_rtmp2 = [B_rtmp, Buf("rtmp1")]
                sq2 = [sq, sb("sq", [128, 128], F32, esB)]; B_sq2 = [B_sq, Buf("sq1")]
                katok2 = [katok, sb("katok", [128, 128], BF16, esB)]; B_katok2 = [B_katok, Buf("katok1")]
                KdB2 = [KdB, sb("KdB", [128, 512], BF16, esB)]; B_KdB2 = [B_KdB, Buf("KdB1")]
                Vr2 = [Vr, sb("Vr", [128, 512], BF16, esB)]; B_Vr2 = [B_Vr, Buf("Vr1")]
                Tbf2 = [Tbf, sb("Tbf", [128, 256], BF16, esB)]; B_Tbf2 = [B_Tbf, Buf("Tbf1")]
                rt2 = [rt, sb("rt", [128, 64], F32, esB)]; B_rt2 = [B_rt, Buf("rt1")]

                def b_post(ti):
                    kind, t = tiles[ti]
                    q = ti % 2
                    kk_, Bkk, rr_, Brlo, Brhi, rtmp_, Brtmp = kk2[q], B_kk2[q], rr2[q], B_rlo2[q], B_rhi2[q], rtmp2[q], B_rtmp2[q]
                    sq_, Bsq, katok_, Bkatok, KdB_, BKdB, Vr_, BVr, Tbf_, BTbf, rt_, Brt = (sq2[q], B_sq2[q], katok2[q], B_katok2[q], KdB2[q], B_KdB2[q],
                                                                                  Vr2[q], B_Vr2[q], Tbf2[q], B_Tbf2[q], rt2[q], B_rt2[q])
                    slot = 1 + 4 * q
                    Bsl = B_small[slot]
                    kidx = b_info(ti)[1]
                    b1, b2, b3 = 1 + 3 * q, 2 + 3 * q, 3 + 3 * q
                    if kind != "ctx":
                        pos0 = t * 128 if kind == "own" else NOWN + t * 128
                        S.dma("sp", [], [Brt], out=rt_[:], in_=rope_d[pos0:pos0 + 128, :])
                    S.act([PB[b1]], [Bkk], "copy", out=kk_[:, 0:128], in_=bank(b1, 128))
                    S.act([PB[b2]], [Bkk], "copy", out=kk_[:, 128:640], in_=bank(b2))
                    yield
                    S.act([PB[b1]], [B_V], "copy", out=Vaug[:, kidx, :, 0:64], in_=v3(bank(b1, 128, 128)))
                    S.act([PB[b3]], [BVr], "copy", out=Vr_[:], in_=bank(b3))
                    S.dve([Bkk], [Bsq], "tensor_tensor", out=sq_[:], in0=kk_[:, 0:128], in1=kk_[:, 0:128], op=ALU.mult)
                    yield
                    S.dve([Bsq], [Bsl], "tensor_reduce", out=sm(slot, 2), in_=v3(sq_[:]), axis=AX.X, op=ALU.add)
                    yield
                    yield
                    S.act([Bsl], [Bsl], "activation", out=sm(slot, 2), in_=sm(slot, 2), func=AF.Ln, scale=1.0 / 64, bias=EPS)
                    yield
                    S.act([Bsl], [Bsl], "activation", out=sm(slot, 2), in_=sm(slot, 2), func=AF.Exp, scale=-0.5)
                    yield
                    S.dve([Bkk, Bsl], [Bkk], "tensor_tensor", out=v3(kk_[:, 0:128]), in0=v3(kk_[:, 0:128]), in1=bcd(sm(slot, 2), 2, 64), op=ALU.mult)
                    yield
                    S.pool([Bkk, B_gqk], [Bkk], "tensor_tensor", out=v3(kk_[:, 0:128]), in0=v3(kk_[:, 0:128]), in1=bch(gqk[:, 64:128], 2, 64), op=ALU.mult)
                    yield
                    if kind != "ctx":
                        s3, d3, t3 = v3(kk_[:]), v3(rr_[:]), v3(rtmp_[:], 32)
                        cosb, sinb = bch(rt_[:, 0:32], 10, 32), bch(rt_[:, 32:64], 10, 32)
                        S.dve([Bkk, Brt], [Brlo], "tensor_tensor", out=d3[:, :, 0:32], in0=s3[:, :, 0:32], in1=cosb, op=ALU.mult)
                        S.pool([Bkk, Brt], [Brtmp], "tensor_tensor", out=t3, in0=s3[:, :, 32:64], in1=sinb, op=ALU.mult)
                        yield
                        S.dve([Brtmp], [Brlo], "tensor_tensor", out=d3[:, :, 0:32], in0=d3[:, :, 0:32], in1=t3, op=ALU.subtract)
                        S.pool([Bkk, Brt], [Brhi], "tensor_tensor", out=d3[:, :, 32:64], in0=s3[:, :, 0:32], in1=sinb, op=ALU.mult)
                        yield
                        S.dve([Bkk, Brt, Brlo], [Brtmp], "tensor_tensor", out=t3, in0=s3[:, :, 32:64], in1=cosb, op=ALU.mult)
                        yield
                        S.pool([Brtmp], [Brhi], "tensor_tensor", out=d3[:, :, 32:64], in0=d3[:, :, 32:64], in1=t3, op=ALU.add)
                        yield
                        ksrc, Bks = rr_, [Brlo, Brhi]
                    else:
                        ksrc, Bks = kk_, [Bkk]
                    S.act(Bks, [Bkatok], "copy", out=katok_[:], in_=ksrc[:, 0:128])
                    S.pool(Bks + [B_dtab], [BKdB], "tensor_tensor", out=v3(KdB_[:]), in0=v3(ksrc[:, 128:640]), in1=bcd(dtab[:, 8:16], 8, 64), op=ALU.mult)
                    yield
                    S.pe([Bkatok, B_ident], [PB[UB]], "transpose", out=bank16(UB)[:, q, :], in_=katok_[:], identity=ident[:])
                    yield
                    S.act([PB[UB]], [B_KT], "copy", out=KT[:, kidx * 128:(kidx + 1) * 128], in_=bank16(UB)[:, q, :])
                    yield
                    if kind == "ctx":
                        S.pool(Bks + [B_dtab], [B_KdAc], "tensor_tensor", out=v3(KdAc[:]), in0=v3(ksrc[:, 128:640]), in1=bcd(dtab[:, 0:8], 8, 64), op=ALU.mult)
                        state_update(SA, B_SA, cdA, UB, KdAc, B_KdAc, Vr_, BVr)
                        if t == 0:
                            state_update(SBs, B_SB, cdB, UB, KdB_, BKdB, Vr_, BVr)
                        else:
                            for hp in range(4):
                                S.pe([BKdB, BVr], [PB[UB]], "matmul", out=bank(UB, 128, hp * 128), lhsT=KdB_[:, hp * 128:(hp + 1) * 128],
                                     rhs=Vr_[:, hp * 128:(hp + 1) * 128], start=True, stop=True)
                            u3 = bank(UB).rearrange("p (hp c) -> p hp c", c=128)
                            for r in range(2):
                                sl = slice(64 * r, 64 * r + 64)
                                S.dve([PB[UB], B_cd], [B_utmp], "tensor_tensor", out=utmp[sl], in0=u3[sl, :, 64 * r:64 * r + 64], in1=cdB[sl], op=ALU.mult)
                                S.dve([B_SB, B_utmp], [B_SB], "tensor_tensor", out=SBs[sl], in0=SBs[sl], in1=utmp[sl], op=ALU.add)
                    else:
                        if kind == "own":
                            S.dve([B_SB], [BTbf], "tensor_copy", out=Tbf_[:], in_=SBs[:].rearrange("p a b -> p (a b)"))
                            S.dma("pool", [BTbf], [B_T[t]], out=T_d[t], in_=Tbf_[:])
                        for hp in range(4):
                            S.pe([BKdB, BVr], [PB[UB]], "matmul", out=bank(UB, 128, hp * 128), lhsT=KdB_[:, hp * 128:(hp + 1) * 128],
                                 rhs=Vr_[:, hp * 128:(hp + 1) * 128], start=True, stop=True)
                        u3 = bank(UB).rearrange("p (hp c) -> p hp c", c=128)
                        for r in range(2):
                            sl = slice(64 * r, 64 * r + 64)
                            S.dve([B_SB, B_cd], [B_SB], "tensor_tensor", out=SBs[sl], in0=SBs[sl], in1=cdB[sl], op=ALU.mult)
                        for r in range(2):
                            sl = slice(64 * r, 64 * r + 64)
                            S.dve([B_SB, PB[UB]], [B_SB], "tensor_tensor", out=SBs[sl], in0=SBs[sl], in1=u3[sl, :, 64 * r:64 * r + 64], op=ALU.add)
                    yield

                active = []

                def pump(until_len):
                    while len(active) > until_len:
                        for gen_ in list(active):
                            if next(gen_, "done") == "done":
                                active.remove(gen_)

                b_front(0)
                for ti in range(len(tiles)):
                    b_proj(ti)
                    if ti + 1 < len(tiles):
                        b_front(ti + 1)
                    active.append(b_post(ti))
                    pump(1)
                pump(0)
                S.dve([B_SA], [B_SAbf], "tensor_copy", out=SAbf[:], in_=SA[:])
                S.flush()
                stage_end("B", [("KT", KT[:], [B_KT]), ("Vaug", Vaug[:], [B_V]), ("SA", SA[:], [B_SA]), ("SB", SBs[:], [B_SB])])
            esAB.close()

            with ExitStack() as esC:
                woutA = sb("woutA", [128, 4, D], BF16, esC); woutR = sb("woutR", [128, 4, D], BF16, esC); B_wout = Buf("wout")
                fx = sb("fx", [128, D], F32, esC); B_fx = Buf("fx")
                ft = sb("ft", [128, D], F32, esC); B_ft = Buf("ft")
                fy = sb("fy", [128, D], BF16, esC); B_fy = Buf("fy")
                for g in range(4):
                    for kv in range(2):
                        r0 = (kv * 4 + g) * 64
                        S.dma("sp", [], [B_fx], out=fx[64 * kv:64 * kv + 64, :], in_=wout_d[r0:r0 + 64, :])
                    S.dve([B_fx], [B_wout], "tensor_copy", out=woutA[:, g, :], in_=fx[:])
                for hp in range(4):
                    S.dma("sp", [], [B_fx], out=fx[:], in_=wout_d[512 + hp * 128:512 + (hp + 1) * 128, :])
                    S.dve([B_fx], [B_wout], "tensor_copy", out=woutR[:, hp, :], in_=fx[:])

                hT = sb("hT", [128, 8, 128], BF16, esC); B_hT = Buf("hT")
                QaT = [sb(f"QaT{i}", [128, 4, 512], BF16, esC) for i in range(2)]; B_QaT = [Buf("QaT0"), Buf("QaT1")]
                retT = [sb(f"retT{i}", [128, 4, 512], BF16, esC) for i in range(2)]; B_retT = [Buf("retT0"), Buf("retT1")]
                attP = [sb(f"attP{i}", [128, 4, 512], BF16, esC) for i in range(2)]
                B_attP = [[Buf(f"attP{i}_{g}") for g in range(4)] for i in range(2)]
                atmp = sb("atmp", [128, 512], BF16, esC); B_atmp = Buf("atmp")
                pT = [sb(f"pT{i}", [128, 1024], BF16, esC) for i in range(3)]; B_pT = [Buf("pT0"), Buf("pT1"), Buf("pT2")]
                rt = sb("rt", [128, 64], F32, esC); B_rt = Buf("rt")
                qq = sb("qq", [128, 1536], F32, esC); B_qq = Buf("qq")
                rr = sb("rr", [128, 1536], F32, esC); B_rlo, B_rhi = Buf("rlo"), Buf("rhi")
                rtmp = sb("rtmp", [128, 768], F32, esC); B_rtmp = Buf("rtmp")
                tok = sb("tok", [128, 5, 512], BF16, esC); B_tok = [Buf(f"tok{i}") for i in range(5)]
                RT = sb("RT", [128, 4, 4, 128], BF16, esC); B_RT = Buf("RT")
                KdA = sb("KdA", [128, 512], BF16, esC); B_KdA = Buf("KdA")
                Vr = sb("Vr", [128, 512], BF16, esC); B_Vr = Buf("Vr")
                graw = sb("graw", [128, 512], F32, esC); B_graw = Buf("graw")
                gexp = sb("gexp", [128, 512], F32, esC); B_gexp = Buf("gexp")
                innD = sb("innD", [128, 8, 128], BF16, esC); B_innD = Buf("innD")
                o32 = sb("o32", [128, 512], F32, esC); B_o32 = Buf("o32")
                sq, B_sq = rtmp[:, 0:512], B_rtmp
                rettok = sb("rettok", [128, 512], BF16, esC); B_rettok = Buf("rettok")
                Tbf = sb("Tbf", [128, 256], BF16, esC); B_Tbf = Buf("Tbf")
                oT = [sb(f"oT{i}", [128, 512], F32, esC) for i in range(2)]; B_oT = [Buf("oT0"), Buf("oT1")]
                xr = sb("xr", [128, D], F32, esC); B_xr = Buf("xr")
                mt, B_mt = ft, B_ft
                rcb = [sb(f"rcb{i}", [128, 512], F32, esC) for i in range(2)]; B_rcb = [Buf("rcb0"), Buf("rcb1")]
                B_rsc = [Buf("rsc0"), Buf("rsc1")]
                B_y = [Buf(f"y{t}") for t in range(NT)]
                BG0, BG1 = 6, 7

                def act_rstd(ap, B, n):
                    S.act([B], [B], "activation", out=ap, in_=ap, func=AF.Ln, scale=1.0 / n, bias=EPS)
                    S.act([B], [B], "activation", out=ap, in_=ap, func=AF.Exp, scale=-0.5)

                def front_part(t):
                    S.dma("sp", [], [B_fx], out=fx[:], in_=xo_d[t * 128:(t + 1) * 128, :])
                    yield 3
                    S.act([B_fx], [B_ft, B_small[0]], "activation", out=ft[:], in_=fx[:], func=AF.Square, accum_out=sm(0, 1))
                    yield 2
                    act_rstd(sm(0, 1), B_small[0], D)
                    S.dve([B_fx, B_small[0], B_multM], [B_ft], "scalar_tensor_tensor", out=ft[:], in0=fx[:], scalar=sm(0, 1), in1=multM[:],
                          op0=ALU.mult, op1=ALU.mult)
                    S.dve([B_ft, B_shM], [B_fy], "tensor_tensor", out=fy[:], in0=ft[:], in1=shM[:], op=ALU.add)

                def tile_work(t, par):
                    tl = t % 4
                    tc = slice(tl * 128, (tl + 1) * 128)
                    S.dma("sp", [], [B_rt], out=rt[:], in_=rope_d[t * 128:(t + 1) * 128, :])
                    S.dma("sp", [B_T[t]], [B_Tbf], out=Tbf[:], in_=T_d[t])
                    yield 1
                    for kc in range(8):
                        S.pe([B_fy, B_ident], [PB[BG0]], "transpose", out=bank16(BG0)[:, kc, :], in_=fy[:, kc * 128:(kc + 1) * 128], identity=ident[:])
                    S.dve([PB[BG0]], [B_hT], "tensor_copy", out=hT[:], in_=bank16(BG0))
                    yield 2
                    for i, c0 in enumerate((0, 768, 1280, 1792, 2304)):
                        bk = BG1 if i % 2 == 0 else BG0
                        for kc in range(8):
                            S.pe([B_hT, B_win], [PB[bk]], "matmul", out=bank(bk), lhsT=hT[:, kc, :], rhs=win[:, kc, c0:c0 + 512],
                                 start=(kc == 0), stop=(kc == 7))
                            if kc == 3:
                                yield 1
                        yield 2
                        if i < 3:
                            S.dve([PB[bk]], [B_qq], "tensor_copy", out=qq[:, i * 512:(i + 1) * 512], in_=bank(bk))
                        elif i == 3:
                            S.dve([PB[bk]], [B_Vr], "tensor_copy", out=Vr[:], in_=bank(bk))
                        else:
                            S.act([PB[bk]], [B_gexp], "activation", out=gexp[:], in_=bank(bk), func=AF.Exp, scale=-1.0)
                            S.dve([PB[bk]], [B_graw], "tensor_copy", out=graw[:], in_=bank(bk))
                    S.dve([B_qq], [B_sq], "tensor_tensor", out=sq[:], in0=qq[:, 0:512], in1=qq[:, 0:512], op=ALU.mult)
                    S.dve([B_sq], [B_small[1]], "tensor_reduce", out=sm(1), in_=v3(sq[:]), axis=AX.X, op=ALU.add)
                    yield 3
                    act_rstd(sm(1), B_small[1], 64)
                    S.dve([B_gexp], [B_gexp], "tensor_scalar", out=gexp[:], in0=gexp[:], scalar1=1.0, scalar2=0.0, op0=ALU.add, op1=ALU.add)
                    S.dve([B_gexp], [B_gexp], "reciprocal", out=gexp[:], in_=gexp[:])
                    S.pool([B_gexp, B_graw], [B_gexp], "tensor_tensor", out=gexp[:], in0=gexp[:], in1=graw[:], op=ALU.mult)
                    S.dve([B_qq, B_small[1]], [B_qq], "tensor_tensor", out=v3(qq[:, 0:512]), in0=v3(qq[:, 0:512]), in1=bcd(sm(1), 8, 64), op=ALU.mult)
                    S.dve([B_qq, B_gqk], [B_qq], "tensor_tensor", out=v3(qq[:, 0:512]), in0=v3(qq[:, 0:512]), in1=bch(gqk[:, 0:64], 8, 64), op=ALU.mult)
                    s3, d3, t3 = v3(qq[:]), v3(rr[:]), v3(rtmp[:], 32)
                    cosb, sinb = bch(rt[:, 0:32], 24, 32), bch(rt[:, 32:64], 24, 32)
                    S.dve([B_qq, B_rt], [B_rlo], "tensor_tensor", out=d3[:, :, 0:32], in0=s3[:, :, 0:32], in1=cosb, op=ALU.mult)
                    S.pool([B_qq, B_rt], [B_rtmp], "tensor_tensor", out=t3, in0=s3[:, :, 32:64], in1=sinb, op=ALU.mult)
                    S.dve([B_rtmp], [B_rlo], "tensor_tensor", out=d3[:, :, 0:32], in0=d3[:, :, 0:32], in1=t3, op=ALU.subtract)
                    S.pool([B_qq, B_rt], [B_rhi], "tensor_tensor", out=d3[:, :, 32:64], in0=s3[:, :, 0:32], in1=sinb, op=ALU.mult)
                    S.dve([B_qq, B_rt, B_rlo], [B_rtmp], "tensor_tensor", out=t3, in0=s3[:, :, 32:64], in1=cosb, op=ALU.mult)
                    S.pool([B_rtmp], [B_rhi], "tensor_tensor", out=d3[:, :, 32:64], in0=d3[:, :, 32:64], in1=t3, op=ALU.add)
                    Brr = [B_rlo, B_rhi]
                    S.pool(Brr, [B_tok[0]], "tensor_copy", out=tok[:, 0, :], in_=rr[:, 0:512])
                    S.dve(Brr, [B_tok[1]], "tensor_copy", out=tok[:, 1, :], in_=rr[:, 512:1024])
                    S.dve(Brr + [B_dtab], [B_tok[2]], "tensor_tensor", out=v3(tok[:, 2, :]), in0=v3(rr[:, 512:1024]), in1=bcd(dtab[:, 16:24], 8, 64), op=ALU.mult)
                    S.pool(Brr + [B_dtab], [B_tok[3]], "tensor_tensor", out=v3(tok[:, 3, :]), in0=v3(rr[:, 512:1024]), in1=bcd(dtab[:, 24:32], 8, 64), op=ALU.mult)
                    S.dve(Brr, [B_tok[4]], "tensor_copy", out=tok[:, 4, :], in_=rr[:, 1024:1536])
                    S.dve(Brr + [B_dtab], [B_KdA], "tensor_tensor", out=v3(KdA[:]), in0=v3(rr[:, 1024:1536]), in1=bcd(dtab[:, 0:8], 8, 64), op=ALU.mult)
                    yield 1
                    if t + 1 < n_blocks * 4:
                        yield from front_part(t + 1)
                    yield 4
                    for (k, bk, s0) in ((0, BG0, 0), (1, BG0, 4), (2, BG1, 0), (3, BG1, 4)):
                        for j in range(4):
                            S.pe([B_tok[k], B_ident], [PB[bk]], "transpose", out=bank16(bk)[:, s0 + j, :], in_=tok[:, k, j * 128:(j + 1) * 128], identity=ident[:])
                    S.dve([PB[BG0]], [B_QaT[par]], "tensor_copy", out=QaT[par][:, :, tc], in_=bank16(BG0)[:, 0:4, :])
                    S.dve([PB[BG0]], [B_RT], "tensor_copy", out=RT[:, 0, :, :], in_=bank16(BG0)[:, 4:8, :])
                    S.dve([PB[BG1]], [B_RT], "tensor_copy", out=RT[:, 1:3, :, :], in_=bank16(BG1).rearrange("p (a b) t -> p a b t", a=2))
                    yield 2
                    for j in range(4):
                        S.pe([B_tok[4], B_ident], [PB[BG0]], "transpose", out=bank16(BG0)[:, j, :], in_=tok[:, 4, j * 128:(j + 1) * 128], identity=ident[:])
                    S.dve([PB[BG0]], [B_RT], "tensor_copy", out=RT[:, 3, :, :], in_=bank16(BG0)[:, 0:4, :])
                    yield 3
                    for h in range(8):
                        hp, r = h // 2, h % 2
                        sl = slice(64 * r, 64 * r + 64)
                        S.pe([B_RT], [PB[BG0 + r]], "matmul", out=bank(BG0 + r, 128, hp * 128), lhsT=RT[sl, 3, hp, :], rhs=RT[sl, 0, hp, :],
                             start=True, stop=True)
                    S.dve([PB[BG0], PB[BG1], B_DsT], [B_innD], "tensor_tensor", out=innD[:].rearrange("p h t -> p (h t)"), in0=ps[:, BG0 * 512:(BG0 + 2) * 512],
                          in1=DsT[:].rearrange("p h t -> p (h t)"), op=ALU.mult)
                    yield 3
                    for h in range(8):
                        hp, r = h // 2, h % 2
                        sl = slice(64 * r, 64 * r + 64)
                        oc = bank(BG0, 64, h * 64)
                        S.pe([B_innD, B_Vr], [PB[BG0]], "matmul", out=oc, lhsT=innD[:, r * 4 + hp, :], rhs=Vr[:, h * 64:(h + 1) * 64], start=True, stop=False)
                        S.pe([B_RT, B_SAbf], [PB[BG0]], "matmul", out=oc, lhsT=RT[sl, 1, hp, :], rhs=SAbf[sl, hp, :], start=False, stop=False)
                        S.pe([B_RT, B_Tbf], [PB[BG0]], "matmul", out=oc, lhsT=RT[sl, 2, hp, :], rhs=Tbf[sl, hp * 64:(hp + 1) * 64], start=False, stop=True)
                        if h % 4 == 3:
                            yield 1
                    for hp in range(4):
                        S.pe([B_KdA, B_Vr], [PB[BG1]], "matmul", out=bank(BG1, 128, hp * 128), lhsT=KdA[:, hp * 128:(hp + 1) * 128],
                             rhs=Vr[:, hp * 128:(hp + 1) * 128], start=True, stop=True)
                    S.dve([PB[BG0]], [B_o32], "tensor_copy", out=o32[:], in_=bank(BG0))
                    u3 = bank(BG1).rearrange("p (hp c) -> p hp c", c=128)
                    for r in range(2):
                        sl = slice(64 * r, 64 * r + 64)
                        S.dve([B_SA, B_cd], [B_SA], "tensor_tensor", out=SA[sl], in0=SA[sl], in1=cdA[sl], op=ALU.mult)
                        S.dve([B_SA, PB[BG1]], [B_SA], "tensor_tensor", out=SA[sl], in0=SA[sl], in1=u3[sl, :, 64 * r:64 * r + 64], op=ALU.add)
                    S.dve([B_SA], [B_SAbf], "tensor_copy", out=SAbf[:], in_=SA[:])
                    S.dve([B_o32], [B_small[2]], "tensor_reduce", out=sm(2), in_=v3(o32[:]), axis=AX.X, op=ALU.add)
                    S.dve([B_small[2]], [B_small[2]], "tensor_scalar", out=sm(2), in0=sm(2), scalar1=1.0 / 64, scalar2=0.0, op0=ALU.mult, op1=ALU.add)
                    S.dve([B_o32, B_small[2]], [B_o32], "tensor_tensor", out=v3(o32[:]), in0=v3(o32[:]), in1=bcd(sm(2), 8, 64), op=ALU.subtract)
                    S.dve([B_o32], [B_sq], "tensor_tensor", out=sq[:], in0=o32[:], in1=o32[:], op=ALU.mult)
                    S.dve([B_sq], [B_small[3]], "tensor_reduce", out=sm(3), in_=v3(sq[:]), axis=AX.X, op=ALU.add)
                    yield 8
                    act_rstd(sm(3), B_small[3], 64)
                    S.dve([B_o32, B_small[3]], [B_o32], "tensor_tensor", out=v3(o32[:]), in0=v3(o32[:]), in1=bcd(sm(3), 8, 64), op=ALU.mult)
                    S.dve([B_o32, B_gexp], [B_rettok], "tensor_tensor", out=rettok[:], in0=o32[:], in1=gexp[:], op=ALU.mult)
                    yield 6
                    for j in range(4):
                        S.pe([B_rettok, B_ident], [PB[BG0]], "transpose", out=bank16(BG0)[:, j, :], in_=rettok[:, j * 128:(j + 1) * 128], identity=ident[:])
                    S.dve([PB[BG0]], [B_retT[par]], "tensor_copy", out=retT[par][:, :, tc], in_=bank16(BG0)[:, 0:4, :])
                    yield 1

                def outproj_work(t, par):
                    tl = t % 4
                    tc = slice(tl * 128, (tl + 1) * 128)
                    S.dma("sp", [], [B_xr], out=xr[:], in_=xo_d[t * 128:(t + 1) * 128, :])
                    for n in range(2):
                        for g in range(4):
                            S.pe([B_attP[par][g], B_wout], [PB[BG0 + n]], "matmul", out=bank(BG0 + n), lhsT=attP[par][:, g, tc], rhs=woutA[:, g, n * 512:(n + 1) * 512],
                                 start=(g == 0), stop=False)
                        for hp in range(4):
                            S.pe([B_retT[par], B_wout], [PB[BG0 + n]], "matmul", out=bank(BG0 + n), lhsT=retT[par][:, hp, tc], rhs=woutR[:, hp, n * 512:(n + 1) * 512],
                                 start=False, stop=(hp == 3))
                    mix = ps[:, BG0 * 512:(BG0 + 2) * 512]
                    PBm = [PB[BG0], PB[BG1]]
                    yield 4
                    S.act(PBm, [B_mt, B_small[4]], "activation", out=mt[:], in_=mix, func=AF.Square, accum_out=sm(4, 1))
                    yield 2
                    act_rstd(sm(4, 1), B_small[4], D)
                    S.dve(PBm + [B_small[4], B_gateM], [B_mt], "scalar_tensor_tensor", out=mt[:], in0=mix, scalar=sm(4, 1), in1=gateM[:], op0=ALU.mult, op1=ALU.mult)
                    S.pool([B_mt, B_xr], [B_xr], "tensor_tensor", out=xr[:], in0=mt[:], in1=xr[:], op=ALU.add)
                    S.dma("pool", [B_xr], [B_y[t]], out=y_d[t * 128:(t + 1) * 128, :], in_=xr[:])
                    yield 1

                def attention(blk, bg):
                    par = blk % 2
                    steps = [(g, kt) for g in range(4) for kt in range(NKT)]
                    n = len(steps)
                    pending = []

                    def a_qk(i):
                        g, kt = steps[i]
                        sb0 = 2 * (i % 2)
                        for kv in range(2):
                            sl = slice(64 * kv, 64 * kv + 64)
                            S.pe([B_KT, B_QaT[par]], [PB[sb0 + kv]], "matmul", out=bank(sb0 + kv), lhsT=KT[sl, kt * 128:(kt + 1) * 128], rhs=QaT[par][sl, g, :],
                                 start=True, stop=True)

                    def a_ex(i):
                        sb0, pp = 2 * (i % 2), i % 3
                        S.act([PB[sb0], PB[sb0 + 1]], [B_pT[pp]], "activation", out=pT[pp][:], in_=ps[:, sb0 * 512:(sb0 + 2) * 512], func=AF.Exp, scale=0.125)

                    def a_pv(i):
                        g, kt = steps[i]
                        pp = i % 3
                        for kv in range(2):
                            S.pe([B_V, B_pT[pp]], [PB[4 + kv]], "matmul", out=ps[0:65, (4 + kv) * 512:(5 + kv) * 512], lhsT=Vaug[:, kt, kv, :],
                                 rhs=pT[pp][:, kv * 512:(kv + 1) * 512], start=(kt == 0), stop=(kt == NKT - 1))

                    def epi0(g):
                        for kv in range(2):
                            S.act([PB[4 + kv]], [B_oT[kv]], "copy", out=oT[kv][0:65, :], in_=ps[0:65, (4 + kv) * 512:(5 + kv) * 512])

                    def epi1(g):
                        for kv in range(2):
                            S.dve([B_oT[kv]], [B_oT[kv]], "reciprocal", out=oT[kv][64:65, :], in_=oT[kv][64:65, :])

                    def epi2(g):
                        for kv in range(2):
                            S.dma("sp", [B_oT[kv]], [B_rsc[kv]], out=rsc_d[kv:kv + 1, :], in_=oT[kv][64:65, :])

                    def epi2b(g):
                        for kv in range(2):
                            S.dma("sp", [B_rsc[kv]], [B_rcb[kv]], out=rcb[kv][0:64, :], in_=rsc_d[kv:kv + 1, :].to_broadcast([64, 512]))

                    def epi3(g):
                        S.dve([B_oT[0], B_rcb[0]], [B_attP[par][g]], "tensor_tensor", out=attP[par][0:64, g, :], in0=oT[0][0:64, :], in1=rcb[0][0:64, :], op=ALU.mult)
                        S.dve([B_oT[1], B_rcb[1]], [B_atmp], "tensor_tensor", out=atmp[0:64, :], in0=oT[1][0:64, :], in1=rcb[1][0:64, :], op=ALU.mult)
                        S.dma("sp", [B_atmp], [B_attP[par][g]], out=attP[par][64:128, g, :], in_=atmp[0:64, :])

                    def after_pv(i, now):
                        g, kt = steps[i]
                        if kt == NKT - 1:
                            epi0(g)
                            pending.extend([(now + 1, epi1, g), (now + 3, epi2, g), (now + 7, epi2b, g), (now + 11, epi3, g)])

                    a_qk(0)
                    for i in range(n):
                        if i + 1 < n:
                            a_qk(i + 1)
                        a_ex(i)
                        if i >= 1:
                            a_pv(i - 1)
                            after_pv(i - 1, i)
                        for item in [p for p in pending if p[0] <= i]:
                            pending.remove(item)
                            item[1](item[2])
                        if bgw[0] > 0:
                            bgw[0] -= 1
                        else:
                            bgw[0] = (next(bg, None) or 1) - 1
                    a_pv(n - 1)
                    after_pv(n - 1, n)
                    for item in sorted(pending, key=lambda p: p[0]):
                        item[1](item[2])

                def drain(gen):
                    for _ in gen:
                        pass

                bgw = [0]
                drain(front_part(0))
                drain(itertools.chain(*[tile_work(t, 0) for t in range(4)]))
                for blk in range(n_blocks):
                    parts = []
                    if blk > 0:
                        parts += [outproj_work(t, (blk - 1) % 2) for t in range((blk - 1) * 4, blk * 4)]
                    if blk + 1 < n_blocks:
                        parts += [tile_work(t, (blk + 1) % 2) for t in range((blk + 1) * 4, (blk + 2) * 4)]
                    bg = itertools.chain(*parts)
                    attention(blk, bg)
                    drain(bg)
                drain(itertools.chain(*[outproj_work(t, (n_blocks - 1) % 2) for t in range((n_blocks - 1) * 4, n_blocks * 4)]))
                S.flush()
            S.flush()

        if do_ffn:
            with ExitStack() as esD:
                multF = multM; shF = shM; gateF = gateM
                B_multF, B_shF, B_gateF = Buf("multF"), Buf("shF"), Buf("gateF")
                with ExitStack() as esD0:
                    jobs = []
                    for s in range(2):
                        jobs.append((6 + s, 0, "sh", 0, shF, B_shF, s * 512))
                        jobs.append((8 + s, 0, "sc", 2, multF, B_multF, s * 512))
                        jobs.append((10 + s, 0, "gt", 3, gateF, B_gateF, s * 512))
                    mod_tables(esD0, jobs)
                    S.flush()
                front = make_front(esD, nbuf=1)
                w1 = sb("w1", [128, 8, 2 * HID], BF16, esD); B_w1 = Buf("w1")
                w2 = sb("w2", [128, 22, D], BF16, esD); B_w2 = Buf("w2")
                with ExitStack() as esD1:
                    wstg = [sb(f"wstg{i}", [128, 2816], F32, esD1) for i in range(2)]; B_wstg = [Buf("wstg0"), Buf("wstg1")]
                    i = 0
                    for kc in range(8):
                        for half in range(2):
                            p = i % 2; i += 1
                            S.dma("sp", [], [B_wstg[p]], out=wstg[p][:], in_=w1_d[kc * 128:(kc + 1) * 128, half * HID:(half + 1) * HID])
                            S.convert(B_wstg[p], B_w1, w1[:, kc, half * HID:(half + 1) * HID], wstg[p][:, 0:HID], HID)
                    for jj in range(11):
                        p = i % 2; i += 1
                        S.dma("sp", [], [B_wstg[p]], out=wstg[p][:, 0:2048].rearrange("p (a n) -> p a n", a=2),
                              in_=w2_d[jj * 256:(jj + 1) * 256, :].rearrange("(a p) n -> p a n", p=128))
                        S.convert(B_wstg[p], B_w2, w2[:, 2 * jj:2 * jj + 2, :].rearrange("p a n -> p (a n)"), wstg[p][:, 0:2048], 2048)
                    S.flush()
                TS = 2
                NW = TS * 128
                NSB = n_blocks * 4 // TS
                xnb = [sb("xnb", [128, TS, D], F32, esD) for _ in range(2)]; B_xnb = [[Buf(f"xnb{p}_{i}") for i in range(TS)] for p in range(2)]
                hfT = [sb("hfT", [128, 8, NW], BF16, esD) for _ in range(2)]; B_hfT = [[Buf(f"hfT{p}_{i}") for i in range(TS)] for p in range(2)]
                uT = sb("uT", [128, 22, NW], BF16, esD); B_uT = Buf("uT")
                sa = [sb(f"sa{i}", [128, NW], F32, esD) for i in range(2)]; B_sa = [Buf("sa0"), Buf("sa1")]
                _mt = sb("mtD", [128, D], F32, esD); _Bmt = Buf("mtD")
                mts = [_mt, _mt]; B_mts = [_Bmt, _Bmt]

                def d_front(sbk):
                    p = sbk % 2
                    for tl in range(TS):
                        t = sbk * TS + tl
                        tc = slice(tl * 128, (tl + 1) * 128)
                        front(y_d[t * 128:(t + 1) * 128, :], [B_y[t]], multF, B_multF, shF, B_shF, hfT[p][:, :, tc], B_hfT[p][tl], 0,
                              keep=(xnb[p][:, tl, :], B_xnb[p][tl]))

                def d_in(sbk):
                    p = sbk % 2
                    for j in range(22):
                        q = j % 2
                        for (bk, c0) in ((2 * q, j * 128), (2 * q + 1, HID + j * 128)):
                            for kc in range(8):
                                S.pe(B_hfT[p] + [B_w1], [PB[bk]], "matmul", out=bank(bk, NW), lhsT=w1[:, kc, c0:c0 + 128], rhs=hfT[p][:, kc, :],
                                     start=(kc == 0), stop=(kc == 7))
                        S.act([PB[2 * q]], [B_sa[q]], "activation", out=sa[q][:], in_=bank(2 * q, NW), func=AF.Silu)
                        S.dve([B_sa[q], PB[2 * q + 1]], [B_uT], "tensor_tensor", out=uT[:, j, :], in0=sa[q][:], in1=bank(2 * q + 1, NW), op=ALU.mult)

                def d_out(sbk):
                    p = sbk % 2
                    for tl in range(TS):
                        t = sbk * TS + tl
                        tc = slice(tl * 128, (tl + 1) * 128)
                        ob = 4 + 2 * (tl % 2)
                        for n in range(2):
                            for j in range(22):
                                S.pe([B_uT, B_w2], [PB[ob + n]], "matmul", out=bank(ob + n), lhsT=uT[:, j, tc], rhs=w2[:, j, n * 512:(n + 1) * 512],
                                     start=(j == 0), stop=(j == 21))
                        f = ps[:, ob * 512:(ob + 2) * 512]
                        PBm = [PB[ob], PB[ob + 1]]
                        mt_, Bmt = mts[tl % 2], B_mts[tl % 2]
                        S.act(PBm, [Bmt, B_small[4 + tl % 2]], "activation", out=mt_[:], in_=f, func=AF.Square, accum_out=sm(4 + tl % 2, 1))
                        rstd_inplace(sm(4 + tl % 2, 1), B_small[4 + tl % 2], D)
                        S.dve(PBm + [B_small[4 + tl % 2], B_gateF], [Bmt], "scalar_tensor_tensor", out=mt_[:], in0=f, scalar=sm(4 + tl % 2, 1), in1=gateF[:],
                              op0=ALU.mult, op1=ALU.mult)
                        S.pool([Bmt, B_xnb[p][tl]], [Bmt], "tensor_tensor", out=mt_[:], in0=mt_[:], in1=xnb[p][:, tl, :], op=ALU.add)
                        S.dma("pool", [Bmt], [B_y[t]], out=y_d[t * 128:(t + 1) * 128, :], in_=mt_[:])

                d_front(0)
                for sbk in range(NSB):
                    d_in(sbk)
                    if sbk + 1 < NSB:
                        d_front(sbk + 1)
                    d_out(sbk)
                S.flush()
        S.flush(final=True)


def _consts():
    c = np.zeros((128, C_END), np.float32)
    c[:, C_ID:C_ID + 128] = np.eye(128, dtype=np.float32)
    p = np.arange(128, dtype=np.float32)
    c[:, C_POS + 0] = p
    c[:, C_POS + 1] = 127.0 - p
    c[:, C_POS + 2] = p + 1.0
    c[:, C_POS + 3] = 128.0 - p
    j = p[:, None]
    i = p[None, :]
    c[:, C_RIJ:C_RIJ + 128] = np.maximum(i - j, 0.0)
    c[:, C_RJI:C_RJI + 128] = np.maximum(j - i, 0.0)
    c[:, C_MGE:C_MGE + 128] = (i >= j).astype(np.float32)
    c[:, C_MLE:C_MLE + 128] = (j >= i).astype(np.float32)
    return c


def _rope_table(n_lat):
    pos = np.arange(n_lat)
    row = (pos // 64).astype(np.float32)
    col = (pos % 64).astype(np.float32)
    inv = (np.float32(10000.0) ** (-np.arange(16, dtype=np.float32) / np.float32(16))).astype(np.float32)
    ang = np.concatenate([row[:, None] * inv, col[:, None] * inv], axis=-1).astype(np.float32)
    return np.concatenate([np.cos(ang), np.sin(ang)], axis=-1).astype(np.float32)


_NC_CACHE = {}


def prep_inputs(x, c, ctx, c_ctx, w_mod, b_mod, g_pre_mix, g_post_mix, g_pre_ffn, g_post_ffn, w_in, q_norm_g, k_norm_g,
                ret_decay_fwd, ret_decay_bwd, w_out, w_ffn_in, w_ffn_out):
    f32 = lambda a: np.ascontiguousarray(np.asarray(a, dtype=np.float32))
    x, c, ctx, c_ctx = f32(x), f32(c), f32(ctx), f32(c_ctx)
    B, N, _ = x.shape
    rope = _rope_table(N)
    cst = _consts()
    rep = lambda v: np.ascontiguousarray(np.broadcast_to(np.asarray(v, np.float32).reshape(1, -1), (128, np.asarray(v).size)))
    gvec = np.concatenate([rep(g_pre_mix[0]), rep(g_post_mix[0]), rep(g_pre_ffn[0]), rep(g_post_ffn[0])], axis=1)
    gqk = np.concatenate([rep(q_norm_g[0]), rep(k_norm_g[0])], axis=1)
    bmod = rep(b_mod[0])
    shared = {"w_mod": f32(w_mod[0]), "bmod": bmod, "gvec": np.ascontiguousarray(gvec), "gqk": np.ascontiguousarray(gqk),
              "w_in": f32(w_in[0]), "w_out": f32(w_out[0]), "w_ffn_in": f32(w_ffn_in[0]), "w_ffn_out": f32(w_ffn_out[0]), "cst": cst}
    in_maps = []
    for core in range(8):
        b, h = core // 2, core % 2
        if h == 0:
            xf, rf, cf_ = x[b], rope, ctx[b]
            dA, dB = ret_decay_fwd[0], ret_decay_bwd[0]
        else:
            xf, rf, cf_ = x[b, ::-1], rope[::-1], ctx[b, ::-1]
            dA, dB = ret_decay_bwd[0], ret_decay_fwd[0]
        cfm = np.concatenate([c[b].reshape(8, 128).T, c_ctx.reshape(8, 128).T], axis=1)
        m = dict(shared)
        m.update({"xo": np.ascontiguousarray(xf[:NOWN]), "xt": np.ascontiguousarray(xf[NOWN:]), "cx": np.ascontiguousarray(cf_),
                  "rope": np.ascontiguousarray(rf), "cfm": np.ascontiguousarray(cfm, dtype=np.float32),
                  "dec": np.concatenate([rep(dA), rep(dB)], axis=1)})
        in_maps.append(m)
    return in_maps


def kernel(**inputs):
    in_maps = prep_inputs(**inputs)
    B, N, _ = np.asarray(inputs["x"]).shape
    if "nc" not in _NC_CACHE:
        _NC_CACHE["nc"] = build_nc()
    res = run_bass_kernel_spmd(_NC_CACHE["nc"], in_maps, core_ids=list(range(8)))
    out = np.empty((B, N, D), np.float32)
    for core in range(8):
        b, h = core // 2, core % 2
        yv = np.asarray(res.results[core]["y"], np.float32)
        if h == 0:
            out[b, :NOWN] = yv
        else:
            out[b, NOWN:] = yv[::-1]
    return out
```

```python
import itertools
import numpy as np
from contextlib import ExitStack

import concourse.bass as bass
import concourse.mybir as mybir
from concourse.bass_utils import run_bass_kernel_spmd

F32 = mybir.dt.float32
BF16 = mybir.dt.bfloat16
ALU = mybir.AluOpType
AF = mybir.ActivationFunctionType
AX = mybir.AxisListType

D = 1024
NOWN = 4096
NT = 32
NKT = 66
EPS = 1e-6
HID = 2816
C_ID, C_POS, C_RIJ, C_RJI, C_MGE, C_MLE, C_END = 0, 128, 132, 260, 388, 516, 644


class Buf:
    __slots__ = ("name", "excl", "w", "r")

    def __init__(self, name, excl=False):
        self.name, self.excl, self.w, self.r = name, excl, {}, {}


class Sched:
    EPOCH = 30000

    def __init__(self, nc, es):
        self.nc, self.es = nc, es
        self.engs = {"pe": nc.tensor, "act": nc.scalar, "dve": nc.vector, "pool": nc.gpsimd, "sp": nc.sync}
        self.count = {e: 0 for e in self.engs}
        self.floor = {e: 0 for e in self.engs}
        self.ops = []
        self.waited = {e: {} for e in self.engs}
        self.ewaited = {e: {} for e in self.engs}
        self.semval = {}
        self.cur = {}
        self.nsem = 0
        self.dq = {}
        for q, n in (("sp", 16), ("pool", 8)):
            sems = [es.enter_context(nc.semaphore(f"dq_{q}{i}")) for i in range(n)]
            self.dq[q] = {"sems": sems, "cnt": [0] * n, "rr": 0, "floor": [0] * n}

    def _deps(self, eng, reads, writes, pwrites=()):
        deps = {}

        def add(t):
            k = (t[0], t[1])
            if k not in deps or deps[k][2] < t[2]:
                deps[k] = t

        for b in reads:
            for t in b.w.values():
                add(t)
            if b.excl:
                for t in b.r.values():
                    add(t)
        for b in writes:
            for t in b.w.values():
                add(t)
            for t in b.r.values():
                add(t)
        for b in pwrites:
            for t in b.r.values():
                add(t)
        out = []
        for k, t in deps.items():
            if t[0] == "c":
                if t[1] == "pe" and eng == "pe":
                    continue
                if t[2] < self.floor[t[1]]:
                    continue
            if self.waited[eng].get(k, -1) >= t[2]:
                continue
            self.waited[eng][k] = t[2]
            out.append(t)
        return out

    def _mark(self, tok, reads, writes, pwrites=()):
        k = (tok[0], tok[1])
        for b in reads:
            if b.excl:
                b.w, b.r = {k: tok}, {}
            else:
                b.r[k] = tok
        for b in writes:
            b.w, b.r = {k: tok}, {}
        for b in pwrites:
            b.w[k] = tok

    def op(self, eng, reads, writes, name, pw=(), **kw):
        deps = self._deps(eng, reads, writes, pw)
        idx = self.count[eng]
        self.count[eng] += 1
        self.ops.append(["c", eng, idx, name, kw, deps, None])
        self._mark(("c", eng, idx), reads, writes, pw)

    def pe(self, r, w, name, **kw):
        self.op("pe", r, w, name, **kw)

    def act(self, r, w, name, **kw):
        self.op("act", r, w, name, **kw)

    def dve(self, r, w, name, **kw):
        self.op("dve", r, w, name, **kw)

    def pool(self, r, w, name, **kw):
        self.op("pool", r, w, name, **kw)

    def convert(self, Bsrc, Bdst, dst, src, n):
        a = (n * 9 // 20) // 32 * 32
        b = (n * 18 // 20) // 32 * 32
        self.op("act", [Bsrc], [], "copy", pw=[Bdst], out=dst[:, 0:a], in_=src[:, 0:a])
        self.op("dve", [Bsrc], [], "tensor_copy", pw=[Bdst], out=dst[:, a:b], in_=src[:, a:b])
        if b < n:
            self.op("pool", [Bsrc], [], "tensor_copy", pw=[Bdst], out=dst[:, b:n], in_=src[:, b:n])

    def dma(self, q, reads, writes, **kw):
        dq = self.dq[q]
        k = dq["rr"]
        dq["rr"] = (k + 1) % len(dq["sems"])
        sem = dq["sems"][k]
        deps = self._deps(q, reads, writes)
        if dq["cnt"][k] > dq["floor"][k]:
            key = ("d", sem)
            val = 16 * dq["cnt"][k]
            if self.waited[q].get(key, -1) < val:
                self.waited[q][key] = val
                deps.append(("d", sem, val))
        dq["cnt"][k] += 1
        tok = ("d", sem, 16 * dq["cnt"][k])
        self.ops.append(["d", q, None, "dma_start", kw, deps, sem])
        self._mark(tok, reads, writes)

    def _newsem(self, eng):
        s = self.es.enter_context(self.nc.semaphore(f"c_{eng}{self.nsem}"))
        self.nsem += 1
        self.cur[eng] = [s, 0]

    def flush(self, final=False):
        need = {e: set() for e in self.engs}
        for o in self.ops:
            for t in o[5]:
                if t[0] == "c":
                    need[t[1]].add(t[2])
        last = {}
        for e in self.engs:
            if self.count[e] > self.floor[e]:
                last[e] = self.count[e] - 1
                need[e].add(last[e])
        for o in self.ops:
            kind, eng, idx, name, kw, deps, dsem = o
            E = self.engs[eng]
            for t in deps:
                if t[0] == "c":
                    sem, val = self.semval[(t[1], t[2])]
                else:
                    sem, val = t[1], t[2]
                if self.ewaited[eng].get(sem, -1) < val:
                    E.wait_ge(sem, val)
                    self.ewaited[eng][sem] = val
            ins = getattr(E, name)(**kw)
            if kind == "d":
                ins.then_inc(dsem, 16)
            elif idx in need[eng]:
                if eng not in self.cur or self.cur[eng][1] >= self.EPOCH:
                    self._newsem(eng)
                self.cur[eng][1] += 1
                ins.then_inc(self.cur[eng][0], 1)
                self.semval[(eng, idx)] = (self.cur[eng][0], self.cur[eng][1])
        self.ops = []
        toks = [self.semval[(e, i)] for e, i in last.items()]
        for q, dq in self.dq.items():
            for k, sem in enumerate(dq["sems"]):
                if dq["cnt"][k] > dq["floor"][k]:
                    toks.append((sem, 16 * dq["cnt"][k]))
                    dq["floor"][k] = dq["cnt"][k]
        targets = list(self.engs) if not final else ["sp"]
        for e in targets:
            for sem, val in toks:
                if self.ewaited[e].get(sem, -1) < val:
                    self.engs[e].wait_ge(sem, val)
                    self.ewaited[e][sem] = val
        for e in self.engs:
            self.floor[e] = self.count[e]


def bch(ap2, H, Dd):
    return ap2.unsqueeze(1).to_broadcast([ap2.shape[0], H, Dd])


def bcd(ap2, H, Dd):
    return ap2.unsqueeze(2).to_broadcast([ap2.shape[0], H, Dd])


def v3(ap2, Dd=64):
    return ap2.rearrange("p (h d) -> p h d", d=Dd)


class _Stop(Exception):
    pass


def build_nc(n_blocks=8, do_ffn=True, stage=None):
    nc = bass.Bass("TRN2", target_bir_lowering=False)
    try:
        _build(nc, n_blocks, do_ffn, stage)
    except _Stop:
        pass
    return nc


def _build(nc, n_blocks, do_ffn, stage):
    dr = lambda name, shape, dt=F32: nc.dram_tensor(name, shape, dt, kind="ExternalInput").ap()
    xo_d = dr("xo", [NOWN, D])
    xt_d = dr("xt", [NOWN, D])
    cx_d = dr("cx", [256, D])
    rope_d = dr("rope", [2 * NOWN, 64])
    cfm_d = dr("cfm", [128, 16])
    wmod_d = dr("w_mod", [D, 6 * D])
    bmod_d = dr("bmod", [128, 6 * D])
    gvec_d = dr("gvec", [128, 4 * D])
    gqk_d = dr("gqk", [128, 128])
    dec_d = dr("dec", [128, 16])
    win_d = dr("w_in", [D, 2816])
    wout_d = dr("w_out", [D, D])
    w1_d = dr("w_ffn_in", [D, 2 * HID])
    w2_d = dr("w_ffn_out", [HID, D])
    cst_d = dr("cst", [128, C_END])
    y_d = nc.dram_tensor("y", [NOWN, D], F32, kind="ExternalOutput").ap()
    T_d = nc.dram_tensor("tscr", [NT, 128, 256], BF16, kind="Internal").ap()
    rsc_d = nc.dram_tensor("rscr", [2, 512], F32, kind="Internal").ap()

    with ExitStack() as es:
        E = es.enter_context
        S = Sched(nc, es)
        _uid = [0]

        def sb(name, shape, dt=F32, st=es):
            _uid[0] += 1
            return st.enter_context(nc.sbuf_tensor(f"s{_uid[0]}_{name}", shape, dt))

        def stage_end(name, items):
            if stage != name:
                return
            for nm, ap, bufs in items:
                dt_ = nc.dram_tensor("dbg_" + nm, list(ap.shape), ap.dtype, kind="ExternalOutput").ap()
                S.dma("sp", bufs, [Buf("dbg")], out=dt_, in_=ap)
            S.flush(final=True)
            raise _Stop()

        ps = E(nc.psum_tensor("ps", [128, 8 * 512], F32))
        PB = [Buf(f"psb{i}", excl=True) for i in range(8)]

        def bank(b, n=512, off=0):
            return ps[:, b * 512 + off:b * 512 + off + n]

        def bank16(b):
            return ps[:, b * 512:(b + 1) * 512].bitcast(BF16).rearrange("p (s t) -> p s t", t=128)

        ident = sb("ident", [128, 128], BF16); B_ident = Buf("ident")
        ones_f = sb("ones_f", [128, 64]); B_ones = Buf("ones")
        lg = sb("lg", [128, 16]); B_lg = Buf("lg")
        dtab = sb("dtab", [128, 32]); B_dtab = Buf("dtab")
        cd8 = sb("cd8", [128, 16]); B_cd8 = Buf("cd8")
        cdA = sb("cdA", [128, 4, 64]); cdB = sb("cdB", [128, 4, 64]); B_cd = Buf("cd")
        DsT = sb("DsT", [128, 8, 128]); B_DsT = Buf("DsT")
        gqk = sb("gqk", [128, 128]); B_gqk = Buf("gqk")
        multM = sb("multM", [128, D]); shM = sb("shM", [128, D]); gateM = sb("gateM", [128, D])
        B_multM, B_shM, B_gateM = Buf("multM"), Buf("shM"), Buf("gateM")
        SA = sb("SA", [128, 4, 64]); SBs = sb("SBs", [128, 4, 64]); B_SA, B_SB = Buf("SA"), Buf("SB")
        SAbf = sb("SAbf", [128, 4, 64], BF16); B_SAbf = Buf("SAbf")
        small = sb("small", [128, 64]); B_small = [Buf(f"small{i}") for i in range(8)]
        es0 = ExitStack()
        cst = sb("cst", [128, C_END], F32, es0); B_cst = Buf("cst")

        def sm(i, n=8):
            return small[:, i * 8:i * 8 + n]

        S.dma("sp", [], [B_cst], out=cst[:], in_=cst_d[:, :])
        S.dma("sp", [], [B_lg], out=lg[:], in_=dec_d[:, :])
        S.dma("sp", [], [B_gqk], out=gqk[:], in_=gqk_d[:, :])
        S.dve([B_cst], [B_ident], "tensor_copy", out=ident[:], in_=cst[:, C_ID:C_ID + 128])
        S.dve([], [B_ones], "memset", ap=ones_f[:], constant=1.0)
        S.dve([], [B_SA], "memset", ap=SA[:], constant=0.0)
        S.dve([], [B_SB], "memset", ap=SBs[:], constant=0.0)
        S.act([B_lg], [B_lg], "activation", out=lg[:], in_=lg[:], func=AF.Exp)
        S.dve([B_lg], [B_lg], "tensor_scalar", out=lg[:], in0=lg[:], scalar1=-1.0, scalar2=0.0, op0=ALU.mult, op1=ALU.add)
        for j, (dirn, col, mul) in enumerate(((0, 1, 0.125), (1, 0, 0.125), (0, 2, 1.0), (1, 3, 1.0))):
            S.act([B_lg, B_cst], [B_dtab], "activation", out=dtab[:, j * 8:(j + 1) * 8], in_=lg[:, dirn * 8:(dirn + 1) * 8],
                  func=AF.Exp, scale=cst[:, C_POS + col:C_POS + col + 1])
            if mul != 1.0:
                S.dve([B_dtab], [B_dtab], "tensor_scalar", out=dtab[:, j * 8:(j + 1) * 8], in0=dtab[:, j * 8:(j + 1) * 8],
                      scalar1=mul, scalar2=0.0, op0=ALU.mult, op1=ALU.add)
        S.act([B_lg], [B_cd8], "activation", out=cd8[:], in_=lg[:], func=AF.Exp, scale=128.0)
        for dirn, cdt in ((0, cdA), (1, cdB)):
            c3 = cd8[:, dirn * 8:(dirn + 1) * 8].rearrange("p (hp r) -> p hp r", r=2)
            for r in range(2):
                S.dve([B_cd8], [B_cd], "tensor_copy", out=cdt[64 * r:64 * r + 64, :, :],
                      in_=c3[64 * r:64 * r + 64, :, r].unsqueeze(2).to_broadcast([64, 4, 64]))
        if True:
            ea = sb("ea", [128, 128], F32, es0); eb = sb("eb", [128, 128], F32, es0)
            B_ea, B_eb = Buf("ea"), Buf("eb")
            for h in range(8):
                S.act([B_lg, B_cst], [B_ea], "activation", out=ea[:], in_=cst[:, C_RIJ:C_RIJ + 128], func=AF.Exp, scale=lg[:, h:h + 1])
                S.act([B_lg, B_cst], [B_eb], "activation", out=eb[:], in_=cst[:, C_RJI:C_RJI + 128], func=AF.Exp, scale=lg[:, 8 + h:9 + h])
                S.dve([B_ea, B_cst], [B_ea], "tensor_tensor", out=ea[:], in0=ea[:], in1=cst[:, C_MGE:C_MGE + 128], op=ALU.mult)
                S.dve([B_eb, B_cst], [B_eb], "tensor_tensor", out=eb[:], in0=eb[:], in1=cst[:, C_MLE:C_MLE + 128], op=ALU.mult)
                S.dve([B_ea, B_eb], [B_ea], "tensor_tensor", out=ea[:], in0=ea[:], in1=eb[:], op=ALU.add)
                S.dve([B_ea], [B_DsT], "tensor_scalar", out=DsT[:, (h % 2) * 4 + h // 2, :], in0=ea[:], scalar1=0.125, scalar2=0.0, op0=ALU.mult, op1=ALU.add)
            S.flush()
            es0.close()

        def make_scb(st):
            scb_ = sb("scb", [128, 2, 8, 128], BF16, st); Bs = Buf("scb")
            cf = sb("cf", [128, 16], F32, st); B_cf = Buf("cf")
            S.dma("sp", [], [B_cf], out=cf[:], in_=cfm_d[:, :])
            S.act([B_cf], [B_cf], "activation", out=cf[:], in_=cf[:], func=AF.Silu)
            for v in range(2):
                S.dve([B_cf], [Bs], "tensor_copy", out=scb_[:, v, :, :], in_=cf[:, v * 8:(v + 1) * 8].unsqueeze(2).to_broadcast([128, 8, 128]))
            return scb_, Bs

        def mod_tables(st, jobs):
            scb, B_scb = make_scb(st)
            wst = [sb(f"wst{i}", [128, 8, 512], F32, st) for i in range(2)]; B_wst = [Buf("wst0"), Buf("wst1")]
            wbf = [sb(f"wbf{i}", [128, 8, 512], BF16, st) for i in range(2)]; B_wbf = [Buf("wbf0"), Buf("wbf1")]
            bst = [sb(f"bst{i}", [128, 512], F32, st) for i in range(2)]; B_bst = [Buf("bst0"), Buf("bst1")]
            gst = [sb(f"gst{i}", [128, 512], F32, st) for i in range(2)]; B_gst = [Buf("gst0"), Buf("gst1")]
            tmp = [sb(f"mtmp{i}", [128, 512], F32, st) for i in range(2)]; B_tmp = [Buf("mtmp0"), Buf("mtmp1")]
            slices = sorted(set(j[0] for j in jobs))
            for i, s in enumerate(slices):
                p = i % 2
                S.dma("sp", [], [B_wst[p]], out=wst[p][:], in_=wmod_d[:, s * 512:(s + 1) * 512].rearrange("(kc p) n -> p kc n", p=128))
                S.dma("sp", [], [B_bst[p]], out=bst[p][:], in_=bmod_d[:, s * 512:(s + 1) * 512])
                S.convert(B_wst[p], B_wbf[p], wbf[p][:].rearrange("p a n -> p (a n)"), wst[p][:].rearrange("p a n -> p (a n)"), 4096)
                for (s2, which, kind, gidx, dst, B_dst, c0) in jobs:
                    if s2 != s:
                        continue
                    bk = 2 * p + which
                    for kc in range(8):
                        S.pe([B_scb, B_wbf[p]], [PB[bk]], "matmul", out=bank(bk), lhsT=scb[:, which, kc, :], rhs=wbf[p][:, kc, :],
                             start=(kc == 0), stop=(kc == 7))
                    if kind == "sh":
                        S.dve([PB[bk], B_bst[p]], [B_dst], "tensor_tensor", out=dst[:, c0:c0 + 512], in0=bank(bk), in1=bst[p][:], op=ALU.add)
                    else:
                        gcol = gidx * D + (s % 2) * 512
                        S.dma("sp", [], [B_gst[which]], out=gst[which][:], in_=gvec_d[:, gcol:gcol + 512])
                        S.dve([PB[bk], B_bst[p]], [B_tmp[which]], "tensor_tensor", out=tmp[which][:], in0=bank(bk), in1=bst[p][:], op=ALU.add)
                        if kind == "sc":
                            S.dve([B_tmp[which], B_gst[which]], [B_dst], "scalar_tensor_tensor", out=dst[:, c0:c0 + 512], in0=tmp[which][:],
                                  scalar=1.0, in1=gst[which][:], op0=ALU.add, op1=ALU.mult)
                        else:
                            S.dve([B_tmp[which], B_gst[which]], [B_dst], "tensor_tensor", out=dst[:, c0:c0 + 512], in0=tmp[which][:],
                                  in1=gst[which][:], op=ALU.mult)

        def rstd_inplace(ap, B, n):
            S.act([B], [B], "activation", out=ap, in_=ap, func=AF.Ln, scale=1.0 / n, bias=EPS)
            S.act([B], [B], "activation", out=ap, in_=ap, func=AF.Exp, scale=-0.5)

        def make_front(st, nbuf=2):
            fxs = [sb("fx", [128, D], F32, st) for _ in range(nbuf)]; fBx = [Buf("fx") for _ in range(nbuf)]
            fts = [sb("ft", [128, D], F32, st) for _ in range(nbuf)]; fBt = [Buf("ft") for _ in range(nbuf)]
            fys = [sb("fy", [128, D], BF16, st) for _ in range(nbuf)]; fBy = [Buf("fy") for _ in range(nbuf)]
            fss = sb("fss", [128, 8], F32, st); fBs = [Buf("fss") for _ in range(nbuf)]
            cnt = [0]

            def front(src_ap, src_bufs, mult, B_mult, sh, B_sh, dst_ap, B_dst, trb, keep=None):
                i = cnt[0] % nbuf
                cnt[0] += 1
                fx, fB_x, ft, fB_t, fy, fB_y = fxs[i], fBx[i], fts[i], fBt[i], fys[i], fBy[i]
                ss, Bss = fss[:, i:i + 1], fBs[i]
                xt_, Bx = (fx, fB_x) if keep is None else keep
                S.dma("sp", src_bufs, [Bx], out=xt_[:] if keep is None else xt_, in_=src_ap)
                xin = xt_[:] if keep is None else xt_
                S.act([Bx], [fB_t, Bss], "activation", out=ft[:], in_=xin, func=AF.Square, accum_out=ss)
                rstd_inplace(ss, Bss, D)
                S.dve([Bx, Bss, B_mult], [fB_t], "scalar_tensor_tensor", out=ft[:], in0=xin, scalar=ss, in1=mult[:],
                      op0=ALU.mult, op1=ALU.mult)
                S.pool([fB_t, B_sh], [fB_y], "tensor_tensor", out=fy[:], in0=ft[:], in1=sh[:], op=ALU.add)
                tv = bank16(trb)
                for kc in range(8):
                    S.pe([fB_y, B_ident], [PB[trb]], "transpose", out=tv[:, kc, :], in_=fy[:, kc * 128:(kc + 1) * 128], identity=ident[:])
                S.dve([PB[trb]], [B_dst], "tensor_copy", out=dst_ap, in_=tv)
            front.fx, front.ft, front.B_x, front.B_t = fxs[0], fts[0], fBx[0], fBt[0]
            return front

        def rope(src, B_src, dst, B_lo, B_hi, H, rt, B_rt, tmp, B_tmp):
            s3, d3 = v3(src), v3(dst)
            t3 = v3(tmp, 32)
            cosb, sinb = bch(rt[:, 0:32], H, 32), bch(rt[:, 32:64], H, 32)
            S.dve([B_src, B_rt], [B_lo], "tensor_tensor", out=d3[:, :, 0:32], in0=s3[:, :, 0:32], in1=cosb, op=ALU.mult)
            S.pool([B_src, B_rt], [B_tmp], "tensor_tensor", out=t3, in0=s3[:, :, 32:64], in1=sinb, op=ALU.mult)
            S.dve([B_tmp], [B_lo], "tensor_tensor", out=d3[:, :, 0:32], in0=d3[:, :, 0:32], in1=t3, op=ALU.subtract)
            S.pool([B_src, B_rt], [B_hi], "tensor_tensor", out=d3[:, :, 32:64], in0=s3[:, :, 0:32], in1=sinb, op=ALU.mult)
            S.dve([B_src, B_rt, B_lo], [B_tmp], "tensor_tensor", out=t3, in0=s3[:, :, 32:64], in1=cosb, op=ALU.mult)
            S.pool([B_tmp], [B_hi], "tensor_tensor", out=d3[:, :, 32:64], in0=d3[:, :, 32:64], in1=t3, op=ALU.add)

        def head_rstd(src_ap, src_bufs, H, sq, B_sq, slot):
            S.dve(src_bufs, [B_sq], "tensor_tensor", out=sq[:, 0:H * 64], in0=src_ap, in1=src_ap, op=ALU.mult)
            S.dve([B_sq], [B_small[slot]], "tensor_reduce", out=sm(slot, H), in_=v3(sq[:, 0:H * 64]), axis=AX.X, op=ALU.add)
            rstd_inplace(sm(slot, H), B_small[slot], 64)

        def state_update(St, B_St, cdt, psb, KdT, B_Kd, Vr_, B_Vr):
            for hp in range(4):
                S.pe([B_Kd, B_Vr], [PB[psb]], "matmul", out=bank(psb, 128, hp * 128), lhsT=KdT[:, hp * 128:(hp + 1) * 128],
                     rhs=Vr_[:, hp * 128:(hp + 1) * 128], start=True, stop=True)
            u3 = bank(psb).rearrange("p (hp c) -> p hp c", c=128)
            for r in range(2):
                sl = slice(64 * r, 64 * r + 64)
                S.dve([B_St, B_cd], [B_St], "tensor_tensor", out=St[sl], in0=St[sl], in1=cdt[sl], op=ALU.mult)
                S.dve([B_St, PB[psb]], [B_St], "tensor_tensor", out=St[sl], in0=St[sl], in1=u3[sl, :, 64 * r:64 * r + 64], op=ALU.add)

        with ExitStack() as esBC:
            win = sb("win", [128, 8, 2816], BF16, esBC); B_win = Buf("win")
            KT = sb("KT", [128, NKT * 128], BF16, esBC); B_KT = Buf("KT")
            Vaug = sb("Vaug", [128, NKT, 2, 65], BF16, esBC); B_V = Buf("Vaug")
            S.pool([], [B_V], "memset", ap=Vaug[:, :, :, 64:65], constant=1.0)

            esAB = ExitStack()
            multC = sb("multC", [128, D], F32, esAB); shC = sb("shC", [128, D], F32, esAB)
            B_multC, B_shC = Buf("multC"), Buf("shC")
            with ExitStack() as esA:
                jobs = []
                for s in range(2):
                    jobs.append((s, 0, "sh", 0, shM, B_shM, s * 512))
                    jobs.append((s, 1, "sh", 0, shC, B_shC, s * 512))
                    jobs.append((2 + s, 0, "sc", 0, multM, B_multM, s * 512))
                    jobs.append((2 + s, 1, "sc", 0, multC, B_multC, s * 512))
                    jobs.append((4 + s, 0, "gt", 1, gateM, B_gateM, s * 512))
                mod_tables(esA, jobs)
                wstg = [sb(f"wstg{i}", [128, 2816], F32, esA) for i in range(3)]; B_wstg = [Buf("wstg0"), Buf("wstg1"), Buf("wstg2")]
                for kc in range(8):
                    p = kc % 3
                    S.dma("sp", [], [B_wstg[p]], out=wstg[p][:], in_=win_d[kc * 128:(kc + 1) * 128, :])
                    S.op("dve", [B_wstg[p]], [], "tensor_copy", pw=[B_win], out=win[:, kc, 0:512].rearrange("p (g kv d) -> p g kv d", g=4, kv=2),
                         in_=wstg[p][:, 0:512].rearrange("p (kv g d) -> p g kv d", g=4, kv=2))
                    S.convert(B_wstg[p], B_win, win[:, kc, 512:2816], wstg[p][:, 512:2816], 2304)
                S.flush()
                stage_end("A", [("multM", multM[:], [B_multM]), ("shM", shM[:], [B_shM]), ("gateM", gateM[:], [B_gateM]),
                                ("multC", multC[:], [B_multC]), ("shC", shC[:], [B_shC]), ("DsT", DsT[:], [B_DsT]),
                                ("dtab", dtab[:], [B_dtab]), ("cdA", cdA[:], [B_cd]), ("cdB", cdB[:], [B_cd]),
                                ("win", win[:, 3, :], [B_win]), ("lg", lg[:], [B_lg])])

            with ExitStack() as esB:
                front = make_front(esB)
                hT = sb("hT", [128, 8, 128], BF16, esB); B_hT = Buf("hT")
                rt = sb("rt", [128, 64], F32, esB); B_rt = Buf("rt")
                kk = sb("kk", [128, 640], F32, esB); B_kk = Buf("kk")
                rr = sb("rr", [128, 640], F32, esB); B_rlo, B_rhi = Buf("rlo"), Buf("rhi")
                rtmp = sb("rtmp", [128, 320], F32, esB); B_rtmp = Buf("rtmp")
                sq = sb("sq", [128, 128], F32, esB); B_sq = Buf("sq")
                katok = sb("katok", [128, 128], BF16, esB); B_katok = Buf("katok")
                KdB = sb("KdB", [128, 512], BF16, esB); B_KdB = Buf("KdB")
                KdAc = sb("KdAc", [128, 512], BF16, esB); B_KdAc = Buf("KdAc")
                Vr = sb("Vr", [128, 512], BF16, esB); B_Vr = Buf("Vr")
                Tbf = sb("Tbf", [128, 256], BF16, esB); B_Tbf = Buf("Tbf")
                utmp = sb("utmp", [128, 4, 64], F32, esB); B_utmp = Buf("utmp")
                B_T = [Buf(f"T{c}") for c in range(NT)]

                tiles = [("ctx", 0), ("ctx", 1)] + [("oth", t) for t in range(NT - 1, -1, -1)] + [("own", t) for t in range(NT - 1, -1, -1)]
                hTs = [hT, sb("hTb", [128, 8, 128], BF16, esB)]; B_hTs = [B_hT, Buf("hTb")]
                UB = 7

                def b_info(ti):
                    kind, t = tiles[ti]
                    if kind == "ctx":
                        return cx_d[t * 128:(t + 1) * 128, :], t, multC, B_multC, shC, B_shC
                    if kind == "own":
                        return xo_d[t * 128:(t + 1) * 128, :], 2 + t, multM, B_multM, shM, B_shM
                    return xt_d[t * 128:(t + 1) * 128, :], 2 + NT + t, multM, B_multM, shM, B_shM

                def b_front(ti):
                    src, kidx, mu, Bmu, shh, Bsh = b_info(ti)
                    front(src, [], mu, Bmu, shh, Bsh, hTs[ti % 2][:], B_hTs[ti % 2], 0)

                def b_proj(ti):
                    p = ti % 2
                    for (c0, c1, bk) in ((512, 768, 1 + 3 * p), (1280, 1792, 2 + 3 * p), (1792, 2304, 3 + 3 * p)):
                        for kc in range(8):
                            S.pe([B_hTs[p], B_win], [PB[bk]], "matmul", out=bank(bk, c1 - c0), lhsT=hTs[p][:, kc, :], rhs=win[:, kc, c0:c1],
                                 start=(kc == 0), stop=(kc == 7))

                kk2 = [kk, sb("kk", [128, 640], F32, esB)]; B_kk2 = [B_kk, Buf("kk1")]
                rr2 = [rr, sb("rr", [128, 640], F32, esB)]; B_rlo2, B_rhi2 = [B_rlo, Buf("rlo1")], [B_rhi, Buf("rhi1")]
                rtmp2 = [rtmp, sb("rtmp", [128, 320], F32, esB)]; B_rtmp2 = [B_rtmp, Buf("rtmp1")]
                sq2 = [sq, sb("sq", [128, 128], F32, esB)]; B_sq2 = [B_sq, Buf("sq1")]
                katok2 = [katok, sb("katok", [128, 128], BF16, esB)]; B_katok2 = [B_katok, Buf("katok1")]
                KdB2 = [KdB, sb("KdB", [128, 512], BF16, esB)]; B_KdB2 = [B_KdB, Buf("KdB1")]
                Vr2 = [Vr, sb("Vr", [128, 512], BF16, esB)]; B_Vr2 = [B_Vr, Buf("Vr1")]
                Tbf2 = [Tbf, sb("Tbf", [128, 256], BF16, esB)]; B_Tbf2 = [B_Tbf, Buf("Tbf1")]
                rt2 = [rt, sb("rt", [128, 64], F32, esB)]; B_rt2 = [B_rt, Buf("rt1")]

                def b_post(ti):
                    kind, t = tiles[ti]
                    q = ti % 2
                    kk_, Bkk, rr_, Brlo, Brhi, rtmp_, Brtmp = kk2[q], B_kk2[q], rr2[q], B_rlo2[q], B_rhi2[q], rtmp2[q], B_rtmp2[q]
                    sq_, Bsq, katok_, Bkatok, KdB_, BKdB, Vr_, BVr, Tbf_, BTbf, rt_, Brt = (sq2[q], B_sq2[q], katok2[q], B_katok2[q], KdB2[q], B_KdB2[q],
                                                                                  Vr2[q], B_Vr2[q], Tbf2[q], B_Tbf2[q], rt2[q], B_rt2[q])
                    slot = 1 + 4 * q
                    Bsl = B_small[slot]
                    kidx = b_info(ti)[1]
                    b1, b2, b3 = 1 + 3 * q, 2 + 3 * q, 3 + 3 * q
                    if kind != "ctx":
                        pos0 = t * 128 if kind == "own" else NOWN + t * 128
                        S.dma("sp", [], [Brt], out=rt_[:], in_=rope_d[pos0:pos0 + 128, :])
                    S.act([PB[b1]], [Bkk], "copy", out=kk_[:, 0:128], in_=bank(b1, 128))
                    S.act([PB[b2]], [Bkk], "copy", out=kk_[:, 128:640], in_=bank(b2))
                    yield
                    S.act([PB[b1]], [B_V], "copy", out=Vaug[:, kidx, :, 0:64], in_=v3(bank(b1, 128, 128)))
                    S.act([PB[b3]], [BVr], "copy", out=Vr_[:], in_=bank(b3))
                    S.dve([Bkk], [Bsq], "tensor_tensor", out=sq_[:], in0=kk_[:, 0:128], in1=kk_[:, 0:128], op=ALU.mult)
                    yield
                    S.dve([Bsq], [Bsl], "tensor_reduce", out=sm(slot, 2), in_=v3(sq_[:]), axis=AX.X, op=ALU.add)
                    yield
                    yield
                    S.act([Bsl], [Bsl], "activation", out=sm(slot, 2), in_=sm(slot, 2), func=AF.Ln, scale=1.0 / 64, bias=EPS)
                    yield
                    S.act([Bsl], [Bsl], "activation", out=sm(slot, 2), in_=sm(slot, 2), func=AF.Exp, scale=-0.5)
                    yield
                    S.dve([Bkk, Bsl], [Bkk], "tensor_tensor", out=v3(kk_[:, 0:128]), in0=v3(kk_[:, 0:128]), in1=bcd(sm(slot, 2), 2, 64), op=ALU.mult)
                    yield
                    S.pool([Bkk, B_gqk], [Bkk], "tensor_tensor", out=v3(kk_[:, 0:128]), in0=v3(kk_[:, 0:128]), in1=bch(gqk[:, 64:128], 2, 64), op=ALU.mult)
                    yield
                    if kind != "ctx":
                        s3, d3, t3 = v3(kk_[:]), v3(rr_[:]), v3(rtmp_[:], 32)
                        cosb, sinb = bch(rt_[:, 0:32], 10, 32), bch(rt_[:, 32:64], 10, 32)
                        S.dve([Bkk, Brt], [Brlo], "tensor_tensor", out=d3[:, :, 0:32], in0=s3[:, :, 0:32], in1=cosb, op=ALU.mult)
                        S.pool([Bkk, Brt], [Brtmp], "tensor_tensor", out=t3, in0=s3[:, :, 32:64], in1=sinb, op=ALU.mult)
                        yield
                        S.dve([Brtmp], [Brlo], "tensor_tensor", out=d3[:, :, 0:32], in0=d3[:, :, 0:32], in1=t3, op=ALU.subtract)
                        S.pool([Bkk, Brt], [Brhi], "tensor_tensor", out=d3[:, :, 32:64], in0=s3[:, :, 0:32], in1=sinb, op=ALU.mult)
                        yield
                        S.dve([Bkk, Brt, Brlo], [Brtmp], "tensor_tensor", out=t3, in0=s3[:, :, 32:64], in1=cosb, op=ALU.mult)
                        yield
                        S.pool([Brtmp], [Brhi], "tensor_tensor", out=d3[:, :, 32:64], in0=d3[:, :, 32:64], in1=t3, op=ALU.add)
                        yield
                        ksrc, Bks = rr_, [Brlo, Brhi]
                    else:
                        ksrc, Bks = kk_, [Bkk]
                    S.act(Bks, [Bkatok], "copy", out=katok_[:], in_=ksrc[:, 0:128])
                    S.pool(Bks + [B_dtab], [BKdB], "tensor_tensor", out=v3(KdB_[:]), in0=v3(ksrc[:, 128:640]), in1=bcd(dtab[:, 8:16], 8, 64), op=ALU.mult)
                    yield
                    S.pe([Bkatok, B_ident], [PB[UB]], "transpose", out=bank16(UB)[:, q, :], in_=katok_[:], identity=ident[:])
                    yield
                    S.act([PB[UB]], [B_KT], "copy", out=KT[:, kidx * 128:(kidx + 1) * 128], in_=bank16(UB)[:, q, :])
                    yield
                    if kind == "ctx":
                        S.pool(Bks + [B_dtab], [B_KdAc], "tensor_tensor", out=v3(KdAc[:]), in0=v3(ksrc[:, 128:640]), in1=bcd(dtab[:, 0:8], 8, 64), op=ALU.mult)
                        state_update(SA, B_SA, cdA, UB, KdAc, B_KdAc, Vr_, BVr)
                        if t == 0:
                            state_update(SBs, B_SB, cdB, UB, KdB_, BKdB, Vr_, BVr)
                        else:
                            for hp in range(4):
                                S.pe([BKdB, BVr], [PB[UB]], "matmul", out=bank(UB, 128, hp * 128), lhsT=KdB_[:, hp * 128:(hp + 1) * 128],
                                     rhs=Vr_[:, hp * 128:(hp + 1) * 128], start=True, stop=True)
                            u3 = bank(UB).rearrange("p (hp c) -> p hp c", c=128)
                            for r in range(2):
                                sl = slice(64 * r, 64 * r + 64)
                                S.dve([PB[UB], B_cd], [B_utmp], "tensor_tensor", out=utmp[sl], in0=u3[sl, :, 64 * r:64 * r + 64], in1=cdB[sl], op=ALU.mult)
                                S.dve([B_SB, B_utmp], [B_SB], "tensor_tensor", out=SBs[sl], in0=SBs[sl], in1=utmp[sl], op=ALU.add)
                    else:
                        if kind == "own":
                            S.dve([B_SB], [BTbf], "tensor_copy", out=Tbf_[:], in_=SBs[:].rearrange("p a b -> p (a b)"))
                            S.dma("pool", [BTbf], [B_T[t]], out=T_d[t], in_=Tbf_[:])
                        for hp in range(4):
                            S.pe([BKdB, BVr], [PB[UB]], "matmul", out=bank(UB, 128, hp * 128), lhsT=KdB_[:, hp * 128:(hp + 1) * 128],
                                 rhs=Vr_[:, hp * 128:(hp + 1) * 128], start=True, stop=True)
                        u3 = bank(UB).rearrange("p (hp c) -> p hp c", c=128)
                        for r in range(2):
                            sl = slice(64 * r, 64 * r + 64)
                            S.dve([B_SB, B_cd], [B_SB], "tensor_tensor", out=SBs[sl], in0=SBs[sl], in1=cdB[sl], op=ALU.mult)
                        for r in range(2):
                            sl = slice(64 * r, 64 * r + 64)
                            S.dve([B_SB, PB[UB]], [B_SB], "tensor_tensor", out=SBs[sl], in0=SBs[sl], in1=u3[sl, :, 64 * r:64 * r + 64], op=ALU.add)
                    yield

                active = []

                def pump(until_len):
                    while len(active) > until_len:
                        for gen_ in list(active):
                            if next(gen_, "done") == "done":
                                active.remove(gen_)

                b_front(0)
                for ti in range(len(tiles)):
                    b_proj(ti)
                    if ti + 1 < len(tiles):
                        b_front(ti + 1)
                    active.append(b_post(ti))
                    pump(1)
                pump(0)
                S.dve([B_SA], [B_SAbf], "tensor_copy", out=SAbf[:], in_=SA[:])
                S.flush()
                stage_end("B", [("KT", KT[:], [B_KT]), ("Vaug", Vaug[:], [B_V]), ("SA", SA[:], [B_SA]), ("SB", SBs[:], [B_SB])])
            esAB.close()

            with ExitStack() as esC:
                woutA = sb("woutA", [128, 4, D], BF16, esC); woutR = sb("woutR", [128, 4, D], BF16, esC); B_wout = Buf("wout")
                fx = sb("fx", [128, D], F32, esC); B_fx = Buf("fx")
                ft = sb("ft", [128, D], F32, esC); B_ft = Buf("ft")
                fy = sb("fy", [128, D], BF16, esC); B_fy = Buf("fy")
                for g in range(4):
                    for kv in range(2):
                        r0 = (kv * 4 + g) * 64
                        S.dma("sp", [], [B_fx], out=fx[64 * kv:64 * kv + 64, :], in_=wout_d[r0:r0 + 64, :])
                    S.dve([B_fx], [B_wout], "tensor_copy", out=woutA[:, g, :], in_=fx[:])
                for hp in range(4):
                    S.dma("sp", [], [B_fx], out=fx[:], in_=wout_d[512 + hp * 128:512 + (hp + 1) * 128, :])
                    S.dve([B_fx], [B_wout], "tensor_copy", out=woutR[:, hp, :], in_=fx[:])

                hT = sb("hT", [128, 8, 128], BF16, esC); B_hT = Buf("hT")
                QaT = [sb(f"QaT{i}", [128, 4, 512], BF16, esC) for i in range(2)]; B_QaT = [Buf("QaT0"), Buf("QaT1")]
                retT = [sb(f"retT{i}", [128, 4, 512], BF16, esC) for i in range(2)]; B_retT = [Buf("retT0"), Buf("retT1")]
                attP = [sb(f"attP{i}", [128, 4, 512], BF16, esC) for i in range(2)]
                B_attP = [[Buf(f"attP{i}_{g}") for g in range(4)] for i in range(2)]
                atmp = sb("atmp", [128, 512], BF16, esC); B_atmp = Buf("atmp")
                pT = [sb(f"pT{i}", [128, 1024], BF16, esC) for i in range(3)]; B_pT = [Buf("pT0"), Buf("pT1"), Buf("pT2")]
                rt = sb("rt", [128, 64], F32, esC); B_rt = Buf("rt")
                qq = sb("qq", [128, 1536], F32, esC); B_qq = Buf("qq")
                rr = sb("rr", [128, 1536], F32, esC); B_rlo, B_rhi = Buf("rlo"), Buf("rhi")
                rtmp = sb("rtmp", [128, 768], F32, esC); B_rtmp = Buf("rtmp")
                tok = sb("tok", [128, 5, 512], BF16, esC); B_tok = [Buf(f"tok{i}") for i in range(5)]
                RT = sb("RT", [128, 4, 4, 128], BF16, esC); B_RT = Buf("RT")
                KdA = sb("KdA", [128, 512], BF16, esC); B_KdA = Buf("KdA")
                Vr = sb("Vr", [128, 512], BF16, esC); B_Vr = Buf("Vr")
                graw = sb("graw", [128, 512], F32, esC); B_graw = Buf("graw")
                gexp = sb("gexp", [128, 512], F32, esC); B_gexp = Buf("gexp")
                innD = sb("innD", [128, 8, 128], BF16, esC); B_innD = Buf("innD")
                o32 = sb("o32", [128, 512], F32, esC); B_o32 = Buf("o32")
                sq, B_sq = rtmp[:, 0:512], B_rtmp
                rettok = sb("rettok", [128, 512], BF16, esC); B_rettok = Buf("rettok")
                Tbf = sb("Tbf", [128, 256], BF16, esC); B_Tbf = Buf("Tbf")
                oT = [sb(f"oT{i}", [128, 512], F32, esC) for i in range(2)]; B_oT = [Buf("oT0"), Buf("oT1")]
                xr = sb("xr", [128, D], F32, esC); B_xr = Buf("xr")
                mt, B_mt = ft, B_ft
                rcb = [sb(f"rcb{i}", [128, 512], F32, esC) for i in range(2)]; B_rcb = [Buf("rcb0"), Buf("rcb1")]
                B_rsc = [Buf("rsc0"), Buf("rsc1")]
                B_y = [Buf(f"y{t}") for t in range(NT)]
                BG0, BG1 = 6, 7

                def act_rstd(ap, B, n):
                    S.act([B], [B], "activation", out=ap, in_=ap, func=AF.Ln, scale=1.0 / n, bias=EPS)
                    S.act([B], [B], "activation", out=ap, in_=ap, func=AF.Exp, scale=-0.5)

                def front_part(t):
                    S.dma("sp", [], [B_fx], out=fx[:], in_=xo_d[t * 128:(t + 1) * 128, :])
                    yield 3
                    S.act([B_fx], [B_ft, B_small[0]], "activation", out=ft[:], in_=fx[:], func=AF.Square, accum_out=sm(0, 1))
                    yield 2
                    act_rstd(sm(0, 1), B_small[0], D)
                    S.dve([B_fx, B_small[0], B_multM], [B_ft], "scalar_tensor_tensor", out=ft[:], in0=fx[:], scalar=sm(0, 1), in1=multM[:],
                          op0=ALU.mult, op1=ALU.mult)
                    S.dve([B_ft, B_shM], [B_fy], "tensor_tensor", out=fy[:], in0=ft[:], in1=shM[:], op=ALU.add)

                def tile_work(t, par):
                    tl = t % 4
                    tc = slice(tl * 128, (tl + 1) * 128)
                    S.dma("sp", [], [B_rt], out=rt[:], in_=rope_d[t * 128:(t + 1) * 128, :])
                    S.dma("sp", [B_T[t]], [B_Tbf], out=Tbf[:], in_=T_d[t])
                    yield 1
                    for kc in range(8):
                        S.pe([B_fy, B_ident], [PB[BG0]], "transpose", out=bank16(BG0)[:, kc, :], in_=fy[:, kc * 128:(kc + 1) * 128], identity=ident[:])
                    S.dve([PB[BG0]], [B_hT], "tensor_copy", out=hT[:], in_=bank16(BG0))
                    yield 2
                    for i, c0 in enumerate((0, 768, 1280, 1792, 2304)):
                        bk = BG1 if i % 2 == 0 else BG0
                        for kc in range(8):
                            S.pe([B_hT, B_win], [PB[bk]], "matmul", out=bank(bk), lhsT=hT[:, kc, :], rhs=win[:, kc, c0:c0 + 512],
                                 start=(kc == 0), stop=(kc == 7))
                            if kc == 3:
                                yield 1
                        yield 2
                        if i < 3:
                            S.dve([PB[bk]], [B_qq], "tensor_copy", out=qq[:, i * 512:(i + 1) * 512], in_=bank(bk))
                        elif i == 3:
                            S.dve([PB[bk]], [B_Vr], "tensor_copy", out=Vr[:], in_=bank(bk))
                        else:
                            S.act([PB[bk]], [B_gexp], "activation", out=gexp[:], in_=bank(bk), func=AF.Exp, scale=-1.0)
                            S.dve([PB[bk]], [B_graw], "tensor_copy", out=graw[:], in_=bank(bk))
                    S.dve([B_qq], [B_sq], "tensor_tensor", out=sq[:], in0=qq[:, 0:512], in1=qq[:, 0:512], op=ALU.mult)
                    S.dve([B_sq], [B_small[1]], "tensor_reduce", out=sm(1), in_=v3(sq[:]), axis=AX.X, op=ALU.add)
                    yield 3
                    act_rstd(sm(1), B_small[1], 64)
                    S.dve([B_gexp], [B_gexp], "tensor_scalar", out=gexp[:], in0=gexp[:], scalar1=1.0, scalar2=0.0, op0=ALU.add, op1=ALU.add)
                    S.dve([B_gexp], [B_gexp], "reciprocal", out=gexp[:], in_=gexp[:])
                    S.pool([B_gexp, B_graw], [B_gexp], "tensor_tensor", out=gexp[:], in0=gexp[:], in1=graw[:], op=ALU.mult)
                    S.dve([B_qq, B_small[1]], [B_qq], "tensor_tensor", out=v3(qq[:, 0:512]), in0=v3(qq[:, 0:512]), in1=bcd(sm(1), 8, 64), op=ALU.mult)
                    S.dve([B_qq, B_gqk], [B_qq], "tensor_tensor", out=v3(qq[:, 0:512]), in0=v3(qq[:, 0:512]), in1=bch(gqk[:, 0:64], 8, 64), op=ALU.mult)
                    s3, d3, t3 = v3(qq[:]), v3(rr[:]), v3(rtmp[:], 32)
                    cosb, sinb = bch(rt[:, 0:32], 24, 32), bch(rt[:, 32:64], 24, 32)
                    S.dve([B_qq, B_rt], [B_rlo], "tensor_tensor", out=d3[:, :, 0:32], in0=s3[:, :, 0:32], in1=cosb, op=ALU.mult)
                    S.pool([B_qq, B_rt], [B_rtmp], "tensor_tensor", out=t3, in0=s3[:, :, 32:64], in1=sinb, op=ALU.mult)
                    S.dve([B_rtmp], [B_rlo], "tensor_tensor", out=d3[:, :, 0:32], in0=d3[:, :, 0:32], in1=t3, op=ALU.subtract)
                    S.pool([B_qq, B_rt], [B_rhi], "tensor_tensor", out=d3[:, :, 32:64], in0=s3[:, :, 0:32], in1=sinb, op=ALU.mult)
                    S.dve([B_qq, B_rt, B_rlo], [B_rtmp], "tensor_tensor", out=t3, in0=s3[:, :, 32:64], in1=cosb, op=ALU.mult)
                    S.pool([B_rtmp], [B_rhi], "tensor_tensor", out=d3[:, :, 32:64], in0=d3[:, :, 32:64], in1=t3, op=ALU.add)
                    Brr = [B_rlo, B_rhi]
                    S.pool(Brr, [B_tok[0]], "tensor_copy", out=tok[:, 0, :], in_=rr[:, 0:512])
                    S.dve(Brr, [B_tok[1]], "tensor_copy", out=tok[:, 1, :], in_=rr[:, 512:1024])
                    S.dve(Brr + [B_dtab], [B_tok[2]], "tensor_tensor", out=v3(tok[:, 2, :]), in0=v3(rr[:, 512:1024]), in1=bcd(dtab[:, 16:24], 8, 64), op=ALU.mult)
                    S.dve(Brr + [B_dtab], [B_tok[3]], "tensor_tensor", out=v3(tok[:, 3, :]), in0=v3(rr[:, 512:1024]), in1=bcd(dtab[:, 24:32], 8, 64), op=ALU.mult)
                    S.dve(Brr, [B_tok[4]], "tensor_copy", out=tok[:, 4, :], in_=rr[:, 1024:1536])
                    S.dve(Brr + [B_dtab], [B_KdA], "tensor_tensor", out=v3(KdA[:]), in0=v3(rr[:, 1024:1536]), in1=bcd(dtab[:, 0:8], 8, 64), op=ALU.mult)
                    yield 1
                    if t + 1 < n_blocks * 4:
                        yield from front_part(t + 1)
                    yield 4
                    for (k, bk, s0) in ((0, BG0, 0), (1, BG0, 4), (2, BG1, 0), (3, BG1, 4)):
                        for j in range(4):
                            S.pe([B_tok[k], B_ident], [PB[bk]], "transpose", out=bank16(bk)[:, s0 + j, :], in_=tok[:, k, j * 128:(j + 1) * 128], identity=ident[:])
                    S.dve([PB[BG0]], [B_QaT[par]], "tensor_copy", out=QaT[par][:, :, tc], in_=bank16(BG0)[:, 0:4, :])
                    S.dve([PB[BG0]], [B_RT], "tensor_copy", out=RT[:, 0, :, :], in_=bank16(BG0)[:, 4:8, :])
                    S.dve([PB[BG1]], [B_RT], "tensor_copy", out=RT[:, 1:3, :, :], in_=bank16(BG1).rearrange("p (a b) t -> p a b t", a=2))
                    yield 2
                    for j in range(4):
                        S.pe([B_tok[4], B_ident], [PB[BG0]], "transpose", out=bank16(BG0)[:, j, :], in_=tok[:, 4, j * 128:(j + 1) * 128], identity=ident[:])
                    S.dve([PB[BG0]], [B_RT], "tensor_copy", out=RT[:, 3, :, :], in_=bank16(BG0)[:, 0:4, :])
                    yield 3
                    for h in range(8):
                        hp, r = h // 2, h % 2
                        sl = slice(64 * r, 64 * r + 64)
                        S.pe([B_RT], [PB[BG0 + r]], "matmul", out=bank(BG0 + r, 128, hp * 128), lhsT=RT[sl, 3, hp, :], rhs=RT[sl, 0, hp, :],
                             start=True, stop=True)
                    S.dve([PB[BG0], PB[BG1], B_DsT], [B_innD], "tensor_tensor", out=innD[:].rearrange("p h t -> p (h t)"), in0=ps[:, BG0 * 512:(BG0 + 2) * 512],
                          in1=DsT[:].rearrange("p h t -> p (h t)"), op=ALU.mult)
                    yield 3
                    for h in range(8):
                        hp, r = h // 2, h % 2
                        sl = slice(64 * r, 64 * r + 64)
                        oc = bank(BG0, 64, h * 64)
                        S.pe([B_innD, B_Vr], [PB[BG0]], "matmul", out=oc, lhsT=innD[:, r * 4 + hp, :], rhs=Vr[:, h * 64:(h + 1) * 64], start=True, stop=False)
                        S.pe([B_RT, B_SAbf], [PB[BG0]], "matmul", out=oc, lhsT=RT[sl, 1, hp, :], rhs=SAbf[sl, hp, :], start=False, stop=False)
                        S.pe([B_RT, B_Tbf], [PB[BG0]], "matmul", out=oc, lhsT=RT[sl, 2, hp, :], rhs=Tbf[sl, hp * 64:(hp + 1) * 64], start=False, stop=True)
                        if h % 4 == 3:
                            yield 1
                    for hp in range(4):
                        S.pe([B_KdA, B_Vr], [PB[BG1]], "matmul", out=bank(BG1, 128, hp * 128), lhsT=KdA[:, hp * 128:(hp + 1) * 128],
                             rhs=Vr[:, hp * 128:(hp + 1) * 128], start=True, stop=True)
                    S.dve([PB[BG0]], [B_o32], "tensor_copy", out=o32[:], in_=bank(BG0))
                    u3 = bank(BG1).rearrange("p (hp c) -> p hp c", c=128)
                    for r in range(2):
                        sl = slice(64 * r, 64 * r + 64)
                        S.dve([B_SA, B_cd], [B_SA], "tensor_tensor", out=SA[sl], in0=SA[sl], in1=cdA[sl], op=ALU.mult)
                        S.dve([B_SA, PB[BG1]], [B_SA], "tensor_tensor", out=SA[sl], in0=SA[sl], in1=u3[sl, :, 64 * r:64 * r + 64], op=ALU.add)
                    S.dve([B_SA], [B_SAbf], "tensor_copy", out=SAbf[:], in_=SA[:])
                    S.dve([B_o32], [B_small[2]], "tensor_reduce", out=sm(2), in_=v3(o32[:]), axis=AX.X, op=ALU.add)
                    S.dve([B_small[2]], [B_small[2]], "tensor_scalar", out=sm(2), in0=sm(2), scalar1=1.0 / 64, scalar2=0.0, op0=ALU.mult, op1=ALU.add)
                    S.dve([B_o32, B_small[2]], [B_o32], "tensor_tensor", out=v3(o32[:]), in0=v3(o32[:]), in1=bcd(sm(2), 8, 64), op=ALU.subtract)
                    S.dve([B_o32], [B_sq], "tensor_tensor", out=sq[:], in0=o32[:], in1=o32[:], op=ALU.mult)
                    S.dve([B_sq], [B_small[3]], "tensor_reduce", out=sm(3), in_=v3(sq[:]), axis=AX.X, op=ALU.add)
                    yield 8
                    act_rstd(sm(3), B_small[3], 64)
                    S.dve([B_o32, B_small[3]], [B_o32], "tensor_tensor", out=v3(o32[:]), in0=v3(o32[:]), in1=bcd(sm(3), 8, 64), op=ALU.mult)
                    S.dve([B_o32, B_gexp], [B_rettok], "tensor_tensor", out=rettok[:], in0=o32[:], in1=gexp[:], op=ALU.mult)
                    yield 6
                    for j in range(4):
                        S.pe([B_rettok, B_ident], [PB[BG0]], "transpose", out=bank16(BG0)[:, j, :], in_=rettok[:, j * 128:(j + 1) * 128], identity=ident[:])
                    S.dve([PB[BG0]], [B_retT[par]], "tensor_copy", out=retT[par][:, :, tc], in_=bank16(BG0)[:, 0:4, :])
                    yield 1

                def outproj_work(t, par):
                    tl = t % 4
                    tc = slice(tl * 128, (tl + 1) * 128)
                    S.dma("sp", [], [B_xr], out=xr[:], in_=xo_d[t * 128:(t + 1) * 128, :])
                    for n in range(2):
                        for g in range(4):
                            S.pe([B_attP[par][g], B_wout], [PB[BG0 + n]], "matmul", out=bank(BG0 + n), lhsT=attP[par][:, g, tc], rhs=woutA[:, g, n * 512:(n + 1) * 512],
                                 start=(g == 0), stop=False)
                        for hp in range(4):
                            S.pe([B_retT[par], B_wout], [PB[BG0 + n]], "matmul", out=bank(BG0 + n), lhsT=retT[par][:, hp, tc], rhs=woutR[:, hp, n * 512:(n + 1) * 512],
                                 start=False, stop=(hp == 3))
                    mix = ps[:, BG0 * 512:(BG0 + 2) * 512]
                    PBm = [PB[BG0], PB[BG1]]
                    yield 4
                    S.act(PBm, [B_mt, B_small[4]], "activation", out=mt[:], in_=mix, func=AF.Square, accum_out=sm(4, 1))
                    yield 2
                    act_rstd(sm(4, 1), B_small[4], D)
                    S.dve(PBm + [B_small[4], B_gateM], [B_mt], "scalar_tensor_tensor", out=mt[:], in0=mix, scalar=sm(4, 1), in1=gateM[:], op0=ALU.mult, op1=ALU.mult)
                    S.pool([B_mt, B_xr], [B_xr], "tensor_tensor", out=xr[:], in0=mt[:], in1=xr[:], op=ALU.add)
                    S.dma("pool", [B_xr], [B_y[t]], out=y_d[t * 128:(t + 1) * 128, :], in_=xr[:])
                    yield 1

                def attention(blk, bg):
                    par = blk % 2
                    steps = [(g, kt) for g in range(4) for kt in range(NKT)]
                    n = len(steps)
                    pending = []

                    def a_qk(i):
                        g, kt = steps[i]
                        sb0 = 2 * (i % 2)
                        for kv in range(2):
                            sl = slice(64 * kv, 64 * kv + 64)
                            S.pe([B_KT, B_QaT[par]], [PB[sb0 + kv]], "matmul", out=bank(sb0 + kv), lhsT=KT[sl, kt * 128:(kt + 1) * 128], rhs=QaT[par][sl, g, :],
                                 start=True, stop=True)

                    def a_ex(i):
                        sb0, pp = 2 * (i % 2), i % 3
                        S.act([PB[sb0], PB[sb0 + 1]], [B_pT[pp]], "activation", out=pT[pp][:], in_=ps[:, sb0 * 512:(sb0 + 2) * 512], func=AF.Exp, scale=0.125)

                    def a_pv(i):
                        g, kt = steps[i]
                        pp = i % 3
                        for kv in range(2):
                            S.pe([B_V, B_pT[pp]], [PB[4 + kv]], "matmul", out=ps[0:65, (4 + kv) * 512:(5 + kv) * 512], lhsT=Vaug[:, kt, kv, :],
                                 rhs=pT[pp][:, kv * 512:(kv + 1) * 512], start=(kt == 0), stop=(kt == NKT - 1))

                    def epi0(g):
                        for kv in range(2):
                            S.act([PB[4 + kv]], [B_oT[kv]], "copy", out=oT[kv][0:65, :], in_=ps[0:65, (4 + kv) * 512:(5 + kv) * 512])

                    def epi1(g):
                        for kv in range(2):
                            S.dve([B_oT[kv]], [B_oT[kv]], "reciprocal", out=oT[kv][64:65, :], in_=oT[kv][64:65, :])

                    def epi2(g):
                        for kv in range(2):
                            S.dma("sp", [B_oT[kv]], [B_rsc[kv]], out=rsc_d[kv:kv + 1, :], in_=oT[kv][64:65, :])

                    def epi2b(g):
                        for kv in range(2):
                            S.dma("sp", [B_rsc[kv]], [B_rcb[kv]], out=rcb[kv][0:64, :], in_=rsc_d[kv:kv + 1, :].to_broadcast([64, 512]))

                    def epi3(g):
                        S.dve([B_oT[0], B_rcb[0]], [B_attP[par][g]], "tensor_tensor", out=attP[par][0:64, g, :], in0=oT[0][0:64, :], in1=rcb[0][0:64, :], op=ALU.mult)
                        S.dve([B_oT[1], B_rcb[1]], [B_atmp], "tensor_tensor", out=atmp[0:64, :], in0=oT[1][0:64, :], in1=rcb[1][0:64, :], op=ALU.mult)
                        S.dma("sp", [B_atmp], [B_attP[par][g]], out=attP[par][64:128, g, :], in_=atmp[0:64, :])

                    def after_pv(i, now):
                        g, kt = steps[i]
                        if kt == NKT - 1:
                            epi0(g)
                            pending.extend([(now + 1, epi1, g), (now + 3, epi2, g), (now + 7, epi2b, g), (now + 11, epi3, g)])

                    a_qk(0)
                    for i in range(n):
                        if i + 1 < n:
                            a_qk(i + 1)
                        a_ex(i)
                        if i >= 1:
                            a_pv(i - 1)
                            after_pv(i - 1, i)
                        for item in [p for p in pending if p[0] <= i]:
                            pending.remove(item)
                            item[1](item[2])
                        if bgw[0] > 0:
                            bgw[0] -= 1
                        else:
                            bgw[0] = (next(bg, None) or 1) - 1
                    a_pv(n - 1)
                    after_pv(n - 1, n)
                    for item in sorted(pending, key=lambda p: p[0]):
                        item[1](item[2])

                def drain(gen):
                    for _ in gen:
                        pass

                bgw = [0]
                drain(front_part(0))
                drain(itertools.chain(*[tile_work(t, 0) for t in range(4)]))
                for blk in range(n_blocks):
                    parts = []
                    if blk > 0:
                        parts += [outproj_work(t, (blk - 1) % 2) for t in range((blk - 1) * 4, blk * 4)]
                    if blk + 1 < n_blocks:
                        parts += [tile_work(t, (blk + 1) % 2) for t in range((blk + 1) * 4, (blk + 2) * 4)]
                    bg = itertools.chain(*parts)
                    attention(blk, bg)
                    drain(bg)
                drain(itertools.chain(*[outproj_work(t, (n_blocks - 1) % 2) for t in range((n_blocks - 1) * 4, n_blocks * 4)]))
                S.flush()
            S.flush()

        if do_ffn:
            with ExitStack() as esD:
                multF = multM; shF = shM; gateF = gateM
                B_multF, B_shF, B_gateF = Buf("multF"), Buf("shF"), Buf("gateF")
                with ExitStack() as esD0:
                    jobs = []
                    for s in range(2):
                        jobs.append((6 + s, 0, "sh", 0, shF, B_shF, s * 512))
                        jobs.append((8 + s, 0, "sc", 2, multF, B_multF, s * 512))
                        jobs.append((10 + s, 0, "gt", 3, gateF, B_gateF, s * 512))
                    mod_tables(esD0, jobs)
                    S.flush()
                front = make_front(esD, nbuf=1)
                w1 = sb("w1", [128, 8, 2 * HID], BF16, esD); B_w1 = Buf("w1")
                w2 = sb("w2", [128, 22, D], BF16, esD); B_w2 = Buf("w2")
                with ExitStack() as esD1:
                    wstg = [sb(f"wstg{i}", [128, 2816], F32, esD1) for i in range(2)]; B_wstg = [Buf("wstg0"), Buf("wstg1")]
                    i = 0
                    for kc in range(8):
                        for half in range(2):
                            p = i % 2; i += 1
                            S.dma("sp", [], [B_wstg[p]], out=wstg[p][:], in_=w1_d[kc * 128:(kc + 1) * 128, half * HID:(half + 1) * HID])
                            S.convert(B_wstg[p], B_w1, w1[:, kc, half * HID:(half + 1) * HID], wstg[p][:, 0:HID], HID)
                    for jj in range(11):
                        p = i % 2; i += 1
                        S.dma("sp", [], [B_wstg[p]], out=wstg[p][:, 0:2048].rearrange("p (a n) -> p a n", a=2),
                              in_=w2_d[jj * 256:(jj + 1) * 256, :].rearrange("(a p) n -> p a n", p=128))
                        S.convert(B_wstg[p], B_w2, w2[:, 2 * jj:2 * jj + 2, :].rearrange("p a n -> p (a n)"), wstg[p][:, 0:2048], 2048)
                    S.flush()
                TS = 2
                NW = TS * 128
                NSB = n_blocks * 4 // TS
                xnb = [sb("xnb", [128, TS, D], F32, esD) for _ in range(2)]; B_xnb = [[Buf(f"xnb{p}_{i}") for i in range(TS)] for p in range(2)]
                hfT = [sb("hfT", [128, 8, NW], BF16, esD) for _ in range(2)]; B_hfT = [[Buf(f"hfT{p}_{i}") for i in range(TS)] for p in range(2)]
                uT = sb("uT", [128, 22, NW], BF16, esD); B_uT = Buf("uT")
                sa = [sb(f"sa{i}", [128, NW], F32, esD) for i in range(2)]; B_sa = [Buf("sa0"), Buf("sa1")]
                _mt = sb("mtD", [128, D], F32, esD); _Bmt = Buf("mtD")
                mts = [_mt, _mt]; B_mts = [_Bmt, _Bmt]

                def d_front(sbk):
                    p = sbk % 2
                    for tl in range(TS):
                        t = sbk * TS + tl
                        tc = slice(tl * 128, (tl + 1) * 128)
                        front(y_d[t * 128:(t + 1) * 128, :], [B_y[t]], multF, B_multF, shF, B_shF, hfT[p][:, :, tc], B_hfT[p][tl], 0,
                              keep=(xnb[p][:, tl, :], B_xnb[p][tl]))

                def d_in(sbk):
                    p = sbk % 2
                    for j in range(22):
                        q = j % 2
                        for (bk, c0) in ((2 * q, j * 128), (2 * q + 1, HID + j * 128)):
                            for kc in range(8):
                                S.pe(B_hfT[p] + [B_w1], [PB[bk]], "matmul", out=bank(bk, NW), lhsT=w1[:, kc, c0:c0 + 128], rhs=hfT[p][:, kc, :],
                                     start=(kc == 0), stop=(kc == 7))
                        S.act([PB[2 * q]], [B_sa[q]], "activation", out=sa[q][:], in_=bank(2 * q, NW), func=AF.Silu)
                        S.dve([B_sa[q], PB[2 * q + 1]], [B_uT], "tensor_tensor", out=uT[:, j, :], in0=sa[q][:], in1=bank(2 * q + 1, NW), op=ALU.mult)

                def d_out(sbk):
                    p = sbk % 2
                    for tl in range(TS):
                        t = sbk * TS + tl
                        tc = slice(tl * 128, (tl + 1) * 128)
                        ob = 4 + 2 * (tl % 2)
                        for n in range(2):
                            for j in range(22):
                                S.pe([B_uT, B_w2], [PB[ob + n]], "matmul", out=bank(ob + n), lhsT=uT[:, j, tc], rhs=w2[:, j, n * 512:(n + 1) * 512],
                                     start=(j == 0), stop=(j == 21))
                        f = ps[:, ob * 512:(ob + 2) * 512]
                        PBm = [PB[ob], PB[ob + 1]]
                        mt_, Bmt = mts[tl % 2], B_mts[tl % 2]
                        S.act(PBm, [Bmt, B_small[4 + tl % 2]], "activation", out=mt_[:], in_=f, func=AF.Square, accum_out=sm(4 + tl % 2, 1))
                        rstd_inplace(sm(4 + tl % 2, 1), B_small[4 + tl % 2], D)
                        S.dve(PBm + [B_small[4 + tl % 2], B_gateF], [Bmt], "scalar_tensor_tensor", out=mt_[:], in0=f, scalar=sm(4 + tl % 2, 1), in1=gateF[:],
                              op0=ALU.mult, op1=ALU.mult)
                        S.pool([Bmt, B_xnb[p][tl]], [Bmt], "tensor_tensor", out=mt_[:], in0=mt_[:], in1=xnb[p][:, tl, :], op=ALU.add)
                        S.dma("pool", [Bmt], [B_y[t]], out=y_d[t * 128:(t + 1) * 128, :], in_=mt_[:])

                d_front(0)
                for sbk in range(NSB):
                    d_in(sbk)
                    if sbk + 1 < NSB:
                        d_front(sbk + 1)
                    d_out(sbk)
                S.flush()
        S.flush(final=True)


def _consts():
    c = np.zeros((128, C_END), np.float32)
    c[:, C_ID:C_ID + 128] = np.eye(128, dtype=np.float32)
    p = np.arange(128, dtype=np.float32)
    c[:, C_POS + 0] = p
    c[:, C_POS + 1] = 127.0 - p
    c[:, C_POS + 2] = p + 1.0
    c[:, C_POS + 3] = 128.0 - p
    j = p[:, None]
    i = p[None, :]
    c[:, C_RIJ:C_RIJ + 128] = np.maximum(i - j, 0.0)
    c[:, C_RJI:C_RJI + 128] = np.maximum(j - i, 0.0)
    c[:, C_MGE:C_MGE + 128] = (i >= j).astype(np.float32)
    c[:, C_MLE:C_MLE + 128] = (j >= i).astype(np.float32)
    return c


def _rope_table(n_lat):
    pos = np.arange(n_lat)
    row = (pos // 64).astype(np.float32)
    col = (pos % 64).astype(np.float32)
    inv = (np.float32(10000.0) ** (-np.arange(16, dtype=np.float32) / np.float32(16))).astype(np.float32)
    ang = np.concatenate([row[:, None] * inv, col[:, None] * inv], axis=-1).astype(np.float32)
    return np.concatenate([np.cos(ang), np.sin(ang)], axis=-1).astype(np.float32)


_NC_CACHE = {}


def prep_inputs(x, c, ctx, c_ctx, w_mod, b_mod, g_pre_mix, g_post_mix, g_pre_ffn, g_post_ffn, w_in, q_norm_g, k_norm_g,
                ret_decay_fwd, ret_decay_bwd, w_out, w_ffn_in, w_ffn_out):
    f32 = lambda a: np.ascontiguousarray(np.asarray(a, dtype=np.float32))
    x, c, ctx, c_ctx = f32(x), f32(c), f32(ctx), f32(c_ctx)
    B, N, _ = x.shape
    rope = _rope_table(N)
    cst = _consts()
    rep = lambda v: np.ascontiguousarray(np.broadcast_to(np.asarray(v, np.float32).reshape(1, -1), (128, np.asarray(v).size)))
    gvec = np.concatenate([rep(g_pre_mix[0]), rep(g_post_mix[0]), rep(g_pre_ffn[0]), rep(g_post_ffn[0])], axis=1)
    gqk = np.concatenate([rep(q_norm_g[0]), rep(k_norm_g[0])], axis=1)
    bmod = rep(b_mod[0])
    shared = {"w_mod": f32(w_mod[0]), "bmod": bmod, "gvec": np.ascontiguousarray(gvec), "gqk": np.ascontiguousarray(gqk),
              "w_in": f32(w_in[0]), "w_out": f32(w_out[0]), "w_ffn_in": f32(w_ffn_in[0]), "w_ffn_out": f32(w_ffn_out[0]), "cst": cst}
    in_maps = []
    for core in range(8):
        b, h = core // 2, core % 2
        if h == 0:
            xf, rf, cf_ = x[b], rope, ctx[b]
            dA, dB = ret_decay_fwd[0], ret_decay_bwd[0]
        else:
            xf, rf, cf_ = x[b, ::-1], rope[::-1], ctx[b, ::-1]
            dA, dB = ret_decay_bwd[0], ret_decay_fwd[0]
        cfm = np.concatenate([c[b].reshape(8, 128).T, c_ctx.reshape(8, 128).T], axis=1)
        m = dict(shared)
        m.update({"xo": np.ascontiguousarray(xf[:NOWN]), "xt": np.ascontiguousarray(xf[NOWN:]), "cx": np.ascontiguousarray(cf_),
                  "rope": np.ascontiguousarray(rf), "cfm": np.ascontiguousarray(cfm, dtype=np.float32),
                  "dec": np.concatenate([rep(dA), rep(dB)], axis=1)})
        in_maps.append(m)
    return in_maps


def kernel(**inputs):
    in_maps = prep_inputs(**inputs)
    B, N, _ = np.asarray(inputs["x"]).shape
    if "nc" not in _NC_CACHE:
        _NC_CACHE["nc"] = build_nc()
    res = run_bass_kernel_spmd(_NC_CACHE["nc"], in_maps, core_ids=list(range(8)))
    out = np.empty((B, N, D), np.float32)
    for core in range(8):
        b, h = core // 2, core % 2
        yv = np.asarray(res.results[core]["y"], np.float32)
        if h == 0:
            out[b, :NOWN] = yv
        else:
            out[b, NOWN:] = yv[::-1]
    return out
```
